# Optimizing a Trainium2 kernel written in Bass

```python
import math
import jax, jax.numpy as jnp
from jax import lax
import numpy as np

D_MODEL = 2048
BATCH = 1
SEQ = 8192
DEPTH = 2
DEC_BATCH = 128
DEC_SEQ = 1
PAST_LEN = 8192
PAGE_SIZE = 128

N_META = 16
MIX_WIDTH = D_MODEL
HEAD_DIM = 64
A_WIDTH = MIX_WIDTH // 2
A_HEADS = A_WIDTH // HEAD_DIM
A_KV_HEADS = A_HEADS // 4
GQA = A_HEADS // A_KV_HEADS
WINDOW = 128
SWA_BLOCK = 128
ROPE_DIM = HEAD_DIM // 4
ROPE_THETA = 500000.0
S5_WIDTH = MIX_WIDTH // 4
S5_CH = 16
S5_GROUPS = S5_WIDTH // S5_CH
S5_STATE = 64
GLA_WIDTH = MIX_WIDTH - A_WIDTH - S5_WIDTH
GLA_HEADS = 4
GLA_DV = GLA_WIDTH // GLA_HEADS
GLA_DK = GLA_DV // 2
GLA_LOWRANK = 16
GLA_TAU = 16.0
GLA_CHUNK = 64
D_FF = ((8 * D_MODEL // 3 + 255) // 256) * 256
CONV_W = 3
LN_EPS = 1e-5
DEEPNORM_ALPHA = (2 * DEPTH) ** 0.25
DEEPNORM_BETA = (8 * DEPTH) ** -0.25
NEG_INF = -1e30
F32 = jnp.float32
COL_SIZES = (A_WIDTH, A_KV_HEADS * HEAD_DIM, A_KV_HEADS * HEAD_DIM, S5_WIDTH,
             GLA_HEADS * GLA_DK, GLA_HEADS * GLA_DK, GLA_WIDTH, GLA_WIDTH, GLA_LOWRANK)
IN_COLS = sum(COL_SIZES)

kernel_name = 'hymba_swa_s5_gla_convffn_step'


def _layer_norm(x, g, b):
    xf = x.astype(F32)
    mu = jnp.mean(xf, -1, keepdims=True)
    var = jnp.mean(jnp.square(xf - mu), -1, keepdims=True)
    return ((xf - mu) * lax.rsqrt(var + LN_EPS) * g.astype(F32) + b.astype(F32)).astype(x.dtype)


def _split_cols(h):
    idx = np.cumsum(COL_SIZES)[:-1].tolist()
    return jnp.split(h, idx, axis=-1)


def _partial_rope(x, pos):
    half = ROPE_DIM // 2
    inv = jnp.power(ROPE_THETA, -jnp.arange(half, dtype=F32) / half)
    ang = pos.astype(F32)[:, None] * inv[None, :]
    cos, sin = jnp.cos(ang)[:, None, :], jnp.sin(ang)[:, None, :]
    xr = x[..., :ROPE_DIM].astype(F32)
    x1, x2 = xr[..., :half], xr[..., half:]
    rot = jnp.concatenate([x1 * cos - x2 * sin, x1 * sin + x2 * cos], -1).astype(x.dtype)
    return jnp.concatenate([rot, x[..., ROPE_DIM:]], -1)


def _swa_attend(q, k, v, sink, q_pos, k_pos):
    s = jnp.einsum('bnqkgd,bnskd->bnkgqs', q.astype(F32), k.astype(F32)) * HEAD_DIM ** -0.5
    diff = q_pos[:, :, None] - k_pos[:, None, :]
    ok = (diff >= 0) & (diff <= WINDOW) & (k_pos[:, None, :] >= 0)
    s = jnp.where(ok[None, :, None, None], s, NEG_INF)
    sk = sink.astype(F32).reshape(A_KV_HEADS, GQA)[None, None, :, :, None, None]
    m = jnp.maximum(jnp.max(s, -1, keepdims=True), sk)
    p = jnp.exp(s - m)
    den = jnp.sum(p, -1, keepdims=True) + jnp.exp(sk - m)
    return jnp.einsum('bnkgqs,bnskd->bnqkgd', p / den, v.astype(F32))


def _s5_discretise(lam_re, lam_im, log_dt, b_re, b_im):
    dt = jnp.exp(log_dt.astype(F32))[:, None]
    lr, li = lam_re.astype(F32), lam_im.astype(F32)
    mag = jnp.exp(lr * dt)
    ab_re, ab_im = mag * jnp.cos(li * dt), mag * jnp.sin(li * dt)
    den = lr * lr + li * li
    nr, ni = ab_re - 1.0, ab_im
    f_re = (nr * lr + ni * li) / den
    f_im = (ni * lr - nr * li) / den
    bb_re = f_re[..., None] * b_re - f_im[..., None] * b_im
    bb_im = f_re[..., None] * b_im + f_im[..., None] * b_re
    return ab_re, ab_im, bb_re, bb_im


def _ssm_combine(left, right):
    ar1, ai1, br1, bi1 = left
    ar2, ai2, br2, bi2 = right
    return (ar2 * ar1 - ai2 * ai1, ar2 * ai1 + ai2 * ar1,
            ar2 * br1 - ai2 * bi1 + br2, ar2 * bi1 + ai2 * br1 + bi2)


def _s5(u, h0_re, h0_im, lam_re, lam_im, log_dt, b_re, b_im, c_re, c_im, d, w_glu, b_glu):
    Bsz, T, _ = u.shape
    uf = u.astype(F32).reshape(Bsz, T, S5_GROUPS, S5_CH)
    ab_re, ab_im, bb_re, bb_im = _s5_discretise(lam_re, lam_im, log_dt, b_re.astype(F32), b_im.astype(F32))
    bu_re = jnp.einsum('btgh,gph->btgp', uf, bb_re)
    bu_im = jnp.einsum('btgh,gph->btgp', uf, bb_im)
    e_re = jnp.concatenate([h0_re.astype(F32)[:, None], bu_re], 1)
    e_im = jnp.concatenate([h0_im.astype(F32)[:, None], bu_im], 1)
    a_re = jnp.broadcast_to(ab_re, e_re.shape)
    a_im = jnp.broadcast_to(ab_im, e_im.shape)
    _, _, h_re, h_im = lax.associative_scan(_ssm_combine, (a_re, a_im, e_re, e_im), axis=1)
    h_re, h_im = h_re[:, 1:], h_im[:, 1:]
    y = (jnp.einsum('btgp,ghp->btgh', h_re, c_re.astype(F32))
         - jnp.einsum('btgp,ghp->btgh', h_im, c_im.astype(F32))
         + d.astype(F32) * uf)
    z = jax.nn.gelu(y).reshape(Bsz, T, S5_WIDTH)
    out = z * jax.nn.sigmoid(z @ w_glu.astype(F32) + b_glu.astype(F32))
    return out, h_re[:, -1], h_im[:, -1]


def _gla(q, k, v, logg, S0, lead):
    Bsz, T = q.shape[0], q.shape[1]
    tail = (-(lead + T)) % GLA_CHUNK
    nc = (lead + T + tail) // GLA_CHUNK

    def blocks(a):
        a = jnp.pad(a, ((0, 0), (lead, tail), (0, 0), (0, 0)))
        return a.reshape(Bsz, nc, GLA_CHUNK, GLA_HEADS, a.shape[-1]).transpose(1, 0, 3, 2, 4)

    causal = jnp.tril(jnp.ones((GLA_CHUNK, GLA_CHUNK), bool))

    def step(S, inp):
        qc, kc, vc, gc = inp
        b = jnp.cumsum(gc, axis=2)
        q_in = qc * jnp.exp(b)
        att = jnp.where(causal, jnp.einsum('bhtd,bhsd->bhts', q_in, kc * jnp.exp(-b)), 0.0)
        o = jnp.einsum('bhtd,bhde->bhte', q_in, S) + jnp.einsum('bhts,bhse->bhte', att, vc)
        b_end = b[:, :, -1]
        S_new = (jnp.exp(b_end)[..., None] * S
                 + jnp.einsum('bhsd,bhse->bhde', kc * jnp.exp(b_end[:, :, None] - b), vc))
        return S_new, o

    S_fin, o = lax.scan(step, S0, (blocks(q), blocks(k), blocks(v), blocks(logg)))
    o = o.transpose(1, 0, 3, 2, 4).reshape(Bsz, nc * GLA_CHUNK, GLA_HEADS, GLA_DV)[:, lead:lead + T]
    return o, S_fin


def _conv_ffn(x, prev, w_up, conv_w, conv_b, w_down):
    T = x.shape[1]
    u, g = jnp.split(x @ w_up, 2, axis=-1)
    ext = jnp.concatenate([prev.astype(u.dtype), u], 1)
    c = conv_b
    for j in range(CONV_W):
        c = c + conv_w[j] * ext[:, j:j + T]
    out = (jax.nn.gelu(c) * g) @ w_down
    return out, ext[:, -(CONV_W - 1):]


def _layer(x, l, prm, st, prompt):
    Bsz, T, _ = x.shape
    qa, ka, va, us, qc, kc, vc, gate_c, ac = _split_cols(x @ prm['w_in'][l])
    pos = jnp.arange(T, dtype=jnp.int32) + (0 if prompt else PAST_LEN)
    qa = _partial_rope(qa.reshape(Bsz, T, A_HEADS, HEAD_DIM), pos).reshape(Bsz, T, A_KV_HEADS, GQA, HEAD_DIM)
    ka = _partial_rope(ka.reshape(Bsz, T, A_KV_HEADS, HEAD_DIM), pos)
    va = va.reshape(Bsz, T, A_KV_HEADS, HEAD_DIM)
    sink = prm['attn_sink'][l]
    if prompt:
        lead = (-T) % SWA_BLOCK
        n = (T + lead) // SWA_BLOCK
        padt = lambda a: jnp.pad(a, [(0, 0), (lead, 0)] + [(0, 0)] * (a.ndim - 2))
        qb = padt(qa).reshape(Bsz, n, SWA_BLOCK, A_KV_HEADS, GQA, HEAD_DIM)
        kb = padt(ka).reshape(Bsz, n, SWA_BLOCK, A_KV_HEADS, HEAD_DIM)
        vb = padt(va).reshape(Bsz, n, SWA_BLOCK, A_KV_HEADS, HEAD_DIM)
        shift = [(0, 0), (1, 0), (0, 0), (0, 0), (0, 0)]
        kb2 = jnp.concatenate([jnp.pad(kb[:, :-1], shift), kb], axis=2)
        vb2 = jnp.concatenate([jnp.pad(vb[:, :-1], shift), vb], axis=2)
        qpos = (jnp.arange(n * SWA_BLOCK, dtype=jnp.int32) - lead).reshape(n, SWA_BLOCK)
        kpos = jnp.concatenate([qpos - SWA_BLOCK, qpos], axis=1)
        oa = _swa_attend(qb, kb2, vb2, sink, qpos, kpos).reshape(Bsz, n * SWA_BLOCK, A_WIDTH)[:, lead:]
        new_k, new_v = ka[:, -WINDOW:], va[:, -WINDOW:]
    else:
        k_all = jnp.concatenate([st[0].astype(ka.dtype), ka], 1)
        v_all = jnp.concatenate([st[1].astype(va.dtype), va], 1)
        kpos = PAST_LEN - WINDOW + jnp.arange(WINDOW + T, dtype=jnp.int32)
        oa = _swa_attend(qa[:, None], k_all[:, None], v_all[:, None], sink, pos[None], kpos[None]).reshape(Bsz, T, A_WIDTH)
        new_k, new_v = k_all[:, -WINDOW:], v_all[:, -WINDOW:]
    if prompt:
        h0_re = jnp.zeros((Bsz, S5_GROUPS, S5_STATE), F32)
        h0_im = jnp.zeros((Bsz, S5_GROUPS, S5_STATE), F32)
    else:
        h0_re, h0_im = st[2], st[3]
    ob, ssm_re, ssm_im = _s5(us, h0_re, h0_im, prm['s5_lam_re'][l], prm['s5_lam_im'][l], prm['s5_log_dt'][l],
                             prm['s5_b_re'][l], prm['s5_b_im'][l], prm['s5_c_re'][l], prm['s5_c_im'][l],
                             prm['s5_d'][l], prm['s5_w_glu'][l], prm['s5_b_glu'][l])
    qg = qc.astype(F32).reshape(Bsz, T, GLA_HEADS, GLA_DK) * GLA_DK ** -0.5
    kg = kc.astype(F32).reshape(Bsz, T, GLA_HEADS, GLA_DK)
    vg = vc.astype(F32).reshape(Bsz, T, GLA_HEADS, GLA_DV)
    logg = jax.nn.log_sigmoid(ac.astype(F32) @ prm['gla_w_a2'][l].astype(F32) + prm['gla_b_a'][l].astype(F32)) / GLA_TAU
    logg = logg.reshape(Bsz, T, GLA_HEADS, GLA_DK)
    if prompt:
        S0 = jnp.zeros((Bsz, GLA_HEADS, GLA_DK, GLA_DV), F32)
        g_lead = (-T) % GLA_CHUNK
    else:
        S0 = st[4].astype(F32)
        g_lead = 0
    og, S_fin = _gla(qg, kg, vg, logg, S0, g_lead)
    og = og * lax.rsqrt(jnp.mean(og * og, -1, keepdims=True) + LN_EPS)
    og = og.reshape(Bsz, T, GLA_WIDTH) * prm['gla_norm_g'][l].astype(F32) * jax.nn.silu(gate_c.astype(F32))
    mix = jnp.concatenate([oa.astype(x.dtype), ob.astype(x.dtype), og.astype(x.dtype)], -1)
    x = _layer_norm(DEEPNORM_ALPHA * x + mix @ prm['w_out'][l], prm['ln1_g'][l], prm['ln1_b'][l])
    conv_prev = jnp.zeros((Bsz, CONV_W - 1, D_FF), x.dtype) if prompt else st[5]
    f, conv_new = _conv_ffn(x, conv_prev, prm['ffn_w_up'][l], prm['ffn_conv_w'][l], prm['ffn_conv_b'][l], prm['ffn_w_down'][l])
    x = _layer_norm(DEEPNORM_ALPHA * x + f, prm['ln2_g'][l], prm['ln2_b'][l])
    return x, (new_k, new_v, ssm_re, ssm_im, S_fin, conv_new)


def setup_inputs(seed: int = 0) -> dict:
    key = jax.random.key(seed)
    ks = iter(list(jax.random.split(key, 48)))
    nrm = lambda shape, scale: scale * jax.random.normal(next(ks), shape, F32)
    L = DEPTH
    d = {}
    d['x_prompt'] = nrm((BATCH, SEQ, D_MODEL), 1.0)
    d['x_sample'] = nrm((DEC_BATCH, DEC_SEQ, D_MODEL), 1.0)
    d['cache_swa_k'] = nrm((L, DEC_BATCH, WINDOW, A_KV_HEADS, HEAD_DIM), 1.0)
    d['cache_swa_v'] = nrm((L, DEC_BATCH, WINDOW, A_KV_HEADS, HEAD_DIM), 1.0)
    d['state_ssm_re'] = nrm((L, DEC_BATCH, S5_GROUPS, S5_STATE), 1.0)
    d['state_ssm_im'] = nrm((L, DEC_BATCH, S5_GROUPS, S5_STATE), 1.0)
    d['state_gla'] = nrm((L, DEC_BATCH, GLA_HEADS, GLA_DK, GLA_DV), 1.0)
    d['state_conv'] = nrm((L, DEC_BATCH, CONV_W - 1, D_FF), 1.0)
    d['meta_tokens'] = nrm((N_META, D_MODEL), 1.0)
    d['ln_in_g'] = 1.0 + nrm((D_MODEL,), 0.01)
    d['ln_in_b'] = nrm((D_MODEL,), 0.01)
    d['w_in'] = nrm((L, D_MODEL, IN_COLS), D_MODEL ** -0.5)
    d['attn_sink'] = nrm((L, A_HEADS), 0.5)
    d['s5_lam_re'] = -0.5 + nrm((L, S5_GROUPS, S5_STATE), 0.01)
    d['s5_lam_im'] = jnp.broadcast_to(math.pi * jnp.arange(S5_STATE, dtype=F32), (L, S5_GROUPS, S5_STATE)) + nrm((L, S5_GROUPS, S5_STATE), 0.01)
    d['s5_log_dt'] = math.log(1e-3) + jax.random.uniform(next(ks), (L, S5_GROUPS), F32) * (math.log(1e-1) - math.log(1e-3))
    d['s5_b_re'] = nrm((L, S5_GROUPS, S5_STATE, S5_CH), 0.5 * S5_CH ** -0.5)
    d['s5_b_im'] = nrm((L, S5_GROUPS, S5_STATE, S5_CH), 0.5 * S5_CH ** -0.5)
    d['s5_c_re'] = nrm((L, S5_GROUPS, S5_CH, S5_STATE), (2 * S5_STATE) ** -0.5)
    d['s5_c_im'] = nrm((L, S5_GROUPS, S5_CH, S5_STATE), (2 * S5_STATE) ** -0.5)
    d['s5_d'] = nrm((L, S5_GROUPS, S5_CH), 1.0)
    d['s5_w_glu'] = nrm((L, S5_WIDTH, S5_WIDTH), S5_WIDTH ** -0.5)
    d['s5_b_glu'] = nrm((L, S5_WIDTH), 0.01)
    d['gla_w_a2'] = nrm((L, GLA_LOWRANK, GLA_HEADS * GLA_DK), GLA_LOWRANK ** -0.5)
    d['gla_b_a'] = nrm((L, GLA_HEADS * GLA_DK), 0.01)
    d['gla_norm_g'] = 1.0 + nrm((L, GLA_WIDTH), 0.01)
    d['w_out'] = nrm((L, MIX_WIDTH, D_MODEL), DEEPNORM_BETA * MIX_WIDTH ** -0.5)
    d['ln1_g'] = 1.0 + nrm((L, D_MODEL), 0.01)
    d['ln1_b'] = nrm((L, D_MODEL), 0.01)
    d['ffn_w_up'] = nrm((L, D_MODEL, 2 * D_FF), D_MODEL ** -0.5)
    d['ffn_conv_w'] = nrm((L, CONV_W, D_FF), CONV_W ** -0.5)
    d['ffn_conv_b'] = nrm((L, D_FF), 0.01)
    d['ffn_w_down'] = nrm((L, D_FF, D_MODEL), DEEPNORM_BETA * D_FF ** -0.5)
    d['ln2_g'] = 1.0 + nrm((L, D_MODEL), 0.01)
    d['ln2_b'] = nrm((L, D_MODEL), 0.01)
    return d


def reference(x_prompt, x_sample, cache_swa_k, cache_swa_v, state_ssm_re, state_ssm_im, state_gla, state_conv,
              meta_tokens, ln_in_g, ln_in_b, w_in, attn_sink, s5_lam_re, s5_lam_im, s5_log_dt, s5_b_re, s5_b_im,
              s5_c_re, s5_c_im, s5_d, s5_w_glu, s5_b_glu, gla_w_a2, gla_b_a, gla_norm_g, w_out, ln1_g, ln1_b,
              ffn_w_up, ffn_conv_w, ffn_conv_b, ffn_w_down, ln2_g, ln2_b):
    prm = dict(w_in=w_in, attn_sink=attn_sink, s5_lam_re=s5_lam_re, s5_lam_im=s5_lam_im, s5_log_dt=s5_log_dt,
               s5_b_re=s5_b_re, s5_b_im=s5_b_im, s5_c_re=s5_c_re, s5_c_im=s5_c_im, s5_d=s5_d, s5_w_glu=s5_w_glu,
               s5_b_glu=s5_b_glu, gla_w_a2=gla_w_a2, gla_b_a=gla_b_a, gla_norm_g=gla_norm_g, w_out=w_out,
               ln1_g=ln1_g, ln1_b=ln1_b, ffn_w_up=ffn_w_up, ffn_conv_w=ffn_conv_w, ffn_conv_b=ffn_conv_b,
               ffn_w_down=ffn_w_down, ln2_g=ln2_g, ln2_b=ln2_b)
    meta = jnp.broadcast_to(meta_tokens.astype(x_prompt.dtype)[None], (x_prompt.shape[0], N_META, D_MODEL))
    xp = _layer_norm(jnp.concatenate([meta, x_prompt], 1), ln_in_g, ln_in_b)
    xs = _layer_norm(x_sample, ln_in_g, ln_in_b)
    sp, ss = [], []
    for l in range(DEPTH):
        xp, st_p = _layer(xp, l, prm, None, True)
        xs, st_s = _layer(xs, l, prm, (cache_swa_k[l], cache_swa_v[l], state_ssm_re[l], state_ssm_im[l],
                                       state_gla[l], state_conv[l]), False)
        sp.append(st_p)
        ss.append(st_s)
    stk = lambda lst, i: jnp.stack([s[i] for s in lst])
    y_prompt = xp[:, N_META:]
    y_sample = xs
    return (y_prompt, y_sample, stk(sp, 0), stk(sp, 1), stk(sp, 2), stk(sp, 3), stk(sp, 4), stk(sp, 5),
            stk(ss, 0), stk(ss, 1), stk(ss, 2), stk(ss, 3), stk(ss, 4), stk(ss, 5))
```

```python
import math
from contextlib import ExitStack
import numpy as np
import ml_dtypes
import concourse.bass as bass
import concourse.mybir as mybir
from concourse.bass_utils import run_bass_kernel_spmd

F32 = mybir.dt.float32
BF16 = mybir.dt.bfloat16
I32 = mybir.dt.int32
AF = mybir.ActivationFunctionType
ALU = mybir.AluOpType

D = 2048
L = 2
NB = 66
G = NB * 128
NPAD = 112
NCORES = 8
DFF = 5632
NFC = DFF // 128
NHG = 8
NWC = 656
W1 = 512
W2 = 1024
ALPHA = float((2 * L) ** 0.25)
EPS = 1e-5
TWO_PI = 2.0 * math.pi
C1 = 6.28125
C2 = TWO_PI - C1
BROWS = 320
import os
DBG_SKIP = set(os.environ.get('DBG_SKIP', '').split(','))


class Prog:
    EPOCH = 4000
    NSLOT = 8
    DMA_EPOCH = 1800

    def __init__(self, nc, es, same_engine_sync=True):
        self.nc = nc
        self.es = es
        self.eng = {"pe": nc.tensor, "act": nc.scalar, "dve": nc.vector,
                    "pool": nc.gpsimd, "sp": nc.sync}
        self.cnt = {e: 0 for e in self.eng}
        self.csem = {}
        self.dsem = {}
        self.dcount = {}
        self.dgen = {}
        self.dslot_next = {e: 0 for e in self.eng}
        self.waited = {e: {} for e in self.eng}
        self.lastw = {}
        self.readers = {}
        self.same_engine_sync = same_engine_sync
        self.live_dma = {}

    def _newsem(self, name):
        return self.es.enter_context(self.nc.semaphore(name))

    def _csem(self, e, ep):
        k = (e, ep)
        if k not in self.csem:
            self.csem[k] = self._newsem(f"c_{e}_{ep}")
        return self.csem[k]

    def _emit_wait(self, x, tok):
        w = self.waited[x]
        if tok[0] == "c":
            _, e, n = tok
            if e == x and (e == "pe" or not self.same_engine_sync):
                return
            key = ("c", e)
            if w.get(key, 0) >= n:
                return
            w[key] = n
            idx = n - 1
            self.eng[x].wait_ge(self._csem(e, idx // self.EPOCH), idx % self.EPOCH + 1)
        else:
            _, q, slot, gen, k = tok
            key = ("d", q, slot, gen)
            if w.get(key, 0) >= k:
                return
            w[key] = k
            self.eng[x].wait_ge(self.dsem[(q, slot, gen)], 16 * k)

    def _deps(self, reads, writes):
        deps = []
        for r in reads:
            if r in self.lastw:
                deps.append(self.lastw[r])
        for wv in writes:
            if wv in self.lastw:
                deps.append(self.lastw[wv])
            deps.extend(self.readers.get(wv, {}).values())
        return deps

    def _commit(self, tok, reads, writes):
        for r in reads:
            d = self.readers.setdefault(r, {})
            if tok[0] == "c":
                d[("c", tok[1])] = tok
            else:
                d[("d", tok[1], tok[2], tok[3])] = tok
        for wv in writes:
            self.lastw[wv] = tok
            self.readers[wv] = {}

    @staticmethod
    def _excl(reads, writes):
        ps = [r for r in reads if isinstance(r, str) and r.startswith("ps")]
        if not ps:
            return reads, writes
        return [r for r in reads if r not in ps], list(writes) + ps

    def op(self, e, fn, reads=(), writes=()):
        reads, writes = self._excl(reads, writes)
        for t in self._deps(reads, writes):
            self._emit_wait(e, t)
        self.cnt[e] += 1
        n = self.cnt[e]
        idx = n - 1
        fns = fn if isinstance(fn, (list, tuple)) else [fn]
        ins = None
        for f in fns:
            ins = f(self.eng[e])
        ins.then_inc(self._csem(e, idx // self.EPOCH), 1)
        tok = ("c", e, n)
        self._commit(tok, reads, writes)
        return tok

    def dma(self, q, out, in_, reads=(), writes=()):
        slot = self.dslot_next[q]
        self.dslot_next[q] = (slot + 1) % self.NSLOT
        gen = self.dgen.get((q, slot), 0)
        k = self.dcount.get((q, slot, gen), 0)
        if k > 0:
            self._emit_wait(q, ("d", q, slot, gen, k))
        if k >= self.DMA_EPOCH:
            gen += 1
            self.dgen[(q, slot)] = gen
            k = 0
        if (q, slot, gen) not in self.dsem:
            self.dsem[(q, slot, gen)] = self._newsem(f"d_{q}_{slot}_{gen}")
        for t in self._deps(reads, writes):
            self._emit_wait(q, t)
        k += 1
        self.dcount[(q, slot, gen)] = k
        self.eng[q].dma_start(out=out, in_=in_).then_inc(self.dsem[(q, slot, gen)], 16)
        tok = ("d", q, slot, gen, k)
        self._commit(tok, reads, writes)
        self.live_dma[(q, slot)] = tok
        return tok

    def barrier(self):
        for x in self.eng:
            for t in self.live_dma.values():
                self._emit_wait(x, t)
            for y in self.eng:
                if y != x and self.cnt[y] > 0:
                    self._emit_wait(x, ("c", y, self.cnt[y]))
        for x in ("act", "dve", "pool"):
            if self.cnt[x] > 0 and self.same_engine_sync:
                self._emit_wait(x, ("c", x, self.cnt[x]))

    def finish(self, e="sp"):
        for t in self.live_dma.values():
            self._emit_wait(e, t)
        for x in self.eng:
            if self.cnt[x] > 0 and x != e:
                self._emit_wait(e, ("c", x, self.cnt[x]))


def MM(out, lhsT, rhs, start=True, stop=True):
    return lambda e: e.matmul(out, lhsT=lhsT, rhs=rhs, start=start, stop=stop)


def TR(out, in_, ident):
    return lambda e: e.transpose(out, in_, ident)


def ACT(out, in_, func, **kw):
    return lambda e: e.activation(out=out, in_=in_, func=func, **kw)


def TT(out, a, b, op):
    return lambda e: e.tensor_tensor(out=out, in0=a, in1=b, op=op)


def TS(out, a, s1, op0, s2=None, op1=None):
    if op1 is None:
        return lambda e: e.tensor_scalar(out=out, in0=a, scalar1=s1, scalar2=None, op0=op0)
    return lambda e: e.tensor_scalar(out=out, in0=a, scalar1=s1, scalar2=s2, op0=op0, op1=op1)


def STT(out, a, s, b, op0, op1):
    return lambda e: e.scalar_tensor_tensor(out=out, in0=a, scalar=s, in1=b, op0=op0, op1=op1)


def CP(out, in_):
    return lambda e: e.tensor_copy(out, in_)


def SCAN(out, d0, d1, init):
    return lambda e: e.tensor_tensor_scan(out=out, data0=d0, data1=d1, initial=init,
                                          op0=ALU.mult, op1=ALU.add)


def RECIP(out, in_):
    return lambda e: e.reciprocal(out, in_)


def MEMSET(ap, v):
    return lambda e: e.memset(ap, v)


def _rope_tables():
    pos = np.zeros(G, np.float32)
    pos[NPAD:G - 128] = np.arange(G - 128 - NPAD, dtype=np.float32)
    pos[G - 128:] = 8192.0
    half = 8
    inv = np.power(np.float32(500000.0), -np.arange(half, dtype=np.float32) / np.float32(half)).astype(np.float32)
    ang = pos[None, :] * inv[:, None]
    cos8, sin8 = np.cos(ang).astype(np.float32), np.sin(ang).astype(np.float32)
    tab = np.zeros((2, 128, G), np.float32)
    tab[0] = 1.0
    for base in (0, 64):
        tab[0, base:base + 8] = cos8
        tab[0, base + 8:base + 16] = cos8
        tab[1, base:base + 8] = sin8
        tab[1, base + 8:base + 16] = sin8
    pm = np.zeros((128, 128), np.float32)
    for base in (0, 64):
        for d in range(8):
            pm[base + d + 8, base + d] = -1.0
            pm[base + d, base + d + 8] = 1.0
    return tab, pm


def _swa_masks():
    s = np.arange(128)[:, None]
    q = np.arange(128)[None, :]
    m = np.zeros((3, 128, 2, 128), np.float32)
    prev = (s >= q).astype(np.float32)
    cur = (s <= q).astype(np.float32)
    real = (s >= NPAD).astype(np.float32)
    m[0, :, 0], m[0, :, 1] = prev, cur
    m[1, :, 0], m[1, :, 1] = 0.0, cur * real
    m[2, :, 0], m[2, :, 1] = prev * real, cur
    return np.ascontiguousarray(m.transpose(1, 0, 2, 3).reshape(128, 3, 256))


def _prep_inputs(inp):
    f = lambda a: np.ascontiguousarray(np.asarray(a, dtype=np.float32))
    d = {}
    xall = np.zeros((G, D), np.float32)
    xall[NPAD:128] = inp["meta_tokens"]
    xall[128:G - 128] = inp["x_prompt"][0]
    xall[G - 128:] = inp["x_sample"][:, 0]
    d["xT"] = f(xall.T)
    vec16 = lambda v: f(np.asarray(v).reshape(16, 128).T)
    d["ln_in"] = f(np.stack([vec16(inp["ln_in_g"]), vec16(inp["ln_in_b"])]))
    w_in = np.asarray(inp["w_in"])
    win_all = np.zeros((L, NHG, D, NWC), np.float32)
    for hg in range(NHG):
        kvh, gh, half = hg // 2, hg // 2, hg % 2
        cols = np.concatenate([
            np.arange(128 * hg, 128 * hg + 128),
            np.arange(1024 + 64 * kvh, 1024 + 64 * kvh + 64),
            np.arange(1024 + 64 * kvh, 1024 + 64 * kvh + 64),
            np.arange(1280 + 64 * kvh, 1280 + 64 * kvh + 64),
            np.arange(1536 + 64 * hg, 1536 + 64 * hg + 64),
            np.arange(2048 + 64 * gh, 2048 + 64 * gh + 64),
            np.arange(2304 + 64 * gh, 2304 + 64 * gh + 64),
            np.arange(2560 + 128 * gh + 64 * half, 2560 + 128 * gh + 64 * half + 64),
            np.arange(3072 + 128 * gh + 64 * half, 3072 + 128 * gh + 64 * half + 64),
            np.arange(3584, 3600),
        ])
        win_all[:, hg] = w_in[:, :, cols]
    d["win_all"] = win_all
    tab, pm = _rope_tables()
    d["rope"] = tab
    d["pm"] = pm
    d["swamask"] = _swa_masks()
    d["ident"] = np.eye(128, dtype=np.float32)
    sink = np.asarray(inp["attn_sink"])
    d["sink"] = f(np.broadcast_to(sink.reshape(L, NHG, 1, 2), (L, NHG, 128, 2)))
    lr, li, ldt = (np.asarray(inp[k]) for k in ("s5_lam_re", "s5_lam_im", "s5_log_dt"))
    bre, bim = np.asarray(inp["s5_b_re"]), np.asarray(inp["s5_b_im"])
    cre, cim = np.asarray(inp["s5_c_re"]), np.asarray(inp["s5_c_im"])
    dsk = np.asarray(inp["s5_d"])
    s5col = np.zeros((L, NHG, 2, 128, 3), np.float32)
    s5row = np.zeros((L, NHG, 2, 64, 3, 128), np.float32)
    s5bz = np.zeros((L, NHG, 2, 64, 2, 128), np.float32)
    s5cz = np.zeros((L, NHG, 2, 128, 2, 64), np.float32)
    s5dz = np.zeros((L, NHG, 64, 64), np.float32)
    for l in range(L):
        for hg in range(NHG):
            for k in range(2):
                for j in range(2):
                    g = 4 * hg + 2 * k + j
                    gl = 2 * k + j
                    sl = slice(64 * j, 64 * j + 64)
                    s5col[l, hg, k, sl, 0] = lr[l, g]
                    s5col[l, hg, k, sl, 1] = li[l, g]
                    s5col[l, hg, k, sl, 2] = ldt[l, g]
                    s5row[l, hg, k, :, 0, sl] = lr[l, g][None, :]
                    s5row[l, hg, k, :, 1, sl] = li[l, g][None, :]
                    s5row[l, hg, k, :, 2, sl] = ldt[l, g]
                    s5bz[l, hg, k, 16 * gl:16 * gl + 16, 0, sl] = bre[l, g].T
                    s5bz[l, hg, k, 16 * gl:16 * gl + 16, 1, sl] = bim[l, g].T
                    s5cz[l, hg, k, sl, 0, 16 * gl:16 * gl + 16] = cre[l, g].T
                    s5cz[l, hg, k, sl, 1, 16 * gl:16 * gl + 16] = cim[l, g].T
            s5dz[l, hg][np.arange(64), np.arange(64)] = dsk[l, 4 * hg:4 * hg + 4].reshape(64)
    d["s5col"], d["s5row"], d["s5bz"], d["s5cz"], d["s5dz"] = s5col, s5row, s5bz, s5cz, s5dz
    wa2 = np.asarray(inp["gla_w_a2"])
    ba = np.asarray(inp["gla_b_a"])
    d["wa2"] = f(np.stack([[wa2[l][:, 64 * (hg // 2):64 * (hg // 2) + 64] for hg in range(NHG)] for l in range(L)]))
    d["ba"] = f(np.stack([[ba[l][64 * (hg // 2):64 * (hg // 2) + 64].reshape(64, 1) for hg in range(NHG)] for l in range(L)]))
    ck, cv = np.asarray(inp["cache_swa_k"]), np.asarray(inp["cache_swa_v"])
    d["ckT"] = f(ck.transpose(0, 3, 4, 1, 2))
    d["cvS"] = f(cv.transpose(0, 3, 2, 1, 4))
    d["ckN"] = f(ck.transpose(0, 3, 1, 2, 4))
    d["cvN"] = f(cv.transpose(0, 3, 1, 2, 4))
    sre, sim = np.asarray(inp["state_ssm_re"]), np.asarray(inp["state_ssm_im"])
    ssm0 = np.stack([sre, sim], 1)
    d["ssm0"] = f(ssm0.reshape(L, 2, 128, 16, 128).transpose(0, 1, 3, 4, 2))
    gst = np.asarray(inp["state_gla"])
    d["gst"] = f(gst.reshape(L, 128, 4, 64, 2, 64).transpose(0, 2, 4, 3, 1, 5).reshape(L, NHG, 64, 128, 64))
    cpv = np.asarray(inp["state_conv"])
    d["cprevT"] = f(cpv.transpose(0, 2, 3, 1))
    d["wout"] = f(inp["w_out"])
    d["wup"] = f(inp["ffn_w_up"])
    d["wdown"] = f(inp["ffn_w_down"])
    d["wglu"] = f(inp["s5_w_glu"])
    d["ln1"] = f(np.stack([np.stack([vec16(inp["ln1_g"][l]), vec16(inp["ln1_b"][l])]) for l in range(L)]))
    d["ln2"] = f(np.stack([np.stack([vec16(inp["ln2_g"][l]), vec16(inp["ln2_b"][l])]) for l in range(L)]))
    cw, cb = np.asarray(inp["ffn_conv_w"]), np.asarray(inp["ffn_conv_b"])
    d["convw"] = f(np.stack([np.stack([cw[l, j].reshape(NFC, 128).T for j in range(3)] + [cb[l].reshape(NFC, 128).T]) for l in range(L)]))
    d["bglu"] = f(np.stack([np.asarray(inp["s5_b_glu"][l]).reshape(4, 128).T for l in range(L)]))
    d["gnorm"] = f(np.stack([np.asarray(inp["gla_norm_g"][l]).reshape(4, 128).T for l in range(L)]))
    return d


INPUT_SHAPES = None


def chunked(ap2d):
    return ap2d.rearrange("(c p) n -> p c n", p=128)


class Builder:
    def __init__(self, shapes, debug=False):
        self.debug = debug
        self.nc = nc = bass.Bass("TRN2", target_bir_lowering=False)
        self.I = {}
        for name, shp in shapes.items():
            self.I[name] = nc.dram_tensor(name, list(shp), F32, kind="ExternalInput").ap()
        O = self.O = {}

        def out(name, shp):
            O[name] = nc.dram_tensor(name, list(shp), F32, kind="ExternalOutput").ap()
        out("y_out", [G, D])
        out("ks_out", [L, 4, 128, 128, 64])
        out("vs_out", [L, 4, 128, 128, 64])
        out("kp_out", [L, 4, 128, 64])
        out("vp_out", [L, 4, 128, 64])
        out("ssmp_out", [L, 2, 16, 128])
        out("ssms_out", [L, 2, 16, 128, 128])
        out("glap_out", [L, NHG, 64, 64])
        out("glas_out", [L, NHG, 64, 128, 64])
        out("convp_out", [L, 128, NFC, 2])
        out("convs_out", [L, 2, DFF, 128])
        kind = "ExternalOutput" if debug else "Internal"
        self.xn_d = nc.dram_tensor("xn_d", [D, G], BF16, kind=kind).ap()
        self.xres_d = nc.dram_tensor("xres_d", [D, G], F32, kind=kind).ap()
        self.mbuf = nc.dram_tensor("mbuf", [NHG, BROWS, G], BF16, kind=kind).ap()
        self.vs_scr = nc.dram_tensor("vs_scr", [128, 64], F32).ap()
        self.CONV = ["ident", "pm", "win_all", "s5dz", "wa2", "ckT", "cvS", "wglu", "wout", "wup", "wdown"]
        self.Wb = {k: nc.dram_tensor(k + "_bf", list(shapes[k]), BF16).ap() for k in self.CONV}

    def T(self, es, name, shape, dt):
        self._n = getattr(self, "_n", 0) + 1
        return es.enter_context(self.nc.sbuf_tensor(f"{name}_{self._n}", shape, dt))

    def build(self, n_layers=L, stop_after=None):
        nc = self.nc
        with ExitStack() as es:
            self.P = P = Prog(nc, es)
            self.es = es
            self.ps = [es.enter_context(nc.psum_tensor(f"ps{i}", [128, 512], F32)) for i in range(6)]
            self.psb = [es.enter_context(nc.psum_tensor(f"psb{i}", [128, 1024], BF16)) for i in range(2)]
            self.ps_i = 0
            self.psb_i = 0
            self.convert_weights()
            self.setup_consts()
            if stop_after != "consts":
                self.p0()
            if stop_after in ("p0", "consts"):
                n_layers = 0
            for l in range(n_layers):
                for hg in range(NHG if not isinstance(stop_after, tuple) or len(stop_after) < 3 else stop_after[2]):
                    self.p1(l, hg)
                if isinstance(stop_after, tuple) and stop_after[:2] == ("p1", l):
                    break
                for s in range((G + W2 - 1) // W2):
                    self.p2(l, s, last=(l == n_layers - 1))
            P.finish("sp")
        return nc

    def convert_weights(self):
        P = self.P
        for k in self.CONV:
            src, dst = self.I[k], self.Wb[k]
            nd = len(src.shape)
            names = " ".join(f"d{i}" for i in range(nd))
            pat = f"{names} -> ({' '.join(f'd{i}' for i in range(nd - 1))}) d{nd - 1}"
            s2, d2 = src.rearrange(pat), dst.rearrange(pat)
            rows = s2.shape[0]
            step = max(1, min(rows, (1 << 22) // max(1, s2.shape[1])))
            for r0 in range(0, rows, step):
                r1 = min(rows, r0 + step)
                P.dma("pool", d2[r0:r1, :], s2[r0:r1, :])
        P.barrier()

    def bank(self):
        i = self.ps_i
        self.ps_i = (i + 1) % len(self.ps)
        return self.ps[i], f"ps{i}"

    def bankb(self):
        i = self.psb_i
        self.psb_i = (i + 1) % len(self.psb)
        return self.psb[i], f"psb{i}"

    def setup_consts(self):
        P, es, I = self.P, self.es, self.I
        T = lambda n, s, d: self.T(es, n, s, d)
        self.ident_f = T("ident_f", [128, 128], F32)
        self.ident_b = T("ident_b", [128, 128], BF16)
        self.pm_b = T("pm_b", [128, 128], BF16)
        self.mask_f = T("mask_f", [128, 3, 256], F32)
        self.ones_f = T("ones_f", [128, 512], F32)
        self.ones_b = T("ones_b", [128, 128], BF16)
        self.iota1 = T("iota1", [128, 512], F32)
        self.iota_i = T("iota_i", [128, 512], I32)
        self.resetm = T("resetm", [128, 512], F32)
        self.blk2 = T("blk2", [128, 2], F32)
        self.sel65 = T("sel65", [128, 64], F32)
        P.dma("sp", self.ident_f[:], I["ident"][:, :], writes=["ident_f"])
        P.dma("sp", self.ident_b[:], self.Wb["ident"][:, :], writes=["ident_b"])
        P.dma("sp", self.pm_b[:], self.Wb["pm"][:, :], writes=["pm_b"])
        P.dma("sp", self.mask_f[:], I["swamask"][:, :, :], writes=["mask_f"])
        P.op("dve", MEMSET(self.ones_f[:], 1.0), writes=["ones_f"])
        P.op("dve", MEMSET(self.ones_b[:], 1.0), writes=["ones_b"])
        P.op("pool", lambda e: e.iota(self.iota_i[:], pattern=[[1, 512]], base=1, channel_multiplier=0), writes=["iota_i"])
        P.op("dve", CP(self.iota1[:], self.iota_i[:]), reads=["iota_i"], writes=["iota1"])
        P.op("dve", MEMSET(self.resetm[:], 1.0), writes=["resetm"])
        for c in range(4):
            P.op("dve", MEMSET(self.resetm[:, c * 128:c * 128 + 1], 0.0), writes=["resetm"])
        P.op("dve", MEMSET(self.blk2[:], 0.0), writes=["blk2"])
        P.op("dve", MEMSET(self.blk2[0:64, 0:1], 1.0), writes=["blk2"])
        P.op("dve", MEMSET(self.blk2[64:128, 1:2], 1.0), writes=["blk2"])
        P.op("dve", MEMSET(self.sel65[:], 0.0), writes=["sel65"])
        P.op("dve", MEMSET(self.sel65[64:65, :], 1.0), writes=["sel65"])

    def sincos(self, ang, ang_key, sin_out, cos_out, out_key, tmp_f, tmp_i, tmp_key, np_, nf):
        P = self.P
        a = ang
        for shift, dst in ((0.0, sin_out), (math.pi / 2, cos_out)):
            if dst is None:
                continue
            P.op("dve", TS(tmp_i, a, 1.0 / TWO_PI, ALU.mult, shift / TWO_PI, ALU.add), reads=[ang_key], writes=[tmp_key + "i"])
            P.op("dve", CP(tmp_f, tmp_i), reads=[tmp_key + "i"], writes=[tmp_key + "f"])
            P.op("dve", STT(dst, tmp_f, -C1, a, ALU.mult, ALU.add), reads=[tmp_key + "f", ang_key], writes=[out_key])
            P.op("dve", STT(dst, tmp_f, -C2, dst, ALU.mult, ALU.add), reads=[tmp_key + "f", out_key], writes=[out_key])
            if shift:
                P.op("dve", TS(dst, dst, shift, ALU.add), reads=[out_key], writes=[out_key])
            P.op("dve", TS(dst, dst, -math.pi, ALU.max, math.pi, ALU.min), reads=[out_key], writes=[out_key])
            P.op("act", ACT(dst, dst, AF.Sin), reads=[out_key], writes=[out_key])

    def layernorm(self, es, xr, xr_key, ncols, gb, gb_key, xb=None, xb_key=None, mask_pad=0):
        P = self.P
        T = lambda n, s, d: self.T(es, n, s, d)
        sq = T("ln_sq", [128, W2], F32)
        mean = T("ln_mean", [128, W2], F32)
        rstd = T("ln_rstd", [128, W2], F32)
        shift = T("ln_shift", [128, W2], F32)
        tmp = T("ln_tmp", [128, W2], F32)
        nblk = (ncols + 511) // 512
        for b in range(nblk):
            c0, c1 = b * 512, min(ncols, b * 512 + 512)
            n = c1 - c0
            ps_s, ks = self.bank()
            ps_q, kq = self.bank()
            fs, fq = [], []
            for k in range(16):
                P.op("act", ACT(sq[:, 0:n], xr[:, k, c0:c1], AF.Square), reads=[xr_key], writes=["ln_sq"])
                P.op("pe", [MM(ps_s[:, 0:n], self.ones_f[:, 0:128], xr[:, k, c0:c1], start=(k == 0), stop=(k == 15)),
                            MM(ps_q[:, 0:n], self.ones_f[:, 0:128], sq[:, 0:n], start=(k == 0), stop=(k == 15))],
                     reads=["ones_f", xr_key, "ln_sq"], writes=[ks, kq])
            P.op("act", ACT(mean[:, c0:c1], ps_s[:, 0:n], AF.Copy, scale=1.0 / D), reads=[ks], writes=["ln_mean"])
            P.op("dve", TT(tmp[:, c0:c1], mean[:, c0:c1], mean[:, c0:c1], ALU.mult), reads=["ln_mean"], writes=["ln_tmp"])
            P.op("dve", STT(tmp[:, c0:c1], ps_q[:, 0:n], 1.0 / D, tmp[:, c0:c1], ALU.mult, ALU.subtract), reads=[kq, "ln_tmp"], writes=["ln_tmp"])
            P.op("dve", TS(tmp[:, c0:c1], tmp[:, c0:c1], EPS, ALU.add), reads=["ln_tmp"], writes=["ln_tmp"])
            P.op("act", ACT(tmp[:, c0:c1], tmp[:, c0:c1], AF.Sqrt), reads=["ln_tmp"], writes=["ln_tmp"])
            P.op("dve", RECIP(rstd[:, c0:c1], tmp[:, c0:c1]), reads=["ln_tmp"], writes=["ln_rstd"])
            P.op("dve", STT(shift[:, c0:c1], mean[:, c0:c1], -1.0, rstd[:, c0:c1], ALU.mult, ALU.mult), reads=["ln_mean", "ln_rstd"], writes=["ln_shift"])
        for k in range(16):
            xk = xr[:, k, 0:ncols]
            P.op("dve", TT(xk, xk, rstd[:, 0:ncols], ALU.mult), reads=[xr_key, "ln_rstd"], writes=[xr_key])
            P.op("dve", TT(xk, xk, shift[:, 0:ncols], ALU.add), reads=[xr_key, "ln_shift"], writes=[xr_key])
            P.op("act", ACT(xk, xk, AF.Identity, scale=gb[:, 0, k:k + 1], bias=gb[:, 1, k:k + 1]), reads=[xr_key, gb_key], writes=[xr_key])
            if xb is not None:
                P.op("dve", CP(xb[:, k, 0:ncols], xk), reads=[xr_key], writes=[xb_key])
                if mask_pad:
                    P.op("dve", MEMSET(xb[:, k, 0:mask_pad], 0.0), writes=[xb_key])

    def p0(self):
        P, I = self.P, self.I
        with ExitStack() as es:
            T = lambda n, s, d: self.T(es, n, s, d)
            xr = T("p0_xr", [128, 16, W2], F32)
            xb = T("p0_xb", [128, 16, W2], BF16)
            gb = T("p0_gb", [128, 2, 16], F32)
            P.dma("sp", gb[:], I["ln_in"].rearrange("t p k -> p t k"), writes=["p0_gb"])
            for s in range((G + W2 - 1) // W2):
                c0, c1 = s * W2, min(G, s * W2 + W2)
                n = c1 - c0
                P.dma("sp", xr[:, :, 0:n], chunked(I["xT"])[:, :, c0:c1], writes=["p0_xr"])
                with ExitStack() as es2:
                    self.layernorm(es2, xr, "p0_xr", n, gb, "p0_gb", xb, "p0_xb", mask_pad=(NPAD if s == 0 else 0))
                    P.dma("sp", chunked(self.xres_d)[:, :, c0:c1], xr[:, :, 0:n], reads=["p0_xr"], writes=[("xres", s)])
                    P.dma("sp", chunked(self.xn_d)[:, :, c0:c1], xb[:, :, 0:n], reads=["p0_xb"], writes=["xn_d"])
                    P.barrier()
            P.barrier()

    def p1(self, l, hg):
        P, I, O = self.P, self.I, self.O
        kvh = hg // 2
        NT = (G + W1 - 1) // W1
        with ExitStack() as es:
            T = lambda n, s, d: self.T(es, n, s, d)
            win = T("win", [128, 16, 768], BF16)
            P.dma("sp", win[:, :, 0:NWC], chunked(self.Wb["win_all"][l, hg]), writes=["win"])
            sinkn = T("sinkn", [128, 2], F32)
            P.dma("sp", sinkn[:], I["sink"][l, hg], writes=["sinkn"])
            P.op("dve", TS(sinkn[:], sinkn[:], -1.0, ALU.mult), reads=["sinkn"], writes=["sinkn"])
            Ec, Es, magt, bbre, bbim, czre, nczim, arc, aic, naic = [], [], [], [], [], [], [], [], [], []
            pre = {}
            for k in range(2):
                pre[("arc", k)], pre[("aic", k)], pre[("naic", k)] = T("arc", [128, 1], F32), T("aic", [128, 1], F32), T("naic", [128, 1], F32)
                pre[("Ec", k)], pre[("Es", k)], pre[("magt", k)] = T("Ec", [128, 512], F32), T("Es", [128, 512], F32), T("magt", [128, 512], F32)
                pre[("bbre", k)], pre[("bbim", k)] = T("bbre", [128, 128], BF16), T("bbim", [128, 128], BF16)
                pre[("czre", k)], pre[("nczim", k)] = T("czre", [128, 64], BF16), T("nczim", [128, 64], BF16)
            dz_b = T("dz_b", [128, 64], BF16)
            wa2_b = T("wa2_b", [16, 64], BF16)
            nba = T("nba", [64, 1], F32)
            with ExitStack() as esp:
                Tp = lambda n, s, d: self.T(esp, n, s, d)
                ang = Tp("ang", [128, 512], F32)
                tf = Tp("sc_tf", [128, 512], F32)
                ti = Tp("sc_ti", [128, 512], I32)
                for k in range(0 if 's5prep' in DBG_SKIP else 2):
                    cp = Tp("cp", [128, 3], F32)
                    sm = Tp("sm", [128, 12], F32)
                    P.dma("sp", cp[:], I["s5col"][l, hg, k], writes=["cp"])
                    dtc, thc, lrdt, magc, sc_, cc_ = (sm[:, j:j + 1] for j in range(6))
                    P.op("act", ACT(dtc, cp[:, 2:3], AF.Exp), reads=["cp"], writes=["sm"])
                    P.op("dve", TT(thc, cp[:, 1:2], dtc, ALU.mult), reads=["cp", "sm"], writes=["sm"])
                    P.op("dve", TT(lrdt, cp[:, 0:1], dtc, ALU.mult), reads=["cp", "sm"], writes=["sm"])
                    P.op("act", ACT(magc, lrdt, AF.Exp), reads=["sm"], writes=["sm"])
                    sct = Tp("sct", [128, 2], F32)
                    self.sincos(thc, "sm", sct[:, 0:1], sct[:, 1:2], "sct", tf[:, 0:1], ti[:, 0:1], "sc_t", 128, 1)
                    a_r, a_i, na_i = pre[("arc", k)], pre[("aic", k)], pre[("naic", k)]
                    P.op("dve", TT(a_r[:], magc, sct[:, 1:2], ALU.mult), reads=["sm", "sct"], writes=["arc"])
                    P.op("dve", TT(a_i[:], magc, sct[:, 0:1], ALU.mult), reads=["sm", "sct"], writes=["aic"])
                    P.op("dve", TS(na_i[:], a_i[:], -1.0, ALU.mult), reads=["aic"], writes=["naic"])
                    arc.append(a_r); aic.append(a_i); naic.append(na_i)
                    ec, esn, mt = pre[("Ec", k)], pre[("Es", k)], pre[("magt", k)]
                    P.op("dve", TS(ang[:], self.iota1[:], thc, ALU.mult), reads=["iota1", "sm"], writes=["ang"])
                    self.sincos(ang[:], "ang", esn[:], ec[:], f"EcEs{k}", tf[:], ti[:], "sc_t", 128, 512)
                    P.op("dve", TS(mt[:], self.ones_f[:], magc, ALU.mult), reads=["ones_f", "sm"], writes=[f"magt{k}"])
                    Ec.append(ec); Es.append(esn); magt.append(mt)
                    rp = Tp("rp", [128, 3, 128], F32)
                    bz = Tp("bz", [128, 2, 128], F32)
                    r = Tp("rwork", [128, 12, 128], F32)
                    P.dma("sp", rp[64:128], I["s5row"][l, hg, k], writes=["rp"])
                    P.dma("sp", bz[64:128], I["s5bz"][l, hg, k], writes=["bz"])
                    h = slice(64, 128)
                    lrr, lir, ldr = rp[h, 0, :], rp[h, 1, :], rp[h, 2, :]
                    R_ = lambda j: r[h, j, :]
                    rk = ["rp", "rwork"]
                    P.op("act", ACT(R_(0), ldr, AF.Exp), reads=rk, writes=["rwork"])
                    P.op("dve", TT(R_(1), lir, R_(0), ALU.mult), reads=rk, writes=["rwork"])
                    P.op("dve", TT(R_(2), lrr, R_(0), ALU.mult), reads=rk, writes=["rwork"])
                    P.op("act", ACT(R_(2), R_(2), AF.Exp), reads=rk, writes=["rwork"])
                    self.sincos(R_(1), "rwork", R_(3), R_(4), "rwork", tf[h, 0:128], ti[h, 0:128], "sc_t", 64, 128)
                    P.op("dve", TT(R_(4), R_(4), R_(2), ALU.mult), reads=rk, writes=["rwork"])
                    P.op("dve", TT(R_(3), R_(3), R_(2), ALU.mult), reads=rk, writes=["rwork"])
                    P.op("dve", TS(R_(4), R_(4), -1.0, ALU.add), reads=rk, writes=["rwork"])
                    P.op("dve", TT(R_(5), lrr, lrr, ALU.mult), reads=rk, writes=["rwork"])
                    P.op("dve", TT(R_(6), lir, lir, ALU.mult), reads=rk, writes=["rwork"])
                    P.op("dve", TT(R_(5), R_(5), R_(6), ALU.add), reads=rk, writes=["rwork"])
                    P.op("dve", RECIP(R_(5), R_(5)), reads=rk, writes=["rwork"])
                    P.op("dve", TT(R_(6), R_(4), lrr, ALU.mult), reads=rk, writes=["rwork"])
                    P.op("dve", TT(R_(7), R_(3), lir, ALU.mult), reads=rk, writes=["rwork"])
                    P.op("dve", TT(R_(6), R_(6), R_(7), ALU.add), reads=rk, writes=["rwork"])
                    P.op("dve", TT(R_(6), R_(6), R_(5), ALU.mult), reads=rk, writes=["rwork"])
                    P.op("dve", TT(R_(7), R_(3), lrr, ALU.mult), reads=rk, writes=["rwork"])
                    P.op("dve", TT(R_(8), R_(4), lir, ALU.mult), reads=rk, writes=["rwork"])
                    P.op("dve", TT(R_(7), R_(7), R_(8), ALU.subtract), reads=rk, writes=["rwork"])
                    P.op("dve", TT(R_(7), R_(7), R_(5), ALU.mult), reads=rk, writes=["rwork"])
                    b_re, b_im = pre[("bbre", k)], pre[("bbim", k)]
                    rb = ["rwork", "bz"]
                    P.op("dve", TT(R_(8), R_(6), bz[h, 0, :], ALU.mult), reads=rb, writes=["rwork"])
                    P.op("dve", TT(R_(9), R_(7), bz[h, 1, :], ALU.mult), reads=rb, writes=["rwork"])
                    P.op("dve", TT(b_re[h, :], R_(8), R_(9), ALU.subtract), reads=rb, writes=[f"bb{k}"])
                    P.op("dve", TT(R_(8), R_(6), bz[h, 1, :], ALU.mult), reads=rb, writes=["rwork"])
                    P.op("dve", TT(R_(9), R_(7), bz[h, 0, :], ALU.mult), reads=rb, writes=["rwork"])
                    P.op("dve", TT(b_im[h, :], R_(8), R_(9), ALU.add), reads=rb, writes=[f"bb{k}"])
                    bbre.append(b_re); bbim.append(b_im)
                    czf = Tp("czf", [128, 2, 64], F32)
                    P.dma("sp", czf[:], I["s5cz"][l, hg, k], writes=["czf"])
                    c_re, nc_im = pre[("czre", k)], pre[("nczim", k)]
                    P.op("dve", CP(c_re[:], czf[:, 0, :]), reads=["czf"], writes=[f"cz{k}"])
                    P.op("dve", TS(nc_im[:], czf[:, 1, :], -1.0, ALU.mult), reads=["czf"], writes=[f"cz{k}"])
                    czre.append(c_re); nczim.append(nc_im)
                P.dma("sp", dz_b[64:128, :], self.Wb["s5dz"][l, hg], writes=["dz_b"])
                P.dma("sp", wa2_b[:], self.Wb["wa2"][l, hg], writes=["wa2_b"])
                P.dma("sp", nba[:], I["ba"][l, hg], writes=["nba"])
                P.op("dve", TS(nba[:], nba[:], -1.0, ALU.mult), reads=["nba"], writes=["nba"])
                P.barrier()
            xt = [T("xt", [128, 16, W1], BF16) for _ in range(2)]
            ropet = [T("ropet", [128, 2, W1], F32) for _ in range(2)]
            qT = T("qT", [128, W1], BF16)
            qb = T("qb", [128, W1], BF16)
            kT = T("kT", [128, 128 + W1], BF16)
            kf = T("kf", [128, 256], F32)
            vf = T("vf", [128, 256], F32)
            t1 = T("t1", [128, W1], F32)
            t2 = T("t2", [128, W1], F32)
            t3 = T("t3", [128, W1], F32)
            t4 = T("t4", [128, W1], F32)
            gre = T("gre", [128, W1], F32)
            gim = T("gim", [128, W1], F32)
            VUb = T("VUb", [128, W1], BF16)
            Vtm = T("Vtm", [128, 5, 65], BF16)
            oa2 = T("oa2", [128, 128], BF16)
            oaT_st = T("oaT_st", [128, W1], BF16)
            pexp = [T("pexp", [128, 256], F32) for _ in range(2)]
            PTt = [T("PT", [128, 256], BF16) for _ in range(2)]
            rden = T("rden", [128, 4], F32)
            hprev = T("hprev", [128, 4], F32)
            hre_b = [T("hre_b", [128, W1], BF16) for _ in range(2)]
            him_b = [T("him_b", [128, W1], BF16) for _ in range(2)]
            z_st = T("z_st", [64, W1], BF16)
            qg_f = T("qg_f", [64, W1], F32)
            kg_f = T("kg_f", [64, W1], F32)
            VGb = T("VGb", [128, W1], BF16)
            vgf = T("vgf", [64, 128], F32)
            acb = T("acb", [16, W1], BF16)
            lg = T("lg", [64, W1], F32)
            cs = T("cs", [64, W1], F32)
            Ep = T("Ep", [64, W1], F32)
            Em = T("Em", [64, W1], F32)
            qin_b = T("qin_b", [64, W1], BF16)
            kout_f = T("kout_f", [64, W1], F32)
            kout_b = T("kout_b", [64, W1], BF16)
            kend_b = T("kend_b", [64, W1], BF16)
            attT_b = [T("attT_b", [128, 128], BF16) for _ in range(2)]
            vtm_b = [T("vtm_b", [128, 64], BF16) for _ in range(2)]
            kend_tm = [T("kend_tm", [128, 64], BF16) for _ in range(2)]
            S_f = T("S_f", [64, 64], F32)
            S_b = T("S_b", [64, 64], BF16)
            o_st = T("o_st", [64, W1], BF16)
            if 'initoa' in DBG_SKIP:
                P.op("dve", MEMSET(oaT_st[:], 0.0), writes=["oaT_st"])
            P.op("dve", MEMSET(kT[:, 0:128], 0.0), writes=["kT"])
            P.op("dve", MEMSET(Vtm[:], 0.0), writes=["Vtm"])
            P.op("dve", MEMSET(Vtm[:, :, 64:65], 1.0), writes=["Vtm"])
            P.op("dve", MEMSET(hprev[:], 0.0), writes=["hprev"])
            P.op("dve", MEMSET(S_f[:], 0.0), writes=["S_f"])
            P.op("dve", MEMSET(S_b[:], 0.0), writes=["S_b"])

            def load_tile(t):
                c0 = t * W1
                n = min(W1, G - c0)
                P.dma("sp", xt[t % 2][:, :, 0:n], chunked(self.xn_d)[:, :, c0:c0 + n], reads=["xn_d"], writes=[f"xt{t % 2}"])
                if 'ropedma' not in DBG_SKIP:
                    P.dma("sp", ropet[t % 2][:, :, 0:n], I["rope"][:, :, c0:c0 + n].rearrange("t p n -> p t n"), writes=[f"ropet{t % 2}"])

            if 'tiles' in DBG_SKIP:
                NT = 0
            else:
                load_tile(0)
            for t in range(NT):
                c0 = t * W1
                n = min(W1, G - c0)
                nblk = n // 128
                last = (t == NT - 1)
                if t + 1 < NT:
                    load_tile(t + 1)
                X, xk_ = xt[t % 2], f"xt{t % 2}"
                RT, rk_ = ropet[t % 2], f"ropet{t % 2}"

                def proj(c_lo, c_hi, m):
                    ps, key = self.bank()
                    P.op("pe", [MM(ps[0:m, 0:n], win[:, k, c_lo:c_hi], X[:, k, 0:n], start=(k == 0), stop=(k == 15)) for k in range(16)],
                         reads=["win", xk_], writes=[key])
                    return ps, key

                def roped(c_lo, dst, dst_key, f32_dst=None):
                    ps, key = proj(c_lo, c_lo + 128, 128)
                    P.op("act", ACT(qb[:, 0:n], ps[:, 0:n], AF.Copy), reads=[key], writes=["qb"])
                    ps2, key2 = self.bank()
                    P.op("pe", MM(ps2[:, 0:n], self.pm_b[:], qb[:, 0:n]), reads=["pm_b", "qb"], writes=[key2])
                    P.op("dve", TT(t1[:, 0:n], ps[:, 0:n], RT[:, 0, 0:n], ALU.mult), reads=[key, rk_], writes=["t1"])
                    P.op("dve", TT(t2[:, 0:n], ps2[:, 0:n], RT[:, 1, 0:n], ALU.mult), reads=[key2, rk_], writes=["t2"])
                    P.op("dve", TT(dst, t1[:, 0:n], t2[:, 0:n], ALU.add), reads=["t1", "t2"], writes=[dst_key])
                    if f32_dst is not None:
                        P.op("dve", TT(f32_dst, t1[:, 0:n], t2[:, 0:n], ALU.add), reads=["t1", "t2"], writes=["kf"])

                if 'rope' not in DBG_SKIP:
                    roped(0, qT[:, 0:n], "qT")
                    roped(128, kT[:, 128:128 + n], "kT", f32_dst=(kf[:, 0:n] if last else None))
                if 'vu' in DBG_SKIP:
                    continue
                ps_vu, kvu = proj(256, 384, 128)
                P.op("act", ACT(VUb[:, 0:n], ps_vu[:, 0:n], AF.Copy), reads=[kvu], writes=["VUb"])
                if last and 'novf' not in DBG_SKIP:
                    P.op("act", ACT(vf[0:64, 0:n], ps_vu[0:64, 0:n], AF.Copy), reads=[kvu], writes=["vf"])
                for i in range(0 if 'vtr' in DBG_SKIP else nblk):
                    pb_, kb_ = self.bankb()
                    P.op("pe", TR(pb_[:, 0:64], VUb[0:64, i * 128:(i + 1) * 128], self.ident_b[0:64, 0:64]), reads=["VUb", "ident_b"], writes=[kb_])
                    P.op("act", ACT(Vtm[:, i + 1, 0:64], pb_[:, 0:64], AF.Copy), reads=[kb_], writes=["Vtm"])
                for i in range(0 if 'swa' in DBG_SKIP else nblk):
                    gb = 4 * t + i
                    if gb == NB - 1:
                        continue
                    mv = 1 if gb == 0 else (2 if gb == 1 else 0)
                    for hh in range(2):
                        pb = 64 * hh
                        ps_s, ks_ = self.bank()
                        qs = qT[pb:pb + 64, i * 128:(i + 1) * 128]
                        P.op("pe", [MM(ps_s[:, 0:128], kT[pb:pb + 64, i * 128:(i + 1) * 128], qs),
                                    MM(ps_s[:, 128:256], kT[pb:pb + 64, (i + 1) * 128:(i + 2) * 128], qs)],
                             reads=["kT", "qT"], writes=[ks_])
                        P.op("act", ACT(pexp[hh][:], ps_s[:, 0:256], AF.Exp, scale=0.125, bias=sinkn[:, hh:hh + 1]), reads=[ks_, "sinkn"], writes=[f"pexp{hh}"])
                        P.op("dve", TT(PTt[hh][:], pexp[hh][:], self.mask_f[:, mv, :], ALU.mult), reads=[f"pexp{hh}", "mask_f"], writes=[f"PT{hh}"])
                        ps_o, ko_ = self.bank()
                        P.op("pe", [MM(ps_o[:, 0:65], PTt[hh][:, 0:128], Vtm[:, i, :], start=True, stop=False),
                                    MM(ps_o[:, 0:65], PTt[hh][:, 128:256], Vtm[:, i + 1, :], start=False, stop=True)],
                             reads=[f"PT{hh}", "Vtm"], writes=[ko_])
                        P.op("dve", TS(rden[:, hh:hh + 1], ps_o[:, 64:65], 1.0, ALU.add), reads=[ko_], writes=["rden"])
                        P.op("dve", RECIP(rden[:, hh:hh + 1], rden[:, hh:hh + 1]), reads=["rden"], writes=["rden"])
                        P.op("act", ACT(oa2[:, pb:pb + 64], ps_o[:, 0:64], AF.Copy, scale=rden[:, hh:hh + 1]), reads=[ko_, "rden"], writes=["oa2"])
                    pb_, kb_ = self.bankb()
                    P.op("pe", TR(pb_[:, 0:128], oa2[:], self.ident_b[:]), reads=["oa2", "ident_b"], writes=[kb_])
                    P.op("act", ACT(oaT_st[:, i * 128:(i + 1) * 128], pb_[:, 0:128], AF.Copy), reads=[kb_], writes=["oaT_st"])
                npr = n - 128 if last else n
                if 'nombuf' not in DBG_SKIP:
                    P.dma("sp", self.mbuf[hg, 0:128, c0:c0 + npr], oaT_st[:, 0:npr], reads=["oaT_st"])
                for k in range(0 if 's5' in DBG_SKIP else 2):
                    ps_re, kre = self.bank()
                    ps_im, kim = self.bank()
                    P.op("pe", MM(ps_re[:, 0:n], bbre[k][64:128, :], VUb[64:128, 0:n]), reads=[f"bb{k}", "VUb"], writes=[kre])
                    P.op("pe", MM(ps_im[:, 0:n], bbim[k][64:128, :], VUb[64:128, 0:n]), reads=[f"bb{k}", "VUb"], writes=[kim])
                    ek = f"EcEs{k}"
                    ec, esn = Ec[k][:, 0:n], Es[k][:, 0:n]
                    P.op("dve", TT(t1[:, 0:n], ps_re[:, 0:n], ec, ALU.mult), reads=[kre, ek], writes=["t1"])
                    P.op("dve", TT(t2[:, 0:n], ps_im[:, 0:n], esn, ALU.mult), reads=[kim, ek], writes=["t2"])
                    P.op("dve", TT(t1[:, 0:n], t1[:, 0:n], t2[:, 0:n], ALU.add), reads=["t1", "t2"], writes=["t1"])
                    P.op("dve", TT(t3[:, 0:n], ps_im[:, 0:n], ec, ALU.mult), reads=[kim, ek], writes=["t3"])
                    P.op("dve", TT(t4[:, 0:n], ps_re[:, 0:n], esn, ALU.mult), reads=[kre, ek], writes=["t4"])
                    P.op("dve", TT(t3[:, 0:n], t3[:, 0:n], t4[:, 0:n], ALU.subtract), reads=["t3", "t4"], writes=["t3"])
                    P.op("dve", SCAN(gre[:, 0:n], magt[k][:, 0:n], t1[:, 0:n], hprev[:, 2 * k:2 * k + 1]), reads=[f"magt{k}", "t1", "hprev"], writes=["gre"])
                    P.op("dve", SCAN(gim[:, 0:n], magt[k][:, 0:n], t3[:, 0:n], hprev[:, 2 * k + 1:2 * k + 2]), reads=[f"magt{k}", "t3", "hprev"], writes=["gim"])
                    P.op("dve", TT(t1[:, 0:n], gre[:, 0:n], ec, ALU.mult), reads=["gre", ek], writes=["t1"])
                    P.op("dve", TT(t2[:, 0:n], gim[:, 0:n], esn, ALU.mult), reads=["gim", ek], writes=["t2"])
                    P.op("dve", TT(hre_b[k][:, 0:n], t1[:, 0:n], t2[:, 0:n], ALU.subtract), reads=["t1", "t2"], writes=[f"hre{k}"])
                    P.op("dve", TT(t3[:, 0:n], gre[:, 0:n], esn, ALU.mult), reads=["gre", ek], writes=["t3"])
                    P.op("dve", TT(t4[:, 0:n], gim[:, 0:n], ec, ALU.mult), reads=["gim", ek], writes=["t4"])
                    P.op("dve", TT(him_b[k][:, 0:n], t3[:, 0:n], t4[:, 0:n], ALU.add), reads=["t3", "t4"], writes=[f"him{k}"])
                    cc = npr - 1
                    cs1 = slice(cc, cc + 1)
                    P.op("dve", TT(hprev[:, 2 * k:2 * k + 1], t1[:, cs1], t2[:, cs1], ALU.subtract), reads=["t1", "t2"], writes=["hprev"])
                    P.op("dve", TT(hprev[:, 2 * k + 1:2 * k + 2], t3[:, cs1], t4[:, cs1], ALU.add), reads=["t3", "t4"], writes=["hprev"])
                    if last:
                        for ri in range(2):
                            P.dma("sp", O["ssmp_out"][l, ri, 2 * hg + k].rearrange("(p o) -> p o", o=1), hprev[:, 2 * k + ri:2 * k + ri + 1], reads=["hprev"])
                        h0 = T("h0", [128, 2, 128], F32)
                        hs = T("hs", [128, 2, 128], F32)
                        for ri in range(2):
                            P.dma("sp", h0[:, ri, :], I["ssm0"][l, ri, 2 * hg + k], writes=["h0"])
                        sc = slice(128, 256)
                        P.op("dve", TS(t1[:, 0:128], h0[:, 0, :], arc[k][:], ALU.mult), reads=["h0", "arc"], writes=["t1"])
                        P.op("dve", STT(t1[:, 0:128], h0[:, 1, :], naic[k][:], t1[:, 0:128], ALU.mult, ALU.add), reads=["h0", "naic", "t1"], writes=["t1"])
                        P.op("dve", TT(hs[:, 0, :], t1[:, 0:128], ps_re[:, sc], ALU.add), reads=["t1", kre], writes=["hs"])
                        P.op("dve", TS(t2[:, 0:128], h0[:, 1, :], arc[k][:], ALU.mult), reads=["h0", "arc"], writes=["t2"])
                        P.op("dve", STT(t2[:, 0:128], h0[:, 0, :], aic[k][:], t2[:, 0:128], ALU.mult, ALU.add), reads=["h0", "aic", "t2"], writes=["t2"])
                        P.op("dve", TT(hs[:, 1, :], t2[:, 0:128], ps_im[:, sc], ALU.add), reads=["t2", kim], writes=["hs"])
                        P.op("dve", CP(hre_b[k][:, sc], hs[:, 0, :]), reads=["hs"], writes=[f"hre{k}"])
                        P.op("dve", CP(him_b[k][:, sc], hs[:, 1, :]), reads=["hs"], writes=[f"him{k}"])
                        for ri in range(2):
                            pst, kst = self.bank()
                            P.op("pe", TR(pst[:, 0:128], hs[:, ri, :], self.ident_f[:]), reads=["hs", "ident_f"], writes=[kst])
                            hst = T("hst", [128, 128], F32)
                            P.op("act", ACT(hst[:], pst[:, 0:128], AF.Copy), reads=[kst], writes=["hst"])
                            P.dma("sp", O["ssms_out"][l, ri, 2 * hg + k], hst[:], reads=["hst"])
                if 's5' in DBG_SKIP:
                    if not last and 'carry' not in DBG_SKIP:
                        P.op("dve", CP(kT[:, 0:128], kT[:, n:n + 128]), reads=["kT"], writes=["kT"])
                        P.op("dve", CP(Vtm[:, 0, :], Vtm[:, nblk, :]), reads=["Vtm"], writes=["Vtm"])
                    continue
                ps_y, ky = self.bank()
                fy = []
                for k in range(2):
                    fy.append(MM(ps_y[0:64, 0:n], czre[k][:], hre_b[k][:, 0:n], start=(k == 0), stop=False))
                    fy.append(MM(ps_y[0:64, 0:n], nczim[k][:], him_b[k][:, 0:n], start=False, stop=False))
                fy.append(MM(ps_y[0:64, 0:n], dz_b[64:128, :], VUb[64:128, 0:n], start=False, stop=True))
                P.op("pe", fy, reads=["cz0", "cz1", "hre0", "hre1", "him0", "him1", "dz_b", "VUb"], writes=[ky])
                P.op("act", ACT(z_st[:, 0:n], ps_y[0:64, 0:n], AF.Gelu), reads=[ky], writes=["z_st"])
                P.dma("sp", self.mbuf[hg, 128:192, c0:c0 + n], z_st[:, 0:n], reads=["z_st"])
                ps_gq, kgq = proj(384, 448, 64)
                P.op("act", ACT(qg_f[:, 0:n], ps_gq[0:64, 0:n], AF.Copy, scale=0.125), reads=[kgq], writes=["qg_f"])
                ps_gk, kgk = proj(448, 512, 64)
                P.op("act", ACT(kg_f[:, 0:n], ps_gk[0:64, 0:n], AF.Copy), reads=[kgk], writes=["kg_f"])
                ps_vg, kvg = proj(512, 640, 128)
                P.op("act", ACT(VGb[:, 0:n], ps_vg[:, 0:n], AF.Copy), reads=[kvg], writes=["VGb"])
                if last:
                    P.op("act", ACT(vgf[:, :], ps_vg[0:64, 128:256], AF.Copy), reads=[kvg], writes=["vgf"])
                ps_ac, kac = proj(640, 656, 16)
                P.op("act", ACT(acb[:, 0:n], ps_ac[0:16, 0:n], AF.Copy), reads=[kac], writes=["acb"])
                ps_g, kg_ = self.bank()
                P.op("pe", MM(ps_g[0:64, 0:n], wa2_b[:, :], acb[:, 0:n]), reads=["wa2_b", "acb"], writes=[kg_])
                P.op("act", ACT(lg[:, 0:n], ps_g[0:64, 0:n], AF.Exp, scale=-1.0, bias=nba[:, 0:1]), reads=[kg_, "nba"], writes=["lg"])
                P.op("act", ACT(lg[:, 0:n], lg[:, 0:n], AF.Ln, bias=1.0), reads=["lg"], writes=["lg"])
                P.op("dve", SCAN(cs[:, 0:n], self.resetm[0:64, 0:n], lg[:, 0:n], 0.0), reads=["resetm", "lg"], writes=["cs"])
                P.op("act", ACT(Ep[:, 0:n], cs[:, 0:n], AF.Exp, scale=-1.0 / 16.0), reads=["cs"], writes=["Ep"])
                P.op("act", ACT(Em[:, 0:n], cs[:, 0:n], AF.Exp, scale=1.0 / 16.0), reads=["cs"], writes=["Em"])
                P.op("dve", TT(qin_b[:, 0:n], qg_f[:, 0:n], Ep[:, 0:n], ALU.mult), reads=["qg_f", "Ep"], writes=["qin_b"])
                P.op("dve", TT(kout_f[:, 0:n], kg_f[:, 0:n], Em[:, 0:n], ALU.mult), reads=["kg_f", "Em"], writes=["kout_f"])
                P.op("act", ACT(kout_b[:, 0:n], kout_f[:, 0:n], AF.Copy), reads=["kout_f"], writes=["kout_b"])
                for i in range(nblk):
                    gb = 4 * t + i
                    if gb == NB - 1:
                        continue
                    ch = slice(i * 128, (i + 1) * 128)
                    eend = Ep[:, i * 128 + 127:i * 128 + 128]
                    j = i % 2
                    P.op("dve", TS(kend_b[:, ch], kout_f[:, ch], eend, ALU.mult), reads=["kout_f", "Ep"], writes=["kend_b"])
                    ps_a, ka_ = self.bank()
                    P.op("pe", MM(ps_a[:, 0:128], kout_b[:, ch], qin_b[:, ch]), reads=["kout_b", "qin_b"], writes=[ka_])
                    P.op("dve", TT(attT_b[j][:], ps_a[:, 0:128], self.mask_f[:, 0, 128:256], ALU.mult), reads=[ka_, "mask_f"], writes=[f"attT{j}"])
                    pb1, kb1 = self.bankb()
                    P.op("pe", TR(pb1[:, 0:64], VGb[0:64, ch], self.ident_b[0:64, 0:64]), reads=["VGb", "ident_b"], writes=[kb1])
                    P.op("act", ACT(vtm_b[j][:], pb1[:, 0:64], AF.Copy), reads=[kb1], writes=[f"vtm{j}"])
                    pb2, kb2 = self.bankb()
                    P.op("pe", TR(pb2[:, 0:64], kend_b[:, ch], self.ident_b[0:64, 0:64]), reads=["kend_b", "ident_b"], writes=[kb2])
                    P.op("dve", CP(kend_tm[j][:], pb2[:, 0:64]), reads=[kb2], writes=[f"kendtm{j}"])
                    ps_o, ko_ = self.bank()
                    P.op("pe", [MM(ps_o[0:64, 0:128], vtm_b[j][:], attT_b[j][:], start=True, stop=False),
                                MM(ps_o[0:64, 0:128], S_b[:], qin_b[:, ch], start=False, stop=True)],
                         reads=[f"vtm{j}", f"attT{j}", "S_b", "qin_b"], writes=[ko_])
                    P.op("act", ACT(o_st[:, ch], ps_o[0:64, 0:128], AF.Copy), reads=[ko_], writes=["o_st"])
                    ps_kv, kkv = self.bank()
                    P.op("pe", MM(ps_kv[0:64, 0:64], kend_tm[j][:], vtm_b[j][:]), reads=[f"kendtm{j}", f"vtm{j}"], writes=[kkv])
                    P.op("dve", STT(S_f[:], S_f[:], eend, ps_kv[0:64, 0:64], ALU.mult, ALU.add), reads=["S_f", "Ep", kkv], writes=["S_f"])
                    P.op("act", ACT(S_b[:], S_f[:], AF.Copy), reads=["S_f"], writes=["S_b"])
                P.dma("sp", self.mbuf[hg, 192:256, c0:c0 + npr], o_st[:, 0:npr], reads=["o_st"])
                P.dma("sp", self.mbuf[hg, 256:320, c0:c0 + n], VGb[64:128, 0:n], reads=["VGb"])
                if last:
                    P.dma("sp", O["glap_out"][l, hg], S_f[:], reads=["S_f"])
                    self.p1_samples(es, l, hg, qT, kT, kf, vf, Vtm, sinkn, qg_f, kg_f, lg, vgf)
                else:
                    if 'carry' in DBG_SKIP:
                        continue
                    P.op("dve", CP(kT[:, 0:128], kT[:, n:n + 128]), reads=["kT"], writes=["kT"])
                    P.op("dve", CP(Vtm[:, 0, :], Vtm[:, nblk, :]), reads=["Vtm"], writes=["Vtm"])
            P.barrier()

    def p1_samples(self, es, l, hg, qT, kT, kf, vf, Vtm, sinkn, qg_f, kg_f, lg, vgf):
        P, I, O = self.P, self.I, self.O
        kvh = hg // 2
        T = lambda n, s, d: self.T(es, n, s, d)
        sc = slice(128, 256)
        GS = G - 128
        prod = T("s_prod", [128, 128], F32)
        pself = T("s_pself", [128, 2], F32)
        Dg = T("s_Dg", [128, 2, 128], BF16)
        Ps = T("s_Ps", [128, 256], BF16)
        Kc = [T("s_Kc", [128, 8, 128], BF16) for _ in range(2)]
        Vc = [T("s_Vc", [128, 8, 65], BF16) for _ in range(2)]
        sC = T("s_sC", [65, 256], F32)
        tot = T("s_tot", [65, 256], F32)
        rd = T("s_rd", [64, 256], F32)
        oas = T("s_oas", [64, 256], BF16)
        P.op("dve", TT(prod[:], qT[:, sc], kT[:, 256:384], ALU.mult), reads=["qT", "kT"], writes=["s_prod"])
        ps_ss, kss = self.bank()
        P.op("pe", MM(ps_ss[:, 0:2], prod[:], self.blk2[:]), reads=["s_prod", "blk2"], writes=[kss])
        for hh in range(2):
            P.op("act", ACT(pself[:, hh:hh + 1], ps_ss[:, hh:hh + 1], AF.Exp, scale=0.125, bias=sinkn[:, hh:hh + 1]), reads=[kss, "sinkn"], writes=["s_pself"])
            P.op("dve", TS(Dg[:, hh, :], self.ident_f[:], pself[:, hh:hh + 1], ALU.mult), reads=["ident_f", "s_pself"], writes=["s_Dg"])
        for j in range(2):
            P.op("dve", MEMSET(Vc[j][:, :, 64:65], 1.0), writes=[f"s_Vc{j}"])
        ps_A, kA = self.bank()
        for g8 in range(16):
            j = g8 % 2
            b0 = g8 * 8
            for half in range(2):
                P.dma("sp", Kc[j][64 * half:64 * half + 64], self.Wb["ckT"][l, kvh][:, b0:b0 + 8, :], writes=[f"s_Kc{j}"])
            fns = []
            for bi in range(8):
                b = b0 + bi
                for hh in range(2):
                    pb = 64 * hh
                    fns.append(MM(ps_A[:, hh * 128 + b:hh * 128 + b + 1], Kc[j][pb:pb + 64, bi, :], qT[pb:pb + 64, 128 + b:129 + b]))
            P.op("pe", fns, reads=[f"s_Kc{j}", "qT"], writes=[kA])
        for hh in range(2):
            P.op("act", ACT(Ps[:, hh * 128:(hh + 1) * 128], ps_A[:, hh * 128:(hh + 1) * 128], AF.Exp, scale=0.125, bias=sinkn[:, hh:hh + 1]), reads=[kA, "sinkn"], writes=["s_Ps"])
        ps_B, kB = self.bank()
        for g8 in range(16):
            j = g8 % 2
            b0 = g8 * 8
            P.dma("sp", Vc[j][:, :, 0:64], self.Wb["cvS"][l, kvh][:, b0:b0 + 8, :], writes=[f"s_Vc{j}"])
            fns = [MM(ps_B[0:65, b0 + bi:256:128], Vc[j][:, bi, :], Ps[:, b0 + bi:256:128]) for bi in range(8)]
            P.op("pe", fns, reads=[f"s_Vc{j}", "s_Ps"], writes=[kB])
        ps_C, kC = self.bank()
        P.op("pe", [MM(ps_C[0:65, hh * 128:(hh + 1) * 128], Vtm[:, 2, :], Dg[:, hh, :]) for hh in range(2)], reads=["Vtm", "s_Dg"], writes=[kC])
        P.op("act", ACT(sC[:], ps_C[0:65, 0:256], AF.Copy), reads=[kC], writes=["s_sC"])
        P.op("dve", TT(tot[:], ps_B[0:65, 0:256], sC[:], ALU.add), reads=[kB, "s_sC"], writes=["s_tot"])
        ps_D, kD = self.bank()
        P.op("pe", MM(ps_D[0:64, 0:256], self.sel65[0:65, :], tot[:]), reads=["sel65", "s_tot"], writes=[kD])
        P.op("dve", TS(rd[:], ps_D[0:64, 0:256], 1.0, ALU.add), reads=[kD], writes=["s_rd"])
        P.op("dve", RECIP(rd[:], rd[:]), reads=["s_rd"], writes=["s_rd"])
        P.op("dve", TT(oas[:], tot[0:64, :], rd[:], ALU.mult), reads=["s_tot", "s_rd"], writes=["s_oas"])
        for hh in range(2):
            P.dma("sp", self.mbuf[hg, 64 * hh:64 * hh + 64, GS:G], oas[:, hh * 128:(hh + 1) * 128], reads=["s_oas"])
        if hg % 2 == 0:
            ktm = T("s_ktm", [128, 4, 64], F32)
            for vi, (src, skey, nm) in enumerate(((kf, "kf", "k"), (vf, "vf", "v"))):
                for i in range(2):
                    pst, kst = self.bank()
                    P.op("pe", TR(pst[:, 0:64], src[0:64, i * 128:(i + 1) * 128], self.ident_f[0:64, 0:64]), reads=[skey, "ident_f"], writes=[kst])
                    P.op("act", ACT(ktm[:, 2 * vi + i, :], pst[:, 0:64], AF.Copy), reads=[kst], writes=["s_ktm"])
                P.dma("sp", O[f"{nm}p_out"][l, kvh], ktm[:, 2 * vi, :], reads=["s_ktm"])
                P.dma("sp", O[f"{nm}s_out"][l, kvh, :, 127, :], ktm[:, 2 * vi + 1, :], reads=["s_ktm"])
                P.dma("sp", O[f"{nm}s_out"][l, kvh, :, 0:127, :], I["ckN" if nm == "k" else "cvN"][l, kvh, :, 1:128, :])
        Eps = T("g_Eps", [64, 128], F32)
        qins = T("g_qins", [64, 128], BF16)
        prodg = T("g_prodg", [64, 128], F32)
        vts = T("g_vts", [128, 64], F32)
        S0f = [T("g_S0f", [64, 16, 64], F32) for _ in range(2)]
        S0b = [T("g_S0b", [64, 16, 64], BF16) for _ in range(2)]
        Vrep = [T("g_Vrep", [64, 16, 64], F32) for _ in range(2)]
        Snew = [T("g_Snew", [64, 16, 64], F32) for _ in range(2)]
        tkv = T("g_tkv", [64, 64], F32)
        osb = T("g_osb", [64, 128], BF16)
        otmp = T("g_otmp", [64, 128], F32)
        P.op("act", ACT(Eps[:], lg[:, sc], AF.Exp, scale=-1.0 / 16.0), reads=["lg"], writes=["g_Eps"])
        P.op("dve", TT(qins[:], qg_f[:, sc], Eps[:], ALU.mult), reads=["qg_f", "g_Eps"], writes=["g_qins"])
        P.op("dve", TT(prodg[:], qg_f[:, sc], kg_f[:, sc], ALU.mult), reads=["qg_f", "kg_f"], writes=["g_prodg"])
        ps_qk, kqk = self.bank()
        P.op("pe", MM(ps_qk[0:64, 0:128], self.ones_f[0:64, 0:64], prodg[:]), reads=["ones_f", "g_prodg"], writes=[kqk])
        pst, kst = self.bank()
        P.op("pe", TR(pst[:, 0:64], vgf[:, :], self.ident_f[0:64, 0:64]), reads=["vgf", "ident_f"], writes=[kst])
        P.op("act", ACT(vts[:], pst[:, 0:64], AF.Copy), reads=[kst], writes=["g_vts"])
        P.dma("sp", self.vs_scr[:, :], vts[:], reads=["g_vts"], writes=["vs_scr"])
        ps_os, kos = self.bank()
        NQ = 16
        for qd in range(128 // NQ):
            j = qd % 2
            bq = slice(qd * NQ, qd * NQ + NQ)
            P.dma("sp", S0f[j][:], I["gst"][l, hg][:, bq, :], writes=[f"g_S0f{j}"])
            P.dma("sp", Vrep[j][:].rearrange("p b e -> p (b e)"),
                  self.vs_scr.rearrange("(o b) e -> o (b e)", o=128 // NQ)[qd:qd + 1, :].partition_broadcast(64),
                  reads=["vs_scr"], writes=[f"g_Vrep{j}"])
            P.op("act", ACT(S0b[j][:], S0f[j][:], AF.Copy), reads=[f"g_S0f{j}"], writes=[f"g_S0b{j}"])
            P.op("pe", [MM(ps_os[0:64, qd * NQ + b:qd * NQ + b + 1], S0b[j][:, b, :], qins[:, qd * NQ + b:qd * NQ + b + 1]) for b in range(NQ)],
                 reads=[f"g_S0b{j}", "g_qins"], writes=[kos])
            for b in range(NQ):
                col = 128 + qd * NQ + b
                P.op("dve", TS(tkv[:], Vrep[j][:, b, :], kg_f[:, col:col + 1], ALU.mult), reads=[f"g_Vrep{j}", "kg_f"], writes=["g_tkv"])
                P.op("dve", STT(Snew[j][:, b, :], S0f[j][:, b, :], Eps[:, qd * NQ + b:qd * NQ + b + 1], tkv[:], ALU.mult, ALU.add),
                     reads=[f"g_S0f{j}", "g_Eps", "g_tkv"], writes=[f"g_Snew{j}"])
            P.dma("sp", O["glas_out"][l, hg][:, bq, :], Snew[j][:], reads=[f"g_Snew{j}"])
        P.op("dve", TT(otmp[:], vgf[:, :], ps_qk[0:64, 0:128], ALU.mult), reads=["vgf", kqk], writes=["g_otmp"])
        P.op("dve", TT(osb[:], otmp[:], ps_os[0:64, 0:128], ALU.add), reads=["g_otmp", kos], writes=["g_osb"])
        P.dma("sp", self.mbuf[hg, 192:256, GS:G], osb[:], reads=["g_osb"])

    def p2(self, l, s, last):
        P, I, O = self.P, self.I, self.O
        c0 = s * W2
        n = min(W2, G - c0)
        blks = [(b, min(n, b + 512)) for b in range(0, n, 512)]
        last_slice = (c0 + n == G)
        if s == 0:
            self.ucarry = self.T(self.es, "ucarry", [128, NFC, 2], F32)
            P.op("dve", MEMSET(self.ucarry[:], 0.0), writes=["ucarry"])
            P.dma("sp", O["convs_out"][l, 0], I["cprevT"][l, 1])
        ucarry = self.ucarry
        with ExitStack() as es:
            T = lambda nm, sh, d: self.T(es, nm, sh, d)
            xr = T("xr", [128, 16, W2], F32)
            P.dma("sp", xr[:, :, 0:n], chunked(self.xres_d)[:, :, c0:c0 + n], reads=[("xres", s)], writes=["xr"])
            gb1 = T("gb1", [128, 2, 16], F32)
            gb2 = T("gb2", [128, 2, 16], F32)
            cw = T("cw", [128, 4, NFC], F32)
            bgl = T("bgl", [128, 4], F32)
            gno = T("gno", [128, 4], F32)
            P.dma("sp", gb1[:], I["ln1"][l].rearrange("t p k -> p t k"), writes=["gb1"])
            P.dma("sp", gb2[:], I["ln2"][l].rearrange("t p k -> p t k"), writes=["gb2"])
            P.dma("sp", cw[:], I["convw"][l].rearrange("t p k -> p t k"), writes=["cw"])
            P.dma("sp", bgl[:], I["bglu"][l], writes=["bgl"])
            P.dma("sp", gno[:], I["gnorm"][l], writes=["gno"])
            with ExitStack() as esa:
                Ta = lambda nm, sh, d: self.T(esa, nm, sh, d)
                oaT = Ta("oaT", [128, 8, W2], BF16)
                zT = Ta("zT", [128, 4, W2], BF16)
                oT = Ta("oT", [128, 4, W2], BF16)
                gT = Ta("gT", [128, 4, W2], BF16)
                obT = Ta("obT", [128, 4, W2], BF16)
                wgl = Ta("wgl", [128, 4, 512], BF16)
                tA = Ta("tA", [128, 512], F32)
                tB = Ta("tB", [128, 512], F32)
                tCb = Ta("tCb", [128, 512], BF16)
                wo = [Ta("wo", [128, 16, 128], BF16) for _ in range(2)]
                for r in range(NHG):
                    hp = slice(64 * (r % 2), 64 * (r % 2) + 64)
                    P.dma("sp", oaT[:, r, 0:n], self.mbuf[r, 0:128, c0:c0 + n], writes=["oaT"])
                    P.dma("sp", zT[hp, r // 2, 0:n], self.mbuf[r, 128:192, c0:c0 + n], writes=["zT"])
                    P.dma("sp", oT[hp, r // 2, 0:n], self.mbuf[r, 192:256, c0:c0 + n], writes=["oT"])
                    P.dma("sp", gT[hp, r // 2, 0:n], self.mbuf[r, 256:320, c0:c0 + n], writes=["gT"])
                P.dma("sp", wgl[:], chunked(self.Wb["wglu"][l]), writes=["wgl"])
                for oc in range(4):
                    for (b0, b1) in blks:
                        nb = b1 - b0
                        ps, kp = self.bank()
                        P.op("pe", [MM(ps[:, 0:nb], wgl[:, k, oc * 128:(oc + 1) * 128], zT[:, k, b0:b1], start=(k == 0), stop=(k == 3)) for k in range(4)],
                             reads=["wgl", "zT"], writes=[kp])
                        P.op("act", ACT(tA[:, 0:nb], ps[:, 0:nb], AF.Sigmoid, bias=bgl[:, oc:oc + 1]), reads=[kp, "bgl"], writes=["tA"])
                        P.op("dve", TT(obT[:, oc, b0:b1], zT[:, oc, b0:b1], tA[:, 0:nb], ALU.mult), reads=["zT", "tA"], writes=["obT"])
                for h in range(4):
                    for (b0, b1) in blks:
                        nb = b1 - b0
                        P.op("act", ACT(tCb[:, 0:nb], oT[:, h, b0:b1], AF.Square), reads=["oT"], writes=["tCb"])
                        ps, kp = self.bank()
                        P.op("pe", MM(ps[:, 0:nb], self.ones_b[:], tCb[:, 0:nb]), reads=["ones_b", "tCb"], writes=[kp])
                        P.op("act", ACT(tA[:, 0:nb], ps[:, 0:nb], AF.Sqrt, scale=1.0 / 128.0, bias=EPS), reads=[kp], writes=["tA"])
                        P.op("dve", RECIP(tA[:, 0:nb], tA[:, 0:nb]), reads=["tA"], writes=["tA"])
                        P.op("dve", TT(tA[:, 0:nb], oT[:, h, b0:b1], tA[:, 0:nb], ALU.mult), reads=["oT", "tA"], writes=["tA"])
                        P.op("act", ACT(tB[:, 0:nb], gT[:, h, b0:b1], AF.Silu), reads=["gT"], writes=["tB"])
                        P.op("dve", STT(oT[:, h, b0:b1], tA[:, 0:nb], gno[:, h:h + 1], tB[:, 0:nb], ALU.mult, ALU.mult), reads=["tA", "gno", "tB"], writes=["oT"])
                for oc in range(16):
                    w = wo[oc % 2]
                    wk = f"wo{oc % 2}"
                    P.dma("sp", w[:], chunked(self.Wb["wout"][l][:, oc * 128:(oc + 1) * 128]), writes=[wk])
                    for (b0, b1) in blks:
                        nb = b1 - b0
                        ps, kp = self.bank()
                        fns = []
                        for k in range(16):
                            rhs = oaT[:, k, b0:b1] if k < 8 else (obT[:, k - 8, b0:b1] if k < 12 else oT[:, k - 12, b0:b1])
                            fns.append(MM(ps[:, 0:nb], w[:, k, :], rhs, start=(k == 0), stop=(k == 15)))
                        P.op("pe", fns, reads=[wk, "oaT", "obT", "oT"], writes=[kp])
                        P.op("dve", STT(xr[:, oc, b0:b1], xr[:, oc, b0:b1], ALPHA, ps[:, 0:nb], ALU.mult, ALU.add), reads=["xr", kp], writes=["xr"])
                P.barrier()
            if 'p2a' in DBG_SKIP:
                return
            x1b = T("x1b", [128, 16, W2], BF16)
            with ExitStack() as esl:
                self.layernorm(esl, xr, "xr", n, gb1, "gb1", x1b, "x1b", mask_pad=(NPAD if s == 0 else 0))
                P.barrier()
            for k in range(16):
                P.op("act", ACT(xr[:, k, 0:n], xr[:, k, 0:n], AF.Copy, scale=ALPHA), reads=["xr"], writes=["xr"])
            if 'p2ln' in DBG_SKIP:
                return
            with ExitStack() as esf:
                Tf = lambda nm, sh, d: self.T(esf, nm, sh, d)
                hT = Tf("hT", [128, 11, W2], BF16)
                U = [Tf("U", [128, W2 + 2], F32) for _ in range(2)]
                cc = [Tf("cc", [128, W2], F32) for _ in range(2)]
                wu = [Tf("wu", [128, 16, 128], BF16) for _ in range(2)]
                wg = [Tf("wg", [128, 16, 128], BF16) for _ in range(2)]
                wd = [Tf("wd", [128, 11, 128], BF16) for _ in range(2)]
                cpv = [Tf("cpv", [128, 2, 128], F32) for _ in range(2)]
                for grp in range(4):
                    for fi in range(11):
                        f = grp * 11 + fi
                        j = f % 2
                        P.dma("sp", wu[j][:], chunked(self.Wb["wup"][l][:, f * 128:(f + 1) * 128]), writes=[f"wu{j}"])
                        P.dma("sp", wg[j][:], chunked(self.Wb["wup"][l][:, DFF + f * 128:DFF + (f + 1) * 128]), writes=[f"wg{j}"])
                        Uj, uk = U[j], f"U{j}"
                        cj, ck_ = cc[j], f"cc{j}"
                        P.op("dve", CP(Uj[:, 0:2], ucarry[:, f, :]), reads=["ucarry"], writes=[uk])
                        gps = []
                        for (b0, b1) in blks:
                            nb = b1 - b0
                            ps, kp = self.bank()
                            P.op("pe", [MM(ps[:, 0:nb], wu[j][:, k, :], x1b[:, k, b0:b1], start=(k == 0), stop=(k == 15)) for k in range(16)],
                                 reads=[f"wu{j}", "x1b"], writes=[kp])
                            P.op("act", ACT(Uj[:, 2 + b0:2 + b1], ps[:, 0:nb], AF.Copy), reads=[kp], writes=[uk])
                            ps2, kp2 = self.bank()
                            P.op("pe", [MM(ps2[:, 0:nb], wg[j][:, k, :], x1b[:, k, b0:b1], start=(k == 0), stop=(k == 15)) for k in range(16)],
                                 reads=[f"wg{j}", "x1b"], writes=[kp2])
                            gps.append((ps2, kp2))
                        npz = n - 128 if last_slice else n
                        P.op("dve", TS(cj[:, 0:n], Uj[:, 0:n], cw[:, 0, f:f + 1], ALU.mult, cw[:, 3, f:f + 1], ALU.add), reads=[uk, "cw"], writes=[ck_])
                        P.op("dve", STT(cj[:, 0:n], Uj[:, 1:n + 1], cw[:, 1, f:f + 1], cj[:, 0:n], ALU.mult, ALU.add), reads=[uk, "cw", ck_], writes=[ck_])
                        P.op("dve", STT(cj[:, 0:n], Uj[:, 2:n + 2], cw[:, 2, f:f + 1], cj[:, 0:n], ALU.mult, ALU.add), reads=[uk, "cw", ck_], writes=[ck_])
                        if last_slice:
                            sc = slice(npz, n)
                            P.dma("sp", cpv[j][:], I["cprevT"][l][:, f * 128:(f + 1) * 128, :].rearrange("t p b -> p t b"), writes=[f"cpv{j}"])
                            P.op("dve", TS(cj[:, sc], cpv[j][:, 0, :], cw[:, 0, f:f + 1], ALU.mult, cw[:, 3, f:f + 1], ALU.add), reads=[f"cpv{j}", "cw", ck_], writes=[ck_])
                            P.op("dve", STT(cj[:, sc], cpv[j][:, 1, :], cw[:, 1, f:f + 1], cj[:, sc], ALU.mult, ALU.add), reads=[f"cpv{j}", "cw", ck_], writes=[ck_])
                            P.op("dve", STT(cj[:, sc], Uj[:, 2 + npz:2 + n], cw[:, 2, f:f + 1], cj[:, sc], ALU.mult, ALU.add), reads=[uk, "cw", ck_], writes=[ck_])
                            P.dma("sp", O["convp_out"][l, :, f, :], Uj[:, npz:npz + 2], reads=[uk])
                            P.dma("sp", O["convs_out"][l, 1, f * 128:(f + 1) * 128, :], Uj[:, 2 + npz:2 + n], reads=[uk])
                        else:
                            P.op("dve", CP(ucarry[:, f, :], Uj[:, n:n + 2]), reads=[uk], writes=["ucarry"])
                        P.op("act", ACT(cj[:, 0:n], cj[:, 0:n], AF.Gelu), reads=[ck_], writes=[ck_])
                        for bi, (b0, b1) in enumerate(blks):
                            ps2, kp2 = gps[bi]
                            P.op("dve", TT(hT[:, fi, b0:b1], cj[:, b0:b1], ps2[:, 0:b1 - b0], ALU.mult), reads=[ck_, kp2], writes=["hT"])
                    for oc in range(16):
                        j = oc % 2
                        P.dma("sp", wd[j][:], chunked(self.Wb["wdown"][l][grp * 1408:(grp + 1) * 1408, oc * 128:(oc + 1) * 128]), writes=[f"wd{j}"])
                        for (b0, b1) in blks:
                            nb = b1 - b0
                            ps, kp = self.bank()
                            P.op("pe", [MM(ps[:, 0:nb], wd[j][:, fi, :], hT[:, fi, b0:b1], start=(fi == 0), stop=(fi == 10)) for fi in range(11)],
                                 reads=[f"wd{j}", "hT"], writes=[kp])
                            P.op("dve", TT(xr[:, oc, b0:b1], xr[:, oc, b0:b1], ps[:, 0:nb], ALU.add), reads=["xr", kp], writes=["xr"])
                P.barrier()
            if 'p2ffn' in DBG_SKIP:
                return
            with ExitStack() as eso:
                To = lambda nm, sh, d: self.T(eso, nm, sh, d)
                if not last:
                    self.layernorm(eso, xr, "xr", n, gb2, "gb2", x1b, "x1b", mask_pad=(NPAD if s == 0 else 0))
                    P.dma("sp", chunked(self.xres_d)[:, :, c0:c0 + n], xr[:, :, 0:n], reads=["xr"], writes=[("xres", s)])
                    P.dma("sp", chunked(self.xn_d)[:, :, c0:c0 + n], x1b[:, :, 0:n], reads=["x1b"], writes=["xn_d"])
                else:
                    self.layernorm(eso, xr, "xr", n, gb2, "gb2")
                    yt = [To("yt", [128, D], F32) for _ in range(2)]
                    for jb in range(0 if 'noy' in DBG_SKIP else n // 128):
                        y_, yk = yt[jb % 2], f"yt{jb % 2}"
                        for kg in range(4):
                            ps, kp = self.bank()
                            for kk in range(4):
                                P.op("pe", TR(ps[:, kk * 128:(kk + 1) * 128], xr[:, 4 * kg + kk, jb * 128:(jb + 1) * 128], self.ident_f[:]),
                                     reads=["xr", "ident_f"], writes=[kp])
                            P.op("act", ACT(y_[:, kg * 512:(kg + 1) * 512], ps[:, :], AF.Copy), reads=[kp], writes=[yk])
                        P.dma("sp", O["y_out"][c0 + jb * 128:c0 + (jb + 1) * 128, :], y_[:], reads=[yk])
                P.barrier()


_CACHE = {}


def _assemble(res, inp):
    r = res
    B = 128
    y = r["y_out"]
    y_prompt = np.ascontiguousarray(y[128:G - 128][None])
    y_sample = np.ascontiguousarray(y[G - 128:][:, None, :])
    kp = r["kp_out"].transpose(0, 2, 1, 3)[:, None]
    vp = r["vp_out"].transpose(0, 2, 1, 3)[:, None]
    ks = r["ks_out"].transpose(0, 2, 3, 1, 4)
    vs = r["vs_out"].transpose(0, 2, 3, 1, 4)
    sp = r["ssmp_out"].reshape(L, 2, 16, 2, 64).reshape(L, 2, 32, 64)
    ssm_re_p, ssm_im_p = sp[:, 0][:, None], sp[:, 1][:, None]
    ss = r["ssms_out"]
    ss = ss.transpose(0, 1, 3, 2, 4).reshape(L, 2, B, 32, 64)
    ssm_re_s, ssm_im_s = ss[:, 0], ss[:, 1]
    gp = r["glap_out"].reshape(L, 4, 2, 64, 64).transpose(0, 1, 3, 2, 4).reshape(L, 4, 64, 128)[:, None]
    gs = r["glas_out"].reshape(L, 4, 2, 64, B, 64).transpose(0, 4, 1, 3, 2, 5).reshape(L, B, 4, 64, 128)
    cpo = r["convp_out"]
    conv_p = cpo.transpose(0, 3, 2, 1).reshape(L, 2, DFF)[:, None]
    conv_s = r["convs_out"].transpose(0, 3, 1, 2)
    outs = (y_prompt, y_sample, kp, vp, ssm_re_p, ssm_im_p, gp, conv_p, ks, vs, ssm_re_s, ssm_im_s, gs, conv_s)
    return tuple(np.ascontiguousarray(o, dtype=np.float32) for o in outs)


def kernel(**inp):
    d = _prep_inputs(inp)
    shapes = {k: v.shape for k, v in d.items()}
    b = Builder(shapes)
    nc = b.build()
    in_maps = [d for _ in range(NCORES)]
    res = run_bass_kernel_spmd(nc, in_maps, core_ids=list(range(NCORES)))
    return _assemble(res.results[0], inp)
```

```python
import math
from contextlib import ExitStack
import numpy as np
import ml_dtypes
import concourse.bass as bass
import concourse.mybir as mybir
from concourse.bass_utils import run_bass_kernel_spmd

F32 = mybir.dt.float32
BF16 = mybir.dt.bfloat16
I32 = mybir.dt.int32
AF = mybir.ActivationFunctionType
ALU = mybir.AluOpType

D = 2048
L = 2
NB = 66
G = NB * 128
NPAD = 112
NCORES = 8
DFF = 5632
NFC = DFF // 128
NHG = 8
NWC = 656
W1 = 512
W2 = 1024
CPC = G // NCORES
ALPHA = float((2 * L) ** 0.25)
EPS = 1e-5
TWO_PI = 2.0 * math.pi
C1 = 6.28125
C2 = TWO_PI - C1
BROWS = 320
import os
DBG_SKIP = set(os.environ.get('DBG_SKIP', '').split(','))


class Prog:
    EPOCH = 4000
    NSLOT = 8
    DMA_EPOCH = 1800

    def __init__(self, nc, es, same_engine_sync=True):
        self.nc = nc
        self.es = es
        self.eng = {"pe": nc.tensor, "act": nc.scalar, "dve": nc.vector,
                    "pool": nc.gpsimd, "sp": nc.sync}
        self.cnt = {e: 0 for e in self.eng}
        self.csem = {}
        self.dsem = {}
        self.dcount = {}
        self.dgen = {}
        self.dslot_next = {e: 0 for e in self.eng}
        self.waited = {e: {} for e in self.eng}
        self.lastw = {}
        self.readers = {}
        self.same_engine_sync = same_engine_sync
        self.live_dma = {}

    def _newsem(self, name):
        return self.es.enter_context(self.nc.semaphore(name))

    def _csem(self, e, ep):
        k = (e, ep)
        if k not in self.csem:
            self.csem[k] = self._newsem(f"c_{e}_{ep}")
        return self.csem[k]

    def _emit_wait(self, x, tok):
        w = self.waited[x]
        if tok[0] == "x":
            if w.get(tok, 0):
                return
            w[tok] = 1
            self.eng[x].wait_ge(self.xsem[tok[1]], 1)
            return
        if tok[0] == "c":
            _, e, n = tok
            if e == x and (e == "pe" or not self.same_engine_sync):
                return
            key = ("c", e)
            if w.get(key, 0) >= n:
                return
            w[key] = n
            idx = n - 1
            self.eng[x].wait_ge(self._csem(e, idx // self.EPOCH), idx % self.EPOCH + 1)
        else:
            _, q, slot, gen, k = tok
            key = ("d", q, slot, gen)
            if w.get(key, 0) >= k:
                return
            w[key] = k
            self.eng[x].wait_ge(self.dsem[(q, slot, gen)], 16 * k)

    def _deps(self, reads, writes):
        deps = []
        for r in reads:
            if r in self.lastw:
                deps.append(self.lastw[r])
        for wv in writes:
            if wv in self.lastw:
                deps.append(self.lastw[wv])
            deps.extend(self.readers.get(wv, {}).values())
        return deps

    def _commit(self, tok, reads, writes):
        for r in reads:
            d = self.readers.setdefault(r, {})
            if tok[0] == "c":
                d[("c", tok[1])] = tok
            elif tok[0] == "x":
                d[tok] = tok
            else:
                d[("d", tok[1], tok[2], tok[3])] = tok
        for wv in writes:
            self.lastw[wv] = tok
            self.readers[wv] = {}

    @staticmethod
    def _excl(reads, writes):
        ps = [r for r in reads if isinstance(r, str) and r.startswith("ps")]
        if not ps:
            return reads, writes
        return [r for r in reads if r not in ps], list(writes) + ps

    def op(self, e, fn, reads=(), writes=()):
        reads, writes = self._excl(reads, writes)
        for t in self._deps(reads, writes):
            self._emit_wait(e, t)
        self.cnt[e] += 1
        n = self.cnt[e]
        idx = n - 1
        fns = fn if isinstance(fn, (list, tuple)) else [fn]
        ins = None
        for f in fns:
            ins = f(self.eng[e])
        ins.then_inc(self._csem(e, idx // self.EPOCH), 1)
        tok = ("c", e, n)
        self._commit(tok, reads, writes)
        return tok

    def dma(self, q, out, in_, reads=(), writes=()):
        slot = self.dslot_next[q]
        self.dslot_next[q] = (slot + 1) % self.NSLOT
        gen = self.dgen.get((q, slot), 0)
        k = self.dcount.get((q, slot, gen), 0)
        if k > 0:
            self._emit_wait(q, ("d", q, slot, gen, k))
        if k >= self.DMA_EPOCH:
            gen += 1
            self.dgen[(q, slot)] = gen
            k = 0
        if (q, slot, gen) not in self.dsem:
            self.dsem[(q, slot, gen)] = self._newsem(f"d_{q}_{slot}_{gen}")
        for t in self._deps(reads, writes):
            self._emit_wait(q, t)
        k += 1
        self.dcount[(q, slot, gen)] = k
        self.eng[q].dma_start(out=out, in_=in_).then_inc(self.dsem[(q, slot, gen)], 16)
        tok = ("d", q, slot, gen, k)
        self._commit(tok, reads, writes)
        self.live_dma[(q, slot)] = tok
        return tok

    def collective(self, kind, in_ap, out_ap, reads=(), writes=()):
        e = "pool"
        for t in self._deps(reads, writes):
            self._emit_wait(e, t)
        self.ncoll = getattr(self, "ncoll", 0) + 1
        if not hasattr(self, "xsem"):
            self.xsem = {}
        sem = self._newsem(f"coll{self.ncoll}")
        self.xsem[self.ncoll] = sem
        self.nc.gpsimd.collective_compute(kind, ALU.bypass, replica_groups=[list(range(NCORES))],
                                          ins=[in_ap.opt()], outs=[out_ap.opt()]).then_inc(sem, 1)
        tok = ("x", self.ncoll)
        self._commit(tok, reads, writes)
        return tok

    def barrier(self):
        for x in self.eng:
            for t in self.live_dma.values():
                self._emit_wait(x, t)
            for ci in range(1, getattr(self, "ncoll", 0) + 1):
                self._emit_wait(x, ("x", ci))
            for y in self.eng:
                if y != x and self.cnt[y] > 0:
                    self._emit_wait(x, ("c", y, self.cnt[y]))
        for x in ("act", "dve", "pool"):
            if self.cnt[x] > 0 and self.same_engine_sync:
                self._emit_wait(x, ("c", x, self.cnt[x]))

    def finish(self, e="sp"):
        for t in self.live_dma.values():
            self._emit_wait(e, t)
        for x in self.eng:
            if self.cnt[x] > 0 and x != e:
                self._emit_wait(e, ("c", x, self.cnt[x]))


def MM(out, lhsT, rhs, start=True, stop=True):
    return lambda e: e.matmul(out, lhsT=lhsT, rhs=rhs, start=start, stop=stop)


def TR(out, in_, ident):
    return lambda e: e.transpose(out, in_, ident)


def ACT(out, in_, func, **kw):
    return lambda e: e.activation(out=out, in_=in_, func=func, **kw)


def TT(out, a, b, op):
    return lambda e: e.tensor_tensor(out=out, in0=a, in1=b, op=op)


def TS(out, a, s1, op0, s2=None, op1=None):
    if op1 is None:
        return lambda e: e.tensor_scalar(out=out, in0=a, scalar1=s1, scalar2=None, op0=op0)
    return lambda e: e.tensor_scalar(out=out, in0=a, scalar1=s1, scalar2=s2, op0=op0, op1=op1)


def STT(out, a, s, b, op0, op1):
    return lambda e: e.scalar_tensor_tensor(out=out, in0=a, scalar=s, in1=b, op0=op0, op1=op1)


def CP(out, in_):
    return lambda e: e.tensor_copy(out, in_)


def SCAN(out, d0, d1, init):
    return lambda e: e.tensor_tensor_scan(out=out, data0=d0, data1=d1, initial=init,
                                          op0=ALU.mult, op1=ALU.add)


def RECIP(out, in_):
    return lambda e: e.reciprocal(out, in_)


def MEMSET(ap, v):
    return lambda e: e.memset(ap, v)


def _rope_tables():
    pos = np.zeros(G, np.float32)
    pos[NPAD:G - 128] = np.arange(G - 128 - NPAD, dtype=np.float32)
    pos[G - 128:] = 8192.0
    half = 8
    inv = np.power(np.float32(500000.0), -np.arange(half, dtype=np.float32) / np.float32(half)).astype(np.float32)
    ang = pos[None, :] * inv[:, None]
    cos8, sin8 = np.cos(ang).astype(np.float32), np.sin(ang).astype(np.float32)
    tab = np.zeros((2, 128, G), np.float32)
    tab[0] = 1.0
    for base in (0, 64):
        tab[0, base:base + 8] = cos8
        tab[0, base + 8:base + 16] = cos8
        tab[1, base:base + 8] = sin8
        tab[1, base + 8:base + 16] = sin8
    pm = np.zeros((128, 128), np.float32)
    for base in (0, 64):
        for d in range(8):
            pm[base + d + 8, base + d] = -1.0
            pm[base + d, base + d + 8] = 1.0
    return tab, pm


def _swa_masks():
    s = np.arange(128)[:, None]
    q = np.arange(128)[None, :]
    m = np.zeros((3, 128, 2, 128), np.float32)
    prev = (s >= q).astype(np.float32)
    cur = (s <= q).astype(np.float32)
    real = (s >= NPAD).astype(np.float32)
    m[0, :, 0], m[0, :, 1] = prev, cur
    m[1, :, 0], m[1, :, 1] = 0.0, cur * real
    m[2, :, 0], m[2, :, 1] = prev * real, cur
    return np.ascontiguousarray(m.transpose(1, 0, 2, 3).reshape(128, 3, 256))


def _prep_inputs(inp):
    f = lambda a: np.ascontiguousarray(np.asarray(a, dtype=np.float32))
    d = {}
    xall = np.zeros((G, D), np.float32)
    xall[NPAD:128] = inp["meta_tokens"]
    xall[128:G - 128] = inp["x_prompt"][0]
    xall[G - 128:] = inp["x_sample"][:, 0]
    d["xT"] = f(xall.T)
    vec16 = lambda v: f(np.asarray(v).reshape(16, 128).T)
    d["ln_in"] = f(np.stack([vec16(inp["ln_in_g"]), vec16(inp["ln_in_b"])]))
    w_in = np.asarray(inp["w_in"])
    win_all = np.zeros((L, NHG, D, NWC), np.float32)
    for hg in range(NHG):
        kvh, gh, half = hg // 2, hg // 2, hg % 2
        cols = np.concatenate([
            np.arange(128 * hg, 128 * hg + 128),
            np.arange(1024 + 64 * kvh, 1024 + 64 * kvh + 64),
            np.arange(1024 + 64 * kvh, 1024 + 64 * kvh + 64),
            np.arange(1280 + 64 * kvh, 1280 + 64 * kvh + 64),
            np.arange(1536 + 64 * hg, 1536 + 64 * hg + 64),
            np.arange(2048 + 64 * gh, 2048 + 64 * gh + 64),
            np.arange(2304 + 64 * gh, 2304 + 64 * gh + 64),
            np.arange(2560 + 128 * gh + 64 * half, 2560 + 128 * gh + 64 * half + 64),
            np.arange(3072 + 128 * gh + 64 * half, 3072 + 128 * gh + 64 * half + 64),
            np.arange(3584, 3600),
        ])
        win_all[:, hg] = w_in[:, :, cols]
    d["win_all"] = win_all
    tab, pm = _rope_tables()
    d["rope"] = tab
    d["pm"] = pm
    d["swamask"] = _swa_masks()
    d["ident"] = np.eye(128, dtype=np.float32)
    sink = np.asarray(inp["attn_sink"])
    d["sink"] = f(np.broadcast_to(sink.reshape(L, NHG, 1, 2), (L, NHG, 128, 2)))
    lr, li, ldt = (np.asarray(inp[k]) for k in ("s5_lam_re", "s5_lam_im", "s5_log_dt"))
    bre, bim = np.asarray(inp["s5_b_re"]), np.asarray(inp["s5_b_im"])
    cre, cim = np.asarray(inp["s5_c_re"]), np.asarray(inp["s5_c_im"])
    dsk = np.asarray(inp["s5_d"])
    s5col = np.zeros((L, NHG, 2, 128, 3), np.float32)
    s5row = np.zeros((L, NHG, 2, 64, 3, 128), np.float32)
    s5bz = np.zeros((L, NHG, 2, 64, 2, 128), np.float32)
    s5cz = np.zeros((L, NHG, 2, 128, 2, 64), np.float32)
    s5dz = np.zeros((L, NHG, 64, 64), np.float32)
    for l in range(L):
        for hg in range(NHG):
            for k in range(2):
                for j in range(2):
                    g = 4 * hg + 2 * k + j
                    gl = 2 * k + j
                    sl = slice(64 * j, 64 * j + 64)
                    s5col[l, hg, k, sl, 0] = lr[l, g]
                    s5col[l, hg, k, sl, 1] = li[l, g]
                    s5col[l, hg, k, sl, 2] = ldt[l, g]
                    s5row[l, hg, k, :, 0, sl] = lr[l, g][None, :]
                    s5row[l, hg, k, :, 1, sl] = li[l, g][None, :]
                    s5row[l, hg, k, :, 2, sl] = ldt[l, g]
                    s5bz[l, hg, k, 16 * gl:16 * gl + 16, 0, sl] = bre[l, g].T
                    s5bz[l, hg, k, 16 * gl:16 * gl + 16, 1, sl] = bim[l, g].T
                    s5cz[l, hg, k, sl, 0, 16 * gl:16 * gl + 16] = cre[l, g].T
                    s5cz[l, hg, k, sl, 1, 16 * gl:16 * gl + 16] = cim[l, g].T
            s5dz[l, hg][np.arange(64), np.arange(64)] = dsk[l, 4 * hg:4 * hg + 4].reshape(64)
    d["s5col"], d["s5row"], d["s5bz"], d["s5cz"], d["s5dz"] = s5col, s5row, s5bz, s5cz, s5dz
    wa2 = np.asarray(inp["gla_w_a2"])
    ba = np.asarray(inp["gla_b_a"])
    d["wa2"] = f(np.stack([[wa2[l][:, 64 * (hg // 2):64 * (hg // 2) + 64] for hg in range(NHG)] for l in range(L)]))
    d["ba"] = f(np.stack([[ba[l][64 * (hg // 2):64 * (hg // 2) + 64].reshape(64, 1) for hg in range(NHG)] for l in range(L)]))
    ck, cv = np.asarray(inp["cache_swa_k"]), np.asarray(inp["cache_swa_v"])
    d["ckT"] = f(ck.transpose(0, 3, 4, 1, 2))
    d["cvS"] = f(cv.transpose(0, 3, 2, 1, 4))
    d["ckN"] = f(ck.transpose(0, 3, 1, 2, 4))
    d["cvN"] = f(cv.transpose(0, 3, 1, 2, 4))
    sre, sim = np.asarray(inp["state_ssm_re"]), np.asarray(inp["state_ssm_im"])
    ssm0 = np.stack([sre, sim], 1)
    d["ssm0"] = f(ssm0.reshape(L, 2, 128, 16, 128).transpose(0, 1, 3, 4, 2))
    gst = np.asarray(inp["state_gla"])
    d["gst"] = f(gst.reshape(L, 128, 4, 64, 2, 64).transpose(0, 2, 4, 3, 1, 5).reshape(L, NHG, 64, 128, 64))
    cpv = np.asarray(inp["state_conv"])
    d["cprevT"] = f(cpv.transpose(0, 2, 3, 1))
    d["wout"] = f(inp["w_out"])
    d["wup"] = f(inp["ffn_w_up"])
    d["wdown"] = f(inp["ffn_w_down"])
    d["wglu"] = f(inp["s5_w_glu"])
    d["ln1"] = f(np.stack([np.stack([vec16(inp["ln1_g"][l]), vec16(inp["ln1_b"][l])]) for l in range(L)]))
    d["ln2"] = f(np.stack([np.stack([vec16(inp["ln2_g"][l]), vec16(inp["ln2_b"][l])]) for l in range(L)]))
    cw, cb = np.asarray(inp["ffn_conv_w"]), np.asarray(inp["ffn_conv_b"])
    d["convw"] = f(np.stack([np.stack([cw[l, j].reshape(NFC, 128).T for j in range(3)] + [cb[l].reshape(NFC, 128).T]) for l in range(L)]))
    d["bglu"] = f(np.stack([np.asarray(inp["s5_b_glu"][l]).reshape(4, 128).T for l in range(L)]))
    d["gnorm"] = f(np.stack([np.asarray(inp["gla_norm_g"][l]).reshape(4, 128).T for l in range(L)]))
    return d


INPUT_SHAPES = None


def chunked(ap2d):
    return ap2d.rearrange("(c p) n -> p c n", p=128)


class Builder:
    def __init__(self, shapes, debug=False):
        self.debug = debug
        self.nc = nc = bass.Bass("TRN2", target_bir_lowering=False)
        self.I = {}
        for name, shp in shapes.items():
            self.I[name] = nc.dram_tensor(name, list(shp), F32, kind="ExternalInput").ap()
        O = self.O = {}

        def out(name, shp):
            O[name] = nc.dram_tensor(name, list(shp), F32, kind="ExternalOutput").ap()
        out("y_out", [CPC, D])
        out("ks_out", [L, 1, 128, 128, 64])
        out("vs_out", [L, 1, 128, 128, 64])
        out("kp_out", [L, 1, 128, 64])
        out("vp_out", [L, 1, 128, 64])
        out("ssmp_out", [L, 2, 2, 128])
        out("ssms_out", [L, 2, 2, 128, 128])
        out("glap_out", [L, 1, 64, 64])
        out("glas_out", [L, 1, 64, 128, 64])
        out("convp_out", [L, 128, NFC, 2])
        out("convs_out", [L, 2, DFF, 128])
        kind = "ExternalOutput" if debug else "Internal"
        self.xn_in = nc.dram_tensor("xn_in", [D, CPC], BF16).ap()
        self.xg = [nc.dram_tensor(f"xg{l}", [NCORES * D, CPC], BF16).ap() for l in range(L)]
        self.xres_loc = nc.dram_tensor("xres_loc", [D, CPC], F32).ap()
        self.mb_in = nc.dram_tensor("mb_in", [BROWS, G + 2], BF16).ap()
        self.mb_loc = nc.dram_tensor("mb_loc", [NHG * BROWS, CPC + 2], BF16).ap()
        self.mbufs = [nc.dram_tensor(f"mbuf{l}", [NHG * BROWS, G + 2], BF16, kind=kind).ap() for l in range(L)]
        self.vs_scr = nc.dram_tensor("vs_scr", [128, 64], F32).ap()
        self.CONV = ["ident", "pm", "win_all", "s5dz", "wa2", "ckT", "cvS", "wglu", "wout", "wup", "wdown"]
        self.Wb = {k: nc.dram_tensor(k + "_bf", list(shapes[k]), BF16).ap() for k in self.CONV}

    def T(self, es, name, shape, dt):
        self._n = getattr(self, "_n", 0) + 1
        return es.enter_context(self.nc.sbuf_tensor(f"{name}_{self._n}", shape, dt))

    def build(self, n_layers=L, stop_after=None):
        nc = self.nc
        with ExitStack() as es:
            self.P = P = Prog(nc, es)
            self.es = es
            self.ps = [es.enter_context(nc.psum_tensor(f"ps{i}", [128, 512], F32)) for i in range(6)]
            self.psb = [es.enter_context(nc.psum_tensor(f"psb{i}", [128, 1024], BF16)) for i in range(2)]
            self.ps_i = 0
            self.psb_i = 0
            self.pid = nc.sync.partition_id()
            self.convert_weights()
            self.setup_consts()
            self.p0()
            for l in range(n_layers):
                self.cur_l = l
                self.p1(l, 0)
                P.collective("AllGather", self.mb_in, self.mbufs[l], reads=["mb_in"], writes=[f"mbuf{l}"])
                if isinstance(stop_after, tuple) and stop_after[:2] == ("p1", l):
                    break
                self.p2(l, last=(l == n_layers - 1))
            P.barrier()
            P.finish("sp")
        return nc

    def convert_weights(self):
        P = self.P
        for k in self.CONV:
            src, dst = self.I[k], self.Wb[k]
            nd = len(src.shape)
            names = " ".join(f"d{i}" for i in range(nd))
            pat = f"{names} -> ({' '.join(f'd{i}' for i in range(nd - 1))}) d{nd - 1}"
            s2, d2 = src.rearrange(pat), dst.rearrange(pat)
            rows = s2.shape[0]
            step = max(1, min(rows, (1 << 22) // max(1, s2.shape[1])))
            for r0 in range(0, rows, step):
                r1 = min(rows, r0 + step)
                P.dma("pool", d2[r0:r1, :], s2[r0:r1, :])
        P.barrier()

    def bank(self):
        i = self.ps_i
        self.ps_i = (i + 1) % len(self.ps)
        return self.ps[i], f"ps{i}"

    def bankb(self):
        i = self.psb_i
        self.psb_i = (i + 1) % len(self.psb)
        return self.psb[i], f"psb{i}"

    def setup_consts(self):
        P, es, I = self.P, self.es, self.I
        T = lambda n, s, d: self.T(es, n, s, d)
        self.ident_f = T("ident_f", [128, 128], F32)
        self.ident_b = T("ident_b", [128, 128], BF16)
        self.pm_b = T("pm_b", [128, 128], BF16)
        self.mask_f = T("mask_f", [128, 3, 256], F32)
        self.ones_f = T("ones_f", [128, 512], F32)
        self.ones_b = T("ones_b", [128, 128], BF16)
        self.iota1 = T("iota1", [128, 512], F32)
        self.iota_i = T("iota_i", [128, 512], I32)
        self.resetm = T("resetm", [128, 512], F32)
        self.blk2 = T("blk2", [128, 2], F32)
        self.sel65 = T("sel65", [128, 64], F32)
        P.dma("sp", self.ident_f[:], I["ident"][:, :], writes=["ident_f"])
        P.dma("sp", self.ident_b[:], self.Wb["ident"][:, :], writes=["ident_b"])
        P.dma("sp", self.pm_b[:], self.Wb["pm"][:, :], writes=["pm_b"])
        P.dma("sp", self.mask_f[:], I["swamask"][:, :, :], writes=["mask_f"])
        P.op("dve", MEMSET(self.ones_f[:], 1.0), writes=["ones_f"])
        P.op("dve", MEMSET(self.ones_b[:], 1.0), writes=["ones_b"])
        P.op("pool", lambda e: e.iota(self.iota_i[:], pattern=[[1, 512]], base=1, channel_multiplier=0), writes=["iota_i"])
        P.op("dve", CP(self.iota1[:], self.iota_i[:]), reads=["iota_i"], writes=["iota1"])
        P.op("dve", MEMSET(self.resetm[:], 1.0), writes=["resetm"])
        for c in range(4):
            P.op("dve", MEMSET(self.resetm[:, c * 128:c * 128 + 1], 0.0), writes=["resetm"])
        P.op("dve", MEMSET(self.blk2[:], 0.0), writes=["blk2"])
        P.op("dve", MEMSET(self.blk2[0:64, 0:1], 1.0), writes=["blk2"])
        P.op("dve", MEMSET(self.blk2[64:128, 1:2], 1.0), writes=["blk2"])
        self.cmask = T("cmask", [128, CPC + 2], F32)
        self.nsm = T("nsm", [128, 1], F32)
        self.zero_b = T("zero_b", [128, 2], BF16)
        P.dma("sp", self.cmask[:], I["cmask"][0:1, :].partition_broadcast(128), writes=["cmask"])
        P.dma("sp", self.nsm[:], I["nsm"][:, :], writes=["nsm"])
        P.op("dve", MEMSET(self.zero_b[:], 0.0), writes=["zero_b"])
        P.dma("sp", self.mb_in[0:128, 0:2], self.zero_b[:], reads=["zero_b"], writes=["mb_in"])
        P.dma("sp", self.mb_in[128:256, 0:2], self.zero_b[:], reads=["zero_b"], writes=["mb_in"])
        P.dma("sp", self.mb_in[256:320, 0:2], self.zero_b[0:64, :], reads=["zero_b"], writes=["mb_in"])
        P.op("dve", MEMSET(self.sel65[:], 0.0), writes=["sel65"])
        P.op("dve", MEMSET(self.sel65[64:65, :], 1.0), writes=["sel65"])

    def sincos(self, ang, ang_key, sin_out, cos_out, out_key, tmp_f, tmp_i, tmp_key, np_, nf):
        P = self.P
        a = ang
        for shift, dst in ((0.0, sin_out), (math.pi / 2, cos_out)):
            if dst is None:
                continue
            P.op("dve", TS(tmp_i, a, 1.0 / TWO_PI, ALU.mult, shift / TWO_PI, ALU.add), reads=[ang_key], writes=[tmp_key + "i"])
            P.op("dve", CP(tmp_f, tmp_i), reads=[tmp_key + "i"], writes=[tmp_key + "f"])
            P.op("dve", STT(dst, tmp_f, -C1, a, ALU.mult, ALU.add), reads=[tmp_key + "f", ang_key], writes=[out_key])
            P.op("dve", STT(dst, tmp_f, -C2, dst, ALU.mult, ALU.add), reads=[tmp_key + "f", out_key], writes=[out_key])
            if shift:
                P.op("dve", TS(dst, dst, shift, ALU.add), reads=[out_key], writes=[out_key])
            P.op("dve", TS(dst, dst, -math.pi, ALU.max, math.pi, ALU.min), reads=[out_key], writes=[out_key])
            P.op("act", ACT(dst, dst, AF.Sin), reads=[out_key], writes=[out_key])

    def layernorm(self, es, xr, xr_key, ncols, gb, gb_key, xb=None, xb_key=None, mask=None):
        P = self.P
        T = lambda n, s, d: self.T(es, n, s, d)
        LW = CPC + 2
        sq = T("ln_sq", [128, 512], F32)
        mean = T("ln_mean", [128, LW], F32)
        rstd = T("ln_rstd", [128, LW], F32)
        shift = T("ln_shift", [128, LW], F32)
        tmp = T("ln_tmp", [128, LW], F32)
        nblk = (ncols + 511) // 512
        for b in range(nblk):
            c0, c1 = b * 512, min(ncols, b * 512 + 512)
            n = c1 - c0
            ps_s, ks = self.bank()
            ps_q, kq = self.bank()
            fs, fq = [], []
            for k in range(16):
                P.op("act", ACT(sq[:, 0:n], xr[:, k, c0:c1], AF.Square), reads=[xr_key], writes=["ln_sq"])
                P.op("pe", [MM(ps_s[:, 0:n], self.ones_f[:, 0:128], xr[:, k, c0:c1], start=(k == 0), stop=(k == 15)),
                            MM(ps_q[:, 0:n], self.ones_f[:, 0:128], sq[:, 0:n], start=(k == 0), stop=(k == 15))],
                     reads=["ones_f", xr_key, "ln_sq"], writes=[ks, kq])
            P.op("act", ACT(mean[:, c0:c1], ps_s[:, 0:n], AF.Copy, scale=1.0 / D), reads=[ks], writes=["ln_mean"])
            P.op("dve", TT(tmp[:, c0:c1], mean[:, c0:c1], mean[:, c0:c1], ALU.mult), reads=["ln_mean"], writes=["ln_tmp"])
            P.op("dve", STT(tmp[:, c0:c1], ps_q[:, 0:n], 1.0 / D, tmp[:, c0:c1], ALU.mult, ALU.subtract), reads=[kq, "ln_tmp"], writes=["ln_tmp"])
            P.op("dve", TS(tmp[:, c0:c1], tmp[:, c0:c1], EPS, ALU.add), reads=["ln_tmp"], writes=["ln_tmp"])
            P.op("act", ACT(tmp[:, c0:c1], tmp[:, c0:c1], AF.Sqrt), reads=["ln_tmp"], writes=["ln_tmp"])
            P.op("dve", RECIP(rstd[:, c0:c1], tmp[:, c0:c1]), reads=["ln_tmp"], writes=["ln_rstd"])
            P.op("dve", STT(shift[:, c0:c1], mean[:, c0:c1], -1.0, rstd[:, c0:c1], ALU.mult, ALU.mult), reads=["ln_mean", "ln_rstd"], writes=["ln_shift"])
        for k in range(16):
            xk = xr[:, k, 0:ncols]
            P.op("dve", TT(xk, xk, rstd[:, 0:ncols], ALU.mult), reads=[xr_key, "ln_rstd"], writes=[xr_key])
            P.op("dve", TT(xk, xk, shift[:, 0:ncols], ALU.add), reads=[xr_key, "ln_shift"], writes=[xr_key])
            P.op("act", ACT(xk, xk, AF.Identity, scale=gb[:, 0, k:k + 1], bias=gb[:, 1, k:k + 1]), reads=[xr_key, gb_key], writes=[xr_key])
            if xb is not None:
                P.op("dve", TT(xb[:, k, 0:ncols], xk, mask, ALU.mult), reads=[xr_key, "cmask"], writes=[xb_key])

    def p0(self):
        P, I = self.P, self.I
        with ExitStack() as es:
            T = lambda n, s, d: self.T(es, n, s, d)
            xr = T("p0_xr", [128, 16, CPC], F32)
            xb = T("p0_xb", [128, 16, CPC], BF16)
            gb = T("p0_gb", [128, 2, 16], F32)
            P.dma("sp", gb[:], I["ln_in"].rearrange("t p k -> p t k"), writes=["p0_gb"])
            P.dma("sp", xr[:], chunked(I["xT"]), writes=["p0_xr"])
            with ExitStack() as es2:
                self.layernorm(es2, xr, "p0_xr", CPC, gb, "p0_gb", xb, "p0_xb", mask=self.cmask[:, 2:CPC + 2])
                P.dma("sp", chunked(self.xres_loc), xr[:], reads=["p0_xr"], writes=["xres_loc"])
                P.dma("sp", chunked(self.xn_in), xb[:], reads=["p0_xb"], writes=["xn_in"])
                P.barrier()
            P.collective("AllGather", self.xn_in, self.xg[0], reads=["xn_in"], writes=["xg0"])
            P.barrier()

    def p1(self, l, hg):
        P, I, O = self.P, self.I, self.O
        kvh = hg // 2
        NT = (G + W1 - 1) // W1
        with ExitStack() as es:
            T = lambda n, s, d: self.T(es, n, s, d)
            win = T("win", [128, 16, 768], BF16)
            P.dma("sp", win[:, :, 0:NWC], chunked(self.Wb["win_all"][l, hg]), writes=["win"])
            sinkn = T("sinkn", [128, 2], F32)
            P.dma("sp", sinkn[:], I["sink"][l, hg], writes=["sinkn"])
            P.op("dve", TS(sinkn[:], sinkn[:], -1.0, ALU.mult), reads=["sinkn"], writes=["sinkn"])
            Ec, Es, magt, bbre, bbim, czre, nczim, arc, aic, naic = [], [], [], [], [], [], [], [], [], []
            pre = {}
            for k in range(2):
                pre[("arc", k)], pre[("aic", k)], pre[("naic", k)] = T("arc", [128, 1], F32), T("aic", [128, 1], F32), T("naic", [128, 1], F32)
                pre[("Ec", k)], pre[("Es", k)], pre[("magt", k)] = T("Ec", [128, 512], F32), T("Es", [128, 512], F32), T("magt", [128, 512], F32)
                pre[("bbre", k)], pre[("bbim", k)] = T("bbre", [128, 128], BF16), T("bbim", [128, 128], BF16)
                pre[("czre", k)], pre[("nczim", k)] = T("czre", [128, 64], BF16), T("nczim", [128, 64], BF16)
            dz_b = T("dz_b", [128, 64], BF16)
            wa2_b = T("wa2_b", [16, 64], BF16)
            nba = T("nba", [64, 1], F32)
            with ExitStack() as esp:
                Tp = lambda n, s, d: self.T(esp, n, s, d)
                ang = Tp("ang", [128, 512], F32)
                tf = Tp("sc_tf", [128, 512], F32)
                ti = Tp("sc_ti", [128, 512], I32)
                for k in range(0 if 's5prep' in DBG_SKIP else 2):
                    cp = Tp("cp", [128, 3], F32)
                    sm = Tp("sm", [128, 12], F32)
                    P.dma("sp", cp[:], I["s5col"][l, hg, k], writes=["cp"])
                    dtc, thc, lrdt, magc, sc_, cc_ = (sm[:, j:j + 1] for j in range(6))
                    P.op("act", ACT(dtc, cp[:, 2:3], AF.Exp), reads=["cp"], writes=["sm"])
                    P.op("dve", TT(thc, cp[:, 1:2], dtc, ALU.mult), reads=["cp", "sm"], writes=["sm"])
                    P.op("dve", TT(lrdt, cp[:, 0:1], dtc, ALU.mult), reads=["cp", "sm"], writes=["sm"])
                    P.op("act", ACT(magc, lrdt, AF.Exp), reads=["sm"], writes=["sm"])
                    sct = Tp("sct", [128, 2], F32)
                    self.sincos(thc, "sm", sct[:, 0:1], sct[:, 1:2], "sct", tf[:, 0:1], ti[:, 0:1], "sc_t", 128, 1)
                    a_r, a_i, na_i = pre[("arc", k)], pre[("aic", k)], pre[("naic", k)]
                    P.op("dve", TT(a_r[:], magc, sct[:, 1:2], ALU.mult), reads=["sm", "sct"], writes=["arc"])
                    P.op("dve", TT(a_i[:], magc, sct[:, 0:1], ALU.mult), reads=["sm", "sct"], writes=["aic"])
                    P.op("dve", TS(na_i[:], a_i[:], -1.0, ALU.mult), reads=["aic"], writes=["naic"])
                    arc.append(a_r); aic.append(a_i); naic.append(na_i)
                    ec, esn, mt = pre[("Ec", k)], pre[("Es", k)], pre[("magt", k)]
                    P.op("dve", TS(ang[:], self.iota1[:], thc, ALU.mult), reads=["iota1", "sm"], writes=["ang"])
                    self.sincos(ang[:], "ang", esn[:], ec[:], f"EcEs{k}", tf[:], ti[:], "sc_t", 128, 512)
                    P.op("dve", TS(mt[:], self.ones_f[:], magc, ALU.mult), reads=["ones_f", "sm"], writes=[f"magt{k}"])
                    Ec.append(ec); Es.append(esn); magt.append(mt)
                    rp = Tp("rp", [128, 3, 128], F32)
                    bz = Tp("bz", [128, 2, 128], F32)
                    r = Tp("rwork", [128, 12, 128], F32)
                    P.dma("sp", rp[64:128], I["s5row"][l, hg, k], writes=["rp"])
                    P.dma("sp", bz[64:128], I["s5bz"][l, hg, k], writes=["bz"])
                    h = slice(64, 128)
                    lrr, lir, ldr = rp[h, 0, :], rp[h, 1, :], rp[h, 2, :]
                    R_ = lambda j: r[h, j, :]
                    rk = ["rp", "rwork"]
                    P.op("act", ACT(R_(0), ldr, AF.Exp), reads=rk, writes=["rwork"])
                    P.op("dve", TT(R_(1), lir, R_(0), ALU.mult), reads=rk, writes=["rwork"])
                    P.op("dve", TT(R_(2), lrr, R_(0), ALU.mult), reads=rk, writes=["rwork"])
                    P.op("act", ACT(R_(2), R_(2), AF.Exp), reads=rk, writes=["rwork"])
                    self.sincos(R_(1), "rwork", R_(3), R_(4), "rwork", tf[h, 0:128], ti[h, 0:128], "sc_t", 64, 128)
                    P.op("dve", TT(R_(4), R_(4), R_(2), ALU.mult), reads=rk, writes=["rwork"])
                    P.op("dve", TT(R_(3), R_(3), R_(2), ALU.mult), reads=rk, writes=["rwork"])
                    P.op("dve", TS(R_(4), R_(4), -1.0, ALU.add), reads=rk, writes=["rwork"])
                    P.op("dve", TT(R_(5), lrr, lrr, ALU.mult), reads=rk, writes=["rwork"])
                    P.op("dve", TT(R_(6), lir, lir, ALU.mult), reads=rk, writes=["rwork"])
                    P.op("dve", TT(R_(5), R_(5), R_(6), ALU.add), reads=rk, writes=["rwork"])
                    P.op("dve", RECIP(R_(5), R_(5)), reads=rk, writes=["rwork"])
                    P.op("dve", TT(R_(6), R_(4), lrr, ALU.mult), reads=rk, writes=["rwork"])
                    P.op("dve", TT(R_(7), R_(3), lir, ALU.mult), reads=rk, writes=["rwork"])
                    P.op("dve", TT(R_(6), R_(6), R_(7), ALU.add), reads=rk, writes=["rwork"])
                    P.op("dve", TT(R_(6), R_(6), R_(5), ALU.mult), reads=rk, writes=["rwork"])
                    P.op("dve", TT(R_(7), R_(3), lrr, ALU.mult), reads=rk, writes=["rwork"])
                    P.op("dve", TT(R_(8), R_(4), lir, ALU.mult), reads=rk, writes=["rwork"])
                    P.op("dve", TT(R_(7), R_(7), R_(8), ALU.subtract), reads=rk, writes=["rwork"])
                    P.op("dve", TT(R_(7), R_(7), R_(5), ALU.mult), reads=rk, writes=["rwork"])
                    b_re, b_im = pre[("bbre", k)], pre[("bbim", k)]
                    rb = ["rwork", "bz"]
                    P.op("dve", TT(R_(8), R_(6), bz[h, 0, :], ALU.mult), reads=rb, writes=["rwork"])
                    P.op("dve", TT(R_(9), R_(7), bz[h, 1, :], ALU.mult), reads=rb, writes=["rwork"])
                    P.op("dve", TT(b_re[h, :], R_(8), R_(9), ALU.subtract), reads=rb, writes=[f"bb{k}"])
                    P.op("dve", TT(R_(8), R_(6), bz[h, 1, :], ALU.mult), reads=rb, writes=["rwork"])
                    P.op("dve", TT(R_(9), R_(7), bz[h, 0, :], ALU.mult), reads=rb, writes=["rwork"])
                    P.op("dve", TT(b_im[h, :], R_(8), R_(9), ALU.add), reads=rb, writes=[f"bb{k}"])
                    bbre.append(b_re); bbim.append(b_im)
                    czf = Tp("czf", [128, 2, 64], F32)
                    P.dma("sp", czf[:], I["s5cz"][l, hg, k], writes=["czf"])
                    c_re, nc_im = pre[("czre", k)], pre[("nczim", k)]
                    P.op("dve", CP(c_re[:], czf[:, 0, :]), reads=["czf"], writes=[f"cz{k}"])
                    P.op("dve", TS(nc_im[:], czf[:, 1, :], -1.0, ALU.mult), reads=["czf"], writes=[f"cz{k}"])
                    czre.append(c_re); nczim.append(nc_im)
                P.dma("sp", dz_b[64:128, :], self.Wb["s5dz"][l, hg], writes=["dz_b"])
                P.dma("sp", wa2_b[:], self.Wb["wa2"][l, hg], writes=["wa2_b"])
                P.dma("sp", nba[:], I["ba"][l, hg], writes=["nba"])
                P.op("dve", TS(nba[:], nba[:], -1.0, ALU.mult), reads=["nba"], writes=["nba"])
                P.barrier()
            xt = [T("xt", [128, 16, W1], BF16) for _ in range(2)]
            ropet = [T("ropet", [128, 2, W1], F32) for _ in range(2)]
            qT = T("qT", [128, W1], BF16)
            qb = T("qb", [128, W1], BF16)
            kT = T("kT", [128, 128 + W1], BF16)
            kf = T("kf", [128, 256], F32)
            vf = T("vf", [128, 256], F32)
            t1 = T("t1", [128, W1], F32)
            t2 = T("t2", [128, W1], F32)
            t3 = T("t3", [128, W1], F32)
            t4 = T("t4", [128, W1], F32)
            gre = T("gre", [128, W1], F32)
            gim = T("gim", [128, W1], F32)
            VUb = T("VUb", [128, W1], BF16)
            Vtm = T("Vtm", [128, 5, 65], BF16)
            oa2 = T("oa2", [128, 128], BF16)
            oaT_st = T("oaT_st", [128, W1], BF16)
            pexp = [T("pexp", [128, 256], F32) for _ in range(2)]
            PTt = [T("PT", [128, 256], BF16) for _ in range(2)]
            rden = T("rden", [128, 4], F32)
            hprev = T("hprev", [128, 4], F32)
            hre_b = [T("hre_b", [128, W1], BF16) for _ in range(2)]
            him_b = [T("him_b", [128, W1], BF16) for _ in range(2)]
            z_st = T("z_st", [64, W1], BF16)
            qg_f = T("qg_f", [64, W1], F32)
            kg_f = T("kg_f", [64, W1], F32)
            VGb = T("VGb", [128, W1], BF16)
            vgf = T("vgf", [64, 128], F32)
            acb = T("acb", [16, W1], BF16)
            lg = T("lg", [64, W1], F32)
            cs = T("cs", [64, W1], F32)
            Ep = T("Ep", [64, W1], F32)
            Em = T("Em", [64, W1], F32)
            qin_b = T("qin_b", [64, W1], BF16)
            kout_f = T("kout_f", [64, W1], F32)
            kout_b = T("kout_b", [64, W1], BF16)
            kend_b = T("kend_b", [64, W1], BF16)
            attT_b = [T("attT_b", [128, 128], BF16) for _ in range(2)]
            vtm_b = [T("vtm_b", [128, 64], BF16) for _ in range(2)]
            kend_tm = [T("kend_tm", [128, 64], BF16) for _ in range(2)]
            S_f = T("S_f", [64, 64], F32)
            S_b = T("S_b", [64, 64], BF16)
            o_st = T("o_st", [64, W1], BF16)
            if 'initoa' in DBG_SKIP:
                P.op("dve", MEMSET(oaT_st[:], 0.0), writes=["oaT_st"])
            P.op("dve", MEMSET(kT[:, 0:128], 0.0), writes=["kT"])
            P.op("dve", MEMSET(Vtm[:], 0.0), writes=["Vtm"])
            P.op("dve", MEMSET(Vtm[:, :, 64:65], 1.0), writes=["Vtm"])
            P.op("dve", MEMSET(hprev[:], 0.0), writes=["hprev"])
            P.op("dve", MEMSET(S_f[:], 0.0), writes=["S_f"])
            P.op("dve", MEMSET(S_b[:], 0.0), writes=["S_b"])

            def load_tile(t):
                c0 = t * W1
                n = min(W1, G - c0)
                xg3 = self.xg[l].rearrange("(r c p) n -> r p c n", r=NCORES, p=128)
                a = c0
                while a < c0 + n:
                    r = a // CPC
                    b = min(c0 + n, (r + 1) * CPC)
                    P.dma("sp", xt[t % 2][:, :, a - c0:b - c0], xg3[r][:, :, a - r * CPC:b - r * CPC], reads=[f"xg{l}"], writes=[f"xt{t % 2}"])
                    a = b
                if 'ropedma' not in DBG_SKIP:
                    P.dma("sp", ropet[t % 2][:, :, 0:n], I["rope"][:, :, c0:c0 + n].rearrange("t p n -> p t n"), writes=[f"ropet{t % 2}"])

            if 'tiles' in DBG_SKIP:
                NT = 0
            else:
                load_tile(0)
            for t in range(NT):
                c0 = t * W1
                n = min(W1, G - c0)
                nblk = n // 128
                last = (t == NT - 1)
                if t + 1 < NT:
                    load_tile(t + 1)
                X, xk_ = xt[t % 2], f"xt{t % 2}"
                RT, rk_ = ropet[t % 2], f"ropet{t % 2}"

                def proj(c_lo, c_hi, m):
                    ps, key = self.bank()
                    P.op("pe", [MM(ps[0:m, 0:n], win[:, k, c_lo:c_hi], X[:, k, 0:n], start=(k == 0), stop=(k == 15)) for k in range(16)],
                         reads=["win", xk_], writes=[key])
                    return ps, key

                def roped(c_lo, dst, dst_key, f32_dst=None):
                    ps, key = proj(c_lo, c_lo + 128, 128)
                    P.op("act", ACT(qb[:, 0:n], ps[:, 0:n], AF.Copy), reads=[key], writes=["qb"])
                    ps2, key2 = self.bank()
                    P.op("pe", MM(ps2[:, 0:n], self.pm_b[:], qb[:, 0:n]), reads=["pm_b", "qb"], writes=[key2])
                    P.op("dve", TT(t1[:, 0:n], ps[:, 0:n], RT[:, 0, 0:n], ALU.mult), reads=[key, rk_], writes=["t1"])
                    P.op("dve", TT(t2[:, 0:n], ps2[:, 0:n], RT[:, 1, 0:n], ALU.mult), reads=[key2, rk_], writes=["t2"])
                    P.op("dve", TT(dst, t1[:, 0:n], t2[:, 0:n], ALU.add), reads=["t1", "t2"], writes=[dst_key])
                    if f32_dst is not None:
                        P.op("dve", TT(f32_dst, t1[:, 0:n], t2[:, 0:n], ALU.add), reads=["t1", "t2"], writes=["kf"])

                if 'rope' not in DBG_SKIP:
                    roped(0, qT[:, 0:n], "qT")
                    roped(128, kT[:, 128:128 + n], "kT", f32_dst=(kf[:, 0:n] if last else None))
                if 'vu' in DBG_SKIP:
                    continue
                ps_vu, kvu = proj(256, 384, 128)
                P.op("act", ACT(VUb[:, 0:n], ps_vu[:, 0:n], AF.Copy), reads=[kvu], writes=["VUb"])
                if last and 'novf' not in DBG_SKIP:
                    P.op("act", ACT(vf[0:64, 0:n], ps_vu[0:64, 0:n], AF.Copy), reads=[kvu], writes=["vf"])
                for i in range(0 if 'vtr' in DBG_SKIP else nblk):
                    pb_, kb_ = self.bankb()
                    P.op("pe", TR(pb_[:, 0:64], VUb[0:64, i * 128:(i + 1) * 128], self.ident_b[0:64, 0:64]), reads=["VUb", "ident_b"], writes=[kb_])
                    P.op("act", ACT(Vtm[:, i + 1, 0:64], pb_[:, 0:64], AF.Copy), reads=[kb_], writes=["Vtm"])
                for i in range(0 if 'swa' in DBG_SKIP else nblk):
                    gb = 4 * t + i
                    if gb == NB - 1:
                        continue
                    mv = 1 if gb == 0 else (2 if gb == 1 else 0)
                    for hh in range(2):
                        pb = 64 * hh
                        ps_s, ks_ = self.bank()
                        qs = qT[pb:pb + 64, i * 128:(i + 1) * 128]
                        P.op("pe", [MM(ps_s[:, 0:128], kT[pb:pb + 64, i * 128:(i + 1) * 128], qs),
                                    MM(ps_s[:, 128:256], kT[pb:pb + 64, (i + 1) * 128:(i + 2) * 128], qs)],
                             reads=["kT", "qT"], writes=[ks_])
                        P.op("act", ACT(pexp[hh][:], ps_s[:, 0:256], AF.Exp, scale=0.125, bias=sinkn[:, hh:hh + 1]), reads=[ks_, "sinkn"], writes=[f"pexp{hh}"])
                        P.op("dve", TT(PTt[hh][:], pexp[hh][:], self.mask_f[:, mv, :], ALU.mult), reads=[f"pexp{hh}", "mask_f"], writes=[f"PT{hh}"])
                        ps_o, ko_ = self.bank()
                        P.op("pe", [MM(ps_o[:, 0:65], PTt[hh][:, 0:128], Vtm[:, i, :], start=True, stop=False),
                                    MM(ps_o[:, 0:65], PTt[hh][:, 128:256], Vtm[:, i + 1, :], start=False, stop=True)],
                             reads=[f"PT{hh}", "Vtm"], writes=[ko_])
                        P.op("dve", TS(rden[:, hh:hh + 1], ps_o[:, 64:65], 1.0, ALU.add), reads=[ko_], writes=["rden"])
                        P.op("dve", RECIP(rden[:, hh:hh + 1], rden[:, hh:hh + 1]), reads=["rden"], writes=["rden"])
                        P.op("act", ACT(oa2[:, pb:pb + 64], ps_o[:, 0:64], AF.Copy, scale=rden[:, hh:hh + 1]), reads=[ko_, "rden"], writes=["oa2"])
                    pb_, kb_ = self.bankb()
                    P.op("pe", TR(pb_[:, 0:128], oa2[:], self.ident_b[:]), reads=["oa2", "ident_b"], writes=[kb_])
                    P.op("act", ACT(oaT_st[:, i * 128:(i + 1) * 128], pb_[:, 0:128], AF.Copy), reads=[kb_], writes=["oaT_st"])
                npr = n - 128 if last else n
                if 'nombuf' not in DBG_SKIP:
                    P.dma("sp", self.mb_in[0:128, 2 + c0:2 + c0 + npr], oaT_st[:, 0:npr], reads=["oaT_st"])
                for k in range(0 if 's5' in DBG_SKIP else 2):
                    ps_re, kre = self.bank()
                    ps_im, kim = self.bank()
                    P.op("pe", MM(ps_re[:, 0:n], bbre[k][64:128, :], VUb[64:128, 0:n]), reads=[f"bb{k}", "VUb"], writes=[kre])
                    P.op("pe", MM(ps_im[:, 0:n], bbim[k][64:128, :], VUb[64:128, 0:n]), reads=[f"bb{k}", "VUb"], writes=[kim])
                    ek = f"EcEs{k}"
                    ec, esn = Ec[k][:, 0:n], Es[k][:, 0:n]
                    P.op("dve", TT(t1[:, 0:n], ps_re[:, 0:n], ec, ALU.mult), reads=[kre, ek], writes=["t1"])
                    P.op("dve", TT(t2[:, 0:n], ps_im[:, 0:n], esn, ALU.mult), reads=[kim, ek], writes=["t2"])
                    P.op("dve", TT(t1[:, 0:n], t1[:, 0:n], t2[:, 0:n], ALU.add), reads=["t1", "t2"], writes=["t1"])
                    P.op("dve", TT(t3[:, 0:n], ps_im[:, 0:n], ec, ALU.mult), reads=[kim, ek], writes=["t3"])
                    P.op("dve", TT(t4[:, 0:n], ps_re[:, 0:n], esn, ALU.mult), reads=[kre, ek], writes=["t4"])
                    P.op("dve", TT(t3[:, 0:n], t3[:, 0:n], t4[:, 0:n], ALU.subtract), reads=["t3", "t4"], writes=["t3"])
                    P.op("dve", SCAN(gre[:, 0:n], magt[k][:, 0:n], t1[:, 0:n], hprev[:, 2 * k:2 * k + 1]), reads=[f"magt{k}", "t1", "hprev"], writes=["gre"])
                    P.op("dve", SCAN(gim[:, 0:n], magt[k][:, 0:n], t3[:, 0:n], hprev[:, 2 * k + 1:2 * k + 2]), reads=[f"magt{k}", "t3", "hprev"], writes=["gim"])
                    P.op("dve", TT(t1[:, 0:n], gre[:, 0:n], ec, ALU.mult), reads=["gre", ek], writes=["t1"])
                    P.op("dve", TT(t2[:, 0:n], gim[:, 0:n], esn, ALU.mult), reads=["gim", ek], writes=["t2"])
                    P.op("dve", TT(hre_b[k][:, 0:n], t1[:, 0:n], t2[:, 0:n], ALU.subtract), reads=["t1", "t2"], writes=[f"hre{k}"])
                    P.op("dve", TT(t3[:, 0:n], gre[:, 0:n], esn, ALU.mult), reads=["gre", ek], writes=["t3"])
                    P.op("dve", TT(t4[:, 0:n], gim[:, 0:n], ec, ALU.mult), reads=["gim", ek], writes=["t4"])
                    P.op("dve", TT(him_b[k][:, 0:n], t3[:, 0:n], t4[:, 0:n], ALU.add), reads=["t3", "t4"], writes=[f"him{k}"])
                    cc = npr - 1
                    cs1 = slice(cc, cc + 1)
                    P.op("dve", TT(hprev[:, 2 * k:2 * k + 1], t1[:, cs1], t2[:, cs1], ALU.subtract), reads=["t1", "t2"], writes=["hprev"])
                    P.op("dve", TT(hprev[:, 2 * k + 1:2 * k + 2], t3[:, cs1], t4[:, cs1], ALU.add), reads=["t3", "t4"], writes=["hprev"])
                    if last:
                        for ri in range(2):
                            P.dma("sp", O["ssmp_out"][l, ri, 2 * hg + k].rearrange("(p o) -> p o", o=1), hprev[:, 2 * k + ri:2 * k + ri + 1], reads=["hprev"])
                        h0 = T("h0", [128, 2, 128], F32)
                        hs = T("hs", [128, 2, 128], F32)
                        for ri in range(2):
                            P.dma("sp", h0[:, ri, :], I["ssm0"][l, ri, 2 * hg + k], writes=["h0"])
                        sc = slice(128, 256)
                        P.op("dve", TS(t1[:, 0:128], h0[:, 0, :], arc[k][:], ALU.mult), reads=["h0", "arc"], writes=["t1"])
                        P.op("dve", STT(t1[:, 0:128], h0[:, 1, :], naic[k][:], t1[:, 0:128], ALU.mult, ALU.add), reads=["h0", "naic", "t1"], writes=["t1"])
                        P.op("dve", TT(hs[:, 0, :], t1[:, 0:128], ps_re[:, sc], ALU.add), reads=["t1", kre], writes=["hs"])
                        P.op("dve", TS(t2[:, 0:128], h0[:, 1, :], arc[k][:], ALU.mult), reads=["h0", "arc"], writes=["t2"])
                        P.op("dve", STT(t2[:, 0:128], h0[:, 0, :], aic[k][:], t2[:, 0:128], ALU.mult, ALU.add), reads=["h0", "aic", "t2"], writes=["t2"])
                        P.op("dve", TT(hs[:, 1, :], t2[:, 0:128], ps_im[:, sc], ALU.add), reads=["t2", kim], writes=["hs"])
                        P.op("dve", CP(hre_b[k][:, sc], hs[:, 0, :]), reads=["hs"], writes=[f"hre{k}"])
                        P.op("dve", CP(him_b[k][:, sc], hs[:, 1, :]), reads=["hs"], writes=[f"him{k}"])
                        for ri in range(2):
                            pst, kst = self.bank()
                            P.op("pe", TR(pst[:, 0:128], hs[:, ri, :], self.ident_f[:]), reads=["hs", "ident_f"], writes=[kst])
                            hst = T("hst", [128, 128], F32)
                            P.op("act", ACT(hst[:], pst[:, 0:128], AF.Copy), reads=[kst], writes=["hst"])
                            P.dma("sp", O["ssms_out"][l, ri, 2 * hg + k], hst[:], reads=["hst"])
                if 's5' in DBG_SKIP:
                    if not last and 'carry' not in DBG_SKIP:
                        P.op("dve", CP(kT[:, 0:128], kT[:, n:n + 128]), reads=["kT"], writes=["kT"])
                        P.op("dve", CP(Vtm[:, 0, :], Vtm[:, nblk, :]), reads=["Vtm"], writes=["Vtm"])
                    continue
                ps_y, ky = self.bank()
                fy = []
                for k in range(2):
                    fy.append(MM(ps_y[0:64, 0:n], czre[k][:], hre_b[k][:, 0:n], start=(k == 0), stop=False))
                    fy.append(MM(ps_y[0:64, 0:n], nczim[k][:], him_b[k][:, 0:n], start=False, stop=False))
                fy.append(MM(ps_y[0:64, 0:n], dz_b[64:128, :], VUb[64:128, 0:n], start=False, stop=True))
                P.op("pe", fy, reads=["cz0", "cz1", "hre0", "hre1", "him0", "him1", "dz_b", "VUb"], writes=[ky])
                P.op("act", ACT(z_st[:, 0:n], ps_y[0:64, 0:n], AF.Gelu), reads=[ky], writes=["z_st"])
                P.dma("sp", self.mb_in[128:192, 2 + c0:2 + c0 + n], z_st[:, 0:n], reads=["z_st"])
                ps_gq, kgq = proj(384, 448, 64)
                P.op("act", ACT(qg_f[:, 0:n], ps_gq[0:64, 0:n], AF.Copy, scale=0.125), reads=[kgq], writes=["qg_f"])
                ps_gk, kgk = proj(448, 512, 64)
                P.op("act", ACT(kg_f[:, 0:n], ps_gk[0:64, 0:n], AF.Copy), reads=[kgk], writes=["kg_f"])
                ps_vg, kvg = proj(512, 640, 128)
                P.op("act", ACT(VGb[:, 0:n], ps_vg[:, 0:n], AF.Copy), reads=[kvg], writes=["VGb"])
                if last:
                    P.op("act", ACT(vgf[:, :], ps_vg[0:64, 128:256], AF.Copy), reads=[kvg], writes=["vgf"])
                ps_ac, kac = proj(640, 656, 16)
                P.op("act", ACT(acb[:, 0:n], ps_ac[0:16, 0:n], AF.Copy), reads=[kac], writes=["acb"])
                ps_g, kg_ = self.bank()
                P.op("pe", MM(ps_g[0:64, 0:n], wa2_b[:, :], acb[:, 0:n]), reads=["wa2_b", "acb"], writes=[kg_])
                P.op("act", ACT(lg[:, 0:n], ps_g[0:64, 0:n], AF.Exp, scale=-1.0, bias=nba[:, 0:1]), reads=[kg_, "nba"], writes=["lg"])
                P.op("act", ACT(lg[:, 0:n], lg[:, 0:n], AF.Ln, bias=1.0), reads=["lg"], writes=["lg"])
                P.op("dve", SCAN(cs[:, 0:n], self.resetm[0:64, 0:n], lg[:, 0:n], 0.0), reads=["resetm", "lg"], writes=["cs"])
                P.op("act", ACT(Ep[:, 0:n], cs[:, 0:n], AF.Exp, scale=-1.0 / 16.0), reads=["cs"], writes=["Ep"])
                P.op("act", ACT(Em[:, 0:n], cs[:, 0:n], AF.Exp, scale=1.0 / 16.0), reads=["cs"], writes=["Em"])
                P.op("dve", TT(qin_b[:, 0:n], qg_f[:, 0:n], Ep[:, 0:n], ALU.mult), reads=["qg_f", "Ep"], writes=["qin_b"])
                P.op("dve", TT(kout_f[:, 0:n], kg_f[:, 0:n], Em[:, 0:n], ALU.mult), reads=["kg_f", "Em"], writes=["kout_f"])
                P.op("act", ACT(kout_b[:, 0:n], kout_f[:, 0:n], AF.Copy), reads=["kout_f"], writes=["kout_b"])
                for i in range(nblk):
                    gb = 4 * t + i
                    if gb == NB - 1:
                        continue
                    ch = slice(i * 128, (i + 1) * 128)
                    eend = Ep[:, i * 128 + 127:i * 128 + 128]
                    j = i % 2
                    P.op("dve", TS(kend_b[:, ch], kout_f[:, ch], eend, ALU.mult), reads=["kout_f", "Ep"], writes=["kend_b"])
                    ps_a, ka_ = self.bank()
                    P.op("pe", MM(ps_a[:, 0:128], kout_b[:, ch], qin_b[:, ch]), reads=["kout_b", "qin_b"], writes=[ka_])
                    P.op("dve", TT(attT_b[j][:], ps_a[:, 0:128], self.mask_f[:, 0, 128:256], ALU.mult), reads=[ka_, "mask_f"], writes=[f"attT{j}"])
                    pb1, kb1 = self.bankb()
                    P.op("pe", TR(pb1[:, 0:64], VGb[0:64, ch], self.ident_b[0:64, 0:64]), reads=["VGb", "ident_b"], writes=[kb1])
                    P.op("act", ACT(vtm_b[j][:], pb1[:, 0:64], AF.Copy), reads=[kb1], writes=[f"vtm{j}"])
                    pb2, kb2 = self.bankb()
                    P.op("pe", TR(pb2[:, 0:64], kend_b[:, ch], self.ident_b[0:64, 0:64]), reads=["kend_b", "ident_b"], writes=[kb2])
                    P.op("dve", CP(kend_tm[j][:], pb2[:, 0:64]), reads=[kb2], writes=[f"kendtm{j}"])
                    ps_o, ko_ = self.bank()
                    P.op("pe", [MM(ps_o[0:64, 0:128], vtm_b[j][:], attT_b[j][:], start=True, stop=False),
                                MM(ps_o[0:64, 0:128], S_b[:], qin_b[:, ch], start=False, stop=True)],
                         reads=[f"vtm{j}", f"attT{j}", "S_b", "qin_b"], writes=[ko_])
                    P.op("act", ACT(o_st[:, ch], ps_o[0:64, 0:128], AF.Copy), reads=[ko_], writes=["o_st"])
                    ps_kv, kkv = self.bank()
                    P.op("pe", MM(ps_kv[0:64, 0:64], kend_tm[j][:], vtm_b[j][:]), reads=[f"kendtm{j}", f"vtm{j}"], writes=[kkv])
                    P.op("dve", STT(S_f[:], S_f[:], eend, ps_kv[0:64, 0:64], ALU.mult, ALU.add), reads=["S_f", "Ep", kkv], writes=["S_f"])
                    P.op("act", ACT(S_b[:], S_f[:], AF.Copy), reads=["S_f"], writes=["S_b"])
                P.dma("sp", self.mb_in[192:256, 2 + c0:2 + c0 + npr], o_st[:, 0:npr], reads=["o_st"])
                P.dma("sp", self.mb_in[256:320, 2 + c0:2 + c0 + n], VGb[64:128, 0:n], reads=["VGb"])
                if last:
                    P.dma("sp", O["glap_out"][l, hg], S_f[:], reads=["S_f"])
                    self.p1_samples(es, l, hg, qT, kT, kf, vf, Vtm, sinkn, qg_f, kg_f, lg, vgf)
                else:
                    if 'carry' in DBG_SKIP:
                        continue
                    P.op("dve", CP(kT[:, 0:128], kT[:, n:n + 128]), reads=["kT"], writes=["kT"])
                    P.op("dve", CP(Vtm[:, 0, :], Vtm[:, nblk, :]), reads=["Vtm"], writes=["Vtm"])
            P.barrier()

    def p1_samples(self, es, l, hg, qT, kT, kf, vf, Vtm, sinkn, qg_f, kg_f, lg, vgf):
        P, I, O = self.P, self.I, self.O
        kvh = hg // 2
        T = lambda n, s, d: self.T(es, n, s, d)
        sc = slice(128, 256)
        GS = G - 128
        prod = T("s_prod", [128, 128], F32)
        pself = T("s_pself", [128, 2], F32)
        Dg = T("s_Dg", [128, 2, 128], BF16)
        Ps = T("s_Ps", [128, 256], BF16)
        Kc = [T("s_Kc", [128, 8, 128], BF16) for _ in range(2)]
        Vc = [T("s_Vc", [128, 8, 65], BF16) for _ in range(2)]
        sC = T("s_sC", [65, 256], F32)
        tot = T("s_tot", [65, 256], F32)
        rd = T("s_rd", [64, 256], F32)
        oas = T("s_oas", [64, 256], BF16)
        P.op("dve", TT(prod[:], qT[:, sc], kT[:, 256:384], ALU.mult), reads=["qT", "kT"], writes=["s_prod"])
        ps_ss, kss = self.bank()
        P.op("pe", MM(ps_ss[:, 0:2], prod[:], self.blk2[:]), reads=["s_prod", "blk2"], writes=[kss])
        for hh in range(2):
            P.op("act", ACT(pself[:, hh:hh + 1], ps_ss[:, hh:hh + 1], AF.Exp, scale=0.125, bias=sinkn[:, hh:hh + 1]), reads=[kss, "sinkn"], writes=["s_pself"])
            P.op("dve", TS(Dg[:, hh, :], self.ident_f[:], pself[:, hh:hh + 1], ALU.mult), reads=["ident_f", "s_pself"], writes=["s_Dg"])
        for j in range(2):
            P.op("dve", MEMSET(Vc[j][:, :, 64:65], 1.0), writes=[f"s_Vc{j}"])
        ps_A, kA = self.bank()
        for g8 in range(16):
            j = g8 % 2
            b0 = g8 * 8
            for half in range(2):
                P.dma("sp", Kc[j][64 * half:64 * half + 64], self.Wb["ckT"][l, kvh][:, b0:b0 + 8, :], writes=[f"s_Kc{j}"])
            fns = []
            for bi in range(8):
                b = b0 + bi
                for hh in range(2):
                    pb = 64 * hh
                    fns.append(MM(ps_A[:, hh * 128 + b:hh * 128 + b + 1], Kc[j][pb:pb + 64, bi, :], qT[pb:pb + 64, 128 + b:129 + b]))
            P.op("pe", fns, reads=[f"s_Kc{j}", "qT"], writes=[kA])
        for hh in range(2):
            P.op("act", ACT(Ps[:, hh * 128:(hh + 1) * 128], ps_A[:, hh * 128:(hh + 1) * 128], AF.Exp, scale=0.125, bias=sinkn[:, hh:hh + 1]), reads=[kA, "sinkn"], writes=["s_Ps"])
        ps_B, kB = self.bank()
        for g8 in range(16):
            j = g8 % 2
            b0 = g8 * 8
            P.dma("sp", Vc[j][:, :, 0:64], self.Wb["cvS"][l, kvh][:, b0:b0 + 8, :], writes=[f"s_Vc{j}"])
            fns = [MM(ps_B[0:65, b0 + bi:256:128], Vc[j][:, bi, :], Ps[:, b0 + bi:256:128]) for bi in range(8)]
            P.op("pe", fns, reads=[f"s_Vc{j}", "s_Ps"], writes=[kB])
        ps_C, kC = self.bank()
        P.op("pe", [MM(ps_C[0:65, hh * 128:(hh + 1) * 128], Vtm[:, 2, :], Dg[:, hh, :]) for hh in range(2)], reads=["Vtm", "s_Dg"], writes=[kC])
        P.op("act", ACT(sC[:], ps_C[0:65, 0:256], AF.Copy), reads=[kC], writes=["s_sC"])
        P.op("dve", TT(tot[:], ps_B[0:65, 0:256], sC[:], ALU.add), reads=[kB, "s_sC"], writes=["s_tot"])
        ps_D, kD = self.bank()
        P.op("pe", MM(ps_D[0:64, 0:256], self.sel65[0:65, :], tot[:]), reads=["sel65", "s_tot"], writes=[kD])
        P.op("dve", TS(rd[:], ps_D[0:64, 0:256], 1.0, ALU.add), reads=[kD], writes=["s_rd"])
        P.op("dve", RECIP(rd[:], rd[:]), reads=["s_rd"], writes=["s_rd"])
        P.op("dve", TT(oas[:], tot[0:64, :], rd[:], ALU.mult), reads=["s_tot", "s_rd"], writes=["s_oas"])
        for hh in range(2):
            P.dma("sp", self.mb_in[64 * hh:64 * hh + 64, 2 + GS:2 + G], oas[:, hh * 128:(hh + 1) * 128], reads=["s_oas"])
        if hg % 2 == 0:
            ktm = T("s_ktm", [128, 4, 64], F32)
            for vi, (src, skey, nm) in enumerate(((kf, "kf", "k"), (vf, "vf", "v"))):
                for i in range(2):
                    pst, kst = self.bank()
                    P.op("pe", TR(pst[:, 0:64], src[0:64, i * 128:(i + 1) * 128], self.ident_f[0:64, 0:64]), reads=[skey, "ident_f"], writes=[kst])
                    P.op("act", ACT(ktm[:, 2 * vi + i, :], pst[:, 0:64], AF.Copy), reads=[kst], writes=["s_ktm"])
                P.dma("sp", O[f"{nm}p_out"][l, kvh], ktm[:, 2 * vi, :], reads=["s_ktm"])
                P.dma("sp", O[f"{nm}s_out"][l, kvh, :, 127, :], ktm[:, 2 * vi + 1, :], reads=["s_ktm"])
                P.dma("sp", O[f"{nm}s_out"][l, kvh, :, 0:127, :], I["ckN" if nm == "k" else "cvN"][l, kvh, :, 1:128, :])
        Eps = T("g_Eps", [64, 128], F32)
        qins = T("g_qins", [64, 128], BF16)
        prodg = T("g_prodg", [64, 128], F32)
        vts = T("g_vts", [128, 64], F32)
        S0f = [T("g_S0f", [64, 16, 64], F32) for _ in range(2)]
        S0b = [T("g_S0b", [64, 16, 64], BF16) for _ in range(2)]
        Vrep = [T("g_Vrep", [64, 16, 64], F32) for _ in range(2)]
        Snew = [T("g_Snew", [64, 16, 64], F32) for _ in range(2)]
        tkv = T("g_tkv", [64, 64], F32)
        osb = T("g_osb", [64, 128], BF16)
        otmp = T("g_otmp", [64, 128], F32)
        P.op("act", ACT(Eps[:], lg[:, sc], AF.Exp, scale=-1.0 / 16.0), reads=["lg"], writes=["g_Eps"])
        P.op("dve", TT(qins[:], qg_f[:, sc], Eps[:], ALU.mult), reads=["qg_f", "g_Eps"], writes=["g_qins"])
        P.op("dve", TT(prodg[:], qg_f[:, sc], kg_f[:, sc], ALU.mult), reads=["qg_f", "kg_f"], writes=["g_prodg"])
        ps_qk, kqk = self.bank()
        P.op("pe", MM(ps_qk[0:64, 0:128], self.ones_f[0:64, 0:64], prodg[:]), reads=["ones_f", "g_prodg"], writes=[kqk])
        pst, kst = self.bank()
        P.op("pe", TR(pst[:, 0:64], vgf[:, :], self.ident_f[0:64, 0:64]), reads=["vgf", "ident_f"], writes=[kst])
        P.op("act", ACT(vts[:], pst[:, 0:64], AF.Copy), reads=[kst], writes=["g_vts"])
        P.dma("sp", self.vs_scr[:, :], vts[:], reads=["g_vts"], writes=["vs_scr"])
        ps_os, kos = self.bank()
        NQ = 16
        for qd in range(128 // NQ):
            j = qd % 2
            bq = slice(qd * NQ, qd * NQ + NQ)
            P.dma("sp", S0f[j][:], I["gst"][l, hg][:, bq, :], writes=[f"g_S0f{j}"])
            P.dma("sp", Vrep[j][:].rearrange("p b e -> p (b e)"),
                  self.vs_scr.rearrange("(o b) e -> o (b e)", o=128 // NQ)[qd:qd + 1, :].partition_broadcast(64),
                  reads=["vs_scr"], writes=[f"g_Vrep{j}"])
            P.op("act", ACT(S0b[j][:], S0f[j][:], AF.Copy), reads=[f"g_S0f{j}"], writes=[f"g_S0b{j}"])
            P.op("pe", [MM(ps_os[0:64, qd * NQ + b:qd * NQ + b + 1], S0b[j][:, b, :], qins[:, qd * NQ + b:qd * NQ + b + 1]) for b in range(NQ)],
                 reads=[f"g_S0b{j}", "g_qins"], writes=[kos])
            for b in range(NQ):
                col = 128 + qd * NQ + b
                P.op("dve", TS(tkv[:], Vrep[j][:, b, :], kg_f[:, col:col + 1], ALU.mult), reads=[f"g_Vrep{j}", "kg_f"], writes=["g_tkv"])
                P.op("dve", STT(Snew[j][:, b, :], S0f[j][:, b, :], Eps[:, qd * NQ + b:qd * NQ + b + 1], tkv[:], ALU.mult, ALU.add),
                     reads=[f"g_S0f{j}", "g_Eps", "g_tkv"], writes=[f"g_Snew{j}"])
            P.dma("sp", O["glas_out"][l, hg][:, bq, :], Snew[j][:], reads=[f"g_Snew{j}"])
        P.op("dve", TT(otmp[:], vgf[:, :], ps_qk[0:64, 0:128], ALU.mult), reads=["vgf", kqk], writes=["g_otmp"])
        P.op("dve", TT(osb[:], otmp[:], ps_os[0:64, 0:128], ALU.add), reads=["g_otmp", kos], writes=["g_osb"])
        P.dma("sp", self.mb_in[192:256, 2 + GS:2 + G], osb[:], reads=["g_osb"])

    @staticmethod
    def blocks(total, off=0):
        nb = (total + 511) // 512
        base, rem = total // nb, total % nb
        out, a = [], off
        for i in range(nb):
            w = base + (1 if i < rem else 0)
            out.append((a, a + w))
            a += w
        return out

    def p2(self, l, last):
        P, I, O = self.P, self.I, self.O
        n = CPC + 2
        blksH = self.blocks(n)
        blksO = self.blocks(CPC)
        pid = self.nc.sync.partition_id()
        P.dma("sp", self.mb_loc[:, :], self.mbufs[l][:, bass.ds(pid * CPC, n)], reads=[f"mbuf{l}"], writes=["mb_loc"])
        mb = self.mb_loc.rearrange("(r q) c -> r q c", r=NHG)
        mkey = "mb_loc"
        with ExitStack() as es:
            T = lambda nm, sh, d: self.T(es, nm, sh, d)
            xr = T("xr", [128, 16, n], F32)
            hb = T("halo_b", [128, 16, 2], BF16)
            P.dma("sp", xr[:, :, 2:n], chunked(self.xres_loc), reads=["xres_loc"], writes=["xr"])
            xg2 = self.xg[l].rearrange("(rk p) n -> p rk n", p=128)
            P.dma("sp", hb[:], xg2[:, bass.ds(((pid + 7) % 8) * 16, 16), CPC - 2:CPC], reads=[f"xg{l}"], writes=["halo_b"])
            P.op("act", ACT(xr[:, :, 0:2], hb[:], AF.Copy), reads=["halo_b"], writes=["xr"])
            gb1 = T("gb1", [128, 2, 16], F32)
            gb2 = T("gb2", [128, 2, 16], F32)
            cw = T("cw", [128, 4, NFC], F32)
            bgl = T("bgl", [128, 4], F32)
            gno = T("gno", [128, 4], F32)
            P.dma("sp", gb1[:], I["ln1"][l].rearrange("t p k -> p t k"), writes=["gb1"])
            P.dma("sp", gb2[:], I["ln2"][l].rearrange("t p k -> p t k"), writes=["gb2"])
            P.dma("sp", cw[:], I["convw"][l].rearrange("t p k -> p t k"), writes=["cw"])
            P.dma("sp", bgl[:], I["bglu"][l], writes=["bgl"])
            P.dma("sp", gno[:], I["gnorm"][l], writes=["gno"])
            P.dma("sp", O["convs_out"][l, 0], I["cprevT"][l, 1])
            with ExitStack() as esa:
                Ta = lambda nm, sh, d: self.T(esa, nm, sh, d)
                oaT = Ta("oaT", [128, 8, n], BF16)
                zT = Ta("zT", [128, 4, n], BF16)
                oT = Ta("oT", [128, 4, n], BF16)
                gT = Ta("gT", [128, 4, n], BF16)
                obT = Ta("obT", [128, 4, n], BF16)
                wgl = Ta("wgl", [128, 4, 512], BF16)
                tA = Ta("tA", [128, 512], F32)
                tB = Ta("tB", [128, 512], F32)
                tCb = Ta("tCb", [128, 512], BF16)
                wo = [Ta("wo", [128, 16, 128], BF16) for _ in range(2)]
                cols = slice(0, n)
                for r in range(NHG):
                    hp = slice(64 * (r % 2), 64 * (r % 2) + 64)
                    P.dma("sp", oaT[:, r, :], mb[r, 0:128, cols], reads=[mkey], writes=["oaT"])
                    P.dma("sp", zT[hp, r // 2, :], mb[r, 128:192, cols], reads=[mkey], writes=["zT"])
                    P.dma("sp", oT[hp, r // 2, :], mb[r, 192:256, cols], reads=[mkey], writes=["oT"])
                    P.dma("sp", gT[hp, r // 2, :], mb[r, 256:320, cols], reads=[mkey], writes=["gT"])
                P.dma("sp", wgl[:], chunked(self.Wb["wglu"][l]), writes=["wgl"])
                for oc in range(4):
                    for (b0, b1) in blksH:
                        nb = b1 - b0
                        ps, kp = self.bank()
                        P.op("pe", [MM(ps[:, 0:nb], wgl[:, k, oc * 128:(oc + 1) * 128], zT[:, k, b0:b1], start=(k == 0), stop=(k == 3)) for k in range(4)],
                             reads=["wgl", "zT"], writes=[kp])
                        P.op("act", ACT(tA[:, 0:nb], ps[:, 0:nb], AF.Sigmoid, bias=bgl[:, oc:oc + 1]), reads=[kp, "bgl"], writes=["tA"])
                        P.op("dve", TT(obT[:, oc, b0:b1], zT[:, oc, b0:b1], tA[:, 0:nb], ALU.mult), reads=["zT", "tA"], writes=["obT"])
                for h in range(4):
                    for (b0, b1) in blksH:
                        nb = b1 - b0
                        P.op("act", ACT(tCb[:, 0:nb], oT[:, h, b0:b1], AF.Square), reads=["oT"], writes=["tCb"])
                        ps, kp = self.bank()
                        P.op("pe", MM(ps[:, 0:nb], self.ones_b[:], tCb[:, 0:nb]), reads=["ones_b", "tCb"], writes=[kp])
                        P.op("act", ACT(tA[:, 0:nb], ps[:, 0:nb], AF.Sqrt, scale=1.0 / 128.0, bias=EPS), reads=[kp], writes=["tA"])
                        P.op("dve", RECIP(tA[:, 0:nb], tA[:, 0:nb]), reads=["tA"], writes=["tA"])
                        P.op("dve", TT(tA[:, 0:nb], oT[:, h, b0:b1], tA[:, 0:nb], ALU.mult), reads=["oT", "tA"], writes=["tA"])
                        P.op("act", ACT(tB[:, 0:nb], gT[:, h, b0:b1], AF.Silu), reads=["gT"], writes=["tB"])
                        P.op("dve", STT(oT[:, h, b0:b1], tA[:, 0:nb], gno[:, h:h + 1], tB[:, 0:nb], ALU.mult, ALU.mult), reads=["tA", "gno", "tB"], writes=["oT"])
                for oc in range(16):
                    w = wo[oc % 2]
                    wk = f"wo{oc % 2}"
                    P.dma("sp", w[:], chunked(self.Wb["wout"][l][:, oc * 128:(oc + 1) * 128]), writes=[wk])
                    for (b0, b1) in blksH:
                        nb = b1 - b0
                        ps, kp = self.bank()
                        fns = []
                        for k in range(16):
                            rhs = oaT[:, k, b0:b1] if k < 8 else (obT[:, k - 8, b0:b1] if k < 12 else oT[:, k - 12, b0:b1])
                            fns.append(MM(ps[:, 0:nb], w[:, k, :], rhs, start=(k == 0), stop=(k == 15)))
                        P.op("pe", fns, reads=[wk, "oaT", "obT", "oT"], writes=[kp])
                        P.op("dve", STT(xr[:, oc, b0:b1], xr[:, oc, b0:b1], ALPHA, ps[:, 0:nb], ALU.mult, ALU.add), reads=["xr", kp], writes=["xr"])
                P.barrier()
            x1b = T("x1b", [128, 16, n], BF16)
            with ExitStack() as esl:
                self.layernorm(esl, xr, "xr", n, gb1, "gb1", x1b, "x1b", mask=self.cmask[:, 0:n])
                P.barrier()
            for k in range(16):
                P.op("act", ACT(xr[:, k, :], xr[:, k, :], AF.Copy, scale=ALPHA), reads=["xr"], writes=["xr"])
            with ExitStack() as esf:
                Tf = lambda nm, sh, d: self.T(esf, nm, sh, d)
                hT = Tf("hT", [128, 11, CPC], BF16)
                U = [Tf("U", [128, n], F32) for _ in range(2)]
                cc = [Tf("cc", [128, CPC], F32) for _ in range(2)]
                e01 = [Tf("e01", [128, 2, 128], F32) for _ in range(2)]
                wu = [Tf("wu", [128, 16, 128], BF16) for _ in range(2)]
                wg = [Tf("wg", [128, 16, 128], BF16) for _ in range(2)]
                wd = [Tf("wd", [128, 11, 128], BF16) for _ in range(2)]
                cpv = [Tf("cpv", [128, 2, 128], F32) for _ in range(2)]
                npz = CPC - 128
                sc = slice(npz, CPC)
                for grp in range(4):
                    for fi in range(11):
                        f = grp * 11 + fi
                        j = f % 2
                        P.dma("sp", wu[j][:], chunked(self.Wb["wup"][l][:, f * 128:(f + 1) * 128]), writes=[f"wu{j}"])
                        P.dma("sp", wg[j][:], chunked(self.Wb["wup"][l][:, DFF + f * 128:DFF + (f + 1) * 128]), writes=[f"wg{j}"])
                        P.dma("sp", cpv[j][:], I["cprevT"][l][:, f * 128:(f + 1) * 128, :].rearrange("t p b -> p t b"), writes=[f"cpv{j}"])
                        Uj, uk = U[j], f"U{j}"
                        cj, ck_ = cc[j], f"cc{j}"
                        for (b0, b1) in blksH:
                            nb = b1 - b0
                            ps, kp = self.bank()
                            P.op("pe", [MM(ps[:, 0:nb], wu[j][:, k, :], x1b[:, k, b0:b1], start=(k == 0), stop=(k == 15)) for k in range(16)],
                                 reads=[f"wu{j}", "x1b"], writes=[kp])
                            P.op("act", ACT(Uj[:, b0:b1], ps[:, 0:nb], AF.Copy), reads=[kp], writes=[uk])
                        gps = []
                        for (b0, b1) in blksO:
                            nb = b1 - b0
                            ps2, kp2 = self.bank()
                            P.op("pe", [MM(ps2[:, 0:nb], wg[j][:, k, :], x1b[:, k, 2 + b0:2 + b1], start=(k == 0), stop=(k == 15)) for k in range(16)],
                                 reads=[f"wg{j}", "x1b"], writes=[kp2])
                            gps.append((ps2, kp2))
                        w0, w1, w2, cb = (cw[:, t_, f:f + 1] for t_ in range(4))
                        P.op("dve", TS(cj[:, 0:CPC], Uj[:, 0:CPC], w0, ALU.mult, cb, ALU.add), reads=[uk, "cw"], writes=[ck_])
                        P.op("dve", STT(cj[:, 0:CPC], Uj[:, 1:CPC + 1], w1, cj[:, 0:CPC], ALU.mult, ALU.add), reads=[uk, "cw", ck_], writes=[ck_])
                        P.op("dve", STT(cj[:, 0:CPC], Uj[:, 2:CPC + 2], w2, cj[:, 0:CPC], ALU.mult, ALU.add), reads=[uk, "cw", ck_], writes=[ck_])
                        ej, ek_ = e01[j], f"e01{j}"
                        P.op("dve", STT(ej[:, 0, :], Uj[:, npz:npz + 128], self.nsm[:, 0:1], cpv[j][:, 0, :], ALU.mult, ALU.add), reads=[uk, "nsm", f"cpv{j}"], writes=[ek_])
                        P.op("dve", STT(ej[:, 1, :], Uj[:, npz + 1:npz + 129], self.nsm[:, 0:1], cpv[j][:, 1, :], ALU.mult, ALU.add), reads=[uk, "nsm", f"cpv{j}"], writes=[ek_])
                        P.op("dve", TS(cj[:, sc], ej[:, 0, :], w0, ALU.mult, cb, ALU.add), reads=[ek_, "cw", ck_], writes=[ck_])
                        P.op("dve", STT(cj[:, sc], ej[:, 1, :], w1, cj[:, sc], ALU.mult, ALU.add), reads=[ek_, "cw", ck_], writes=[ck_])
                        P.op("dve", STT(cj[:, sc], Uj[:, 2 + npz:2 + CPC], w2, cj[:, sc], ALU.mult, ALU.add), reads=[uk, "cw", ck_], writes=[ck_])
                        P.dma("sp", O["convp_out"][l, :, f, :], Uj[:, npz:npz + 2], reads=[uk])
                        P.dma("sp", O["convs_out"][l, 1, f * 128:(f + 1) * 128, :], Uj[:, 2 + npz:2 + CPC], reads=[uk])
                        P.op("act", ACT(cj[:, 0:CPC], cj[:, 0:CPC], AF.Gelu), reads=[ck_], writes=[ck_])
                        for bi, (b0, b1) in enumerate(blksO):
                            ps2, kp2 = gps[bi]
                            P.op("dve", TT(hT[:, fi, b0:b1], cj[:, b0:b1], ps2[:, 0:b1 - b0], ALU.mult), reads=[ck_, kp2], writes=["hT"])
                    for oc in range(16):
                        j = oc % 2
                        P.dma("sp", wd[j][:], chunked(self.Wb["wdown"][l][grp * 1408:(grp + 1) * 1408, oc * 128:(oc + 1) * 128]), writes=[f"wd{j}"])
                        for (b0, b1) in blksO:
                            nb = b1 - b0
                            ps, kp = self.bank()
                            P.op("pe", [MM(ps[:, 0:nb], wd[j][:, fi, :], hT[:, fi, b0:b1], start=(fi == 0), stop=(fi == 10)) for fi in range(11)],
                                 reads=[f"wd{j}", "hT"], writes=[kp])
                            P.op("dve", TT(xr[:, oc, 2 + b0:2 + b1], xr[:, oc, 2 + b0:2 + b1], ps[:, 0:nb], ALU.add), reads=["xr", kp], writes=["xr"])
                P.barrier()
            xo = xr[:, :, 2:n]
            with ExitStack() as eso:
                To = lambda nm, sh, d: self.T(eso, nm, sh, d)
                if not last:
                    self.layernorm(eso, xo, "xr", CPC, gb2, "gb2", x1b, "x1b", mask=self.cmask[:, 2:n])
                    P.dma("sp", chunked(self.xres_loc), xo, reads=["xr"], writes=["xres_loc"])
                    P.dma("sp", chunked(self.xn_in), x1b[:, :, 0:CPC], reads=["x1b"], writes=["xn_in"])
                    P.barrier()
                    P.collective("AllGather", self.xn_in, self.xg[l + 1], reads=["xn_in"], writes=[f"xg{l + 1}"])
                else:
                    self.layernorm(eso, xo, "xr", CPC, gb2, "gb2")
                    yt = [To("yt", [128, D], F32) for _ in range(2)]
                    nfull = CPC // 128
                    for jb in range((CPC + 127) // 128):
                        w = min(128, CPC - jb * 128)
                        y_, yk = yt[jb % 2], f"yt{jb % 2}"
                        for kg in range(4):
                            ps, kp = self.bank()
                            for kk in range(4):
                                P.op("pe", TR(ps[0:w, kk * 128:(kk + 1) * 128], xo[:, 4 * kg + kk, jb * 128:jb * 128 + w], self.ident_f[:]),
                                     reads=["xr", "ident_f"], writes=[kp])
                            P.op("act", ACT(y_[0:w, kg * 512:(kg + 1) * 512], ps[0:w, :], AF.Copy), reads=[kp], writes=[yk])
                        P.dma("sp", O["y_out"][jb * 128:jb * 128 + w, :], y_[0:w, :], reads=[yk])
                P.barrier()


SHARED = ["ln_in", "rope", "pm", "swamask", "ident", "wout", "wup", "wdown", "wglu", "ln1", "ln2", "convw", "bglu", "gnorm"]


def _percore(d, c):
    f = np.ascontiguousarray
    e = {k: d[k] for k in SHARED}
    e["xT"] = f(d["xT"][:, c * CPC:(c + 1) * CPC])
    for k in ("win_all", "sink", "s5col", "s5row", "s5bz", "s5cz", "s5dz", "wa2", "ba", "gst"):
        e[k] = f(d[k][:, c:c + 1])
    for k in ("ckT", "cvS", "ckN", "cvN"):
        e[k] = f(d[k][:, c // 2:c // 2 + 1])
    e["ssm0"] = f(d["ssm0"][:, :, 2 * c:2 * c + 2])
    e["cprevT"] = d["cprevT"] if c == NCORES - 1 else np.zeros_like(d["cprevT"])
    cm = np.zeros((1, CPC + 2), np.float32)
    g = c * CPC - 2 + np.arange(CPC + 2)
    cm[0, g >= NPAD] = 1.0
    e["cmask"] = cm
    e["nsm"] = np.full((128, 1), 0.0 if c == NCORES - 1 else 1.0, np.float32)
    return e


def _assemble(rs, inp):
    B = 128
    cat = lambda k, ax: np.concatenate([r[k] for r in rs], axis=ax)
    y = cat("y_out", 0)
    y_prompt = y[128:G - 128][None]
    y_sample = y[G - 128:][:, None, :]
    ev = [rs[c] for c in range(0, NCORES, 2)]
    kp = np.concatenate([r["kp_out"] for r in ev], 1).transpose(0, 2, 1, 3)[:, None]
    vp = np.concatenate([r["vp_out"] for r in ev], 1).transpose(0, 2, 1, 3)[:, None]
    ks = np.concatenate([r["ks_out"] for r in ev], 1).transpose(0, 2, 3, 1, 4)
    vs = np.concatenate([r["vs_out"] for r in ev], 1).transpose(0, 2, 3, 1, 4)
    sp = cat("ssmp_out", 2).reshape(L, 2, 16, 2, 64).reshape(L, 2, 32, 64)
    ssm_re_p, ssm_im_p = sp[:, 0][:, None], sp[:, 1][:, None]
    ss = cat("ssms_out", 2).transpose(0, 1, 3, 2, 4).reshape(L, 2, B, 32, 64)
    ssm_re_s, ssm_im_s = ss[:, 0], ss[:, 1]
    gp = cat("glap_out", 1).reshape(L, 4, 2, 64, 64).transpose(0, 1, 3, 2, 4).reshape(L, 4, 64, 128)[:, None]
    gs = cat("glas_out", 1).reshape(L, 4, 2, 64, B, 64).transpose(0, 4, 1, 3, 2, 5).reshape(L, B, 4, 64, 128)
    rl = rs[NCORES - 1]
    conv_p = rl["convp_out"].transpose(0, 3, 2, 1).reshape(L, 2, DFF)[:, None]
    conv_s = rl["convs_out"].transpose(0, 3, 1, 2)
    outs = (y_prompt, y_sample, kp, vp, ssm_re_p, ssm_im_p, gp, conv_p, ks, vs, ssm_re_s, ssm_im_s, gs, conv_s)
    return tuple(np.ascontiguousarray(o, dtype=np.float32) for o in outs)


def kernel(**inp):
    d = _prep_inputs(inp)
    in_maps = [_percore(d, c) for c in range(NCORES)]
    shapes = {k: v.shape for k, v in in_maps[0].items()}
    b = Builder(shapes)
    nc = b.build()
    res = run_bass_kernel_spmd(nc, in_maps, core_ids=list(range(NCORES)))
    return _assemble(res.results, inp)
```

```python
import math
from contextlib import ExitStack
import numpy as np
import ml_dtypes
import concourse.bass as bass
import concourse.mybir as mybir
from concourse.bass_utils import run_bass_kernel_spmd

F32 = mybir.dt.float32
BF16 = mybir.dt.bfloat16
I32 = mybir.dt.int32
AF = mybir.ActivationFunctionType
ALU = mybir.AluOpType

D = 2048
L = 2
NB = 66
G = NB * 128
NPAD = 112
NCORES = 8
DFF = 5632
NFC = DFF // 128
NHG = 8
NWC = 656
W1 = 512
W2 = 1024
CPC = G // NCORES
ALPHA = float((2 * L) ** 0.25)
EPS = 1e-5
TWO_PI = 2.0 * math.pi
C1 = 6.28125
C2 = TWO_PI - C1
BROWS = 320
import os
DBG_SKIP = set(os.environ.get('DBG_SKIP', '').split(','))


class Prog:
    EPOCH = 4000
    NSLOT = 8
    DMA_EPOCH = 1800

    def __init__(self, nc, es, same_engine_sync=True):
        self.nc = nc
        self.es = es
        self.eng = {"pe": nc.tensor, "act": nc.scalar, "dve": nc.vector,
                    "pool": nc.gpsimd, "sp": nc.sync}
        self.cnt = {e: 0 for e in self.eng}
        self.csem = {}
        self.dsem = {}
        self.dcount = {}
        self.dgen = {}
        self.dslot_next = {e: 0 for e in self.eng}
        self.waited = {e: {} for e in self.eng}
        self.lastw = {}
        self.readers = {}
        self.same_engine_sync = same_engine_sync
        self.live_dma = {}

    def _newsem(self, name):
        return self.es.enter_context(self.nc.semaphore(name))

    def _csem(self, e, ep):
        k = (e, ep)
        if k not in self.csem:
            self.csem[k] = self._newsem(f"c_{e}_{ep}")
        return self.csem[k]

    def _emit_wait(self, x, tok):
        w = self.waited[x]
        if tok[0] == "x":
            if w.get(tok, 0):
                return
            w[tok] = 1
            self.eng[x].wait_ge(self.xsem[tok[1]], 1)
            return
        if tok[0] == "c":
            _, e, n = tok
            if e == x and (e == "pe" or not self.same_engine_sync):
                return
            key = ("c", e)
            if w.get(key, 0) >= n:
                return
            w[key] = n
            idx = n - 1
            self.eng[x].wait_ge(self._csem(e, idx // self.EPOCH), idx % self.EPOCH + 1)
        else:
            _, q, slot, gen, k = tok
            key = ("d", q, slot, gen)
            if w.get(key, 0) >= k:
                return
            w[key] = k
            self.eng[x].wait_ge(self.dsem[(q, slot, gen)], 16 * k)

    def _deps(self, reads, writes):
        deps = []
        for r in reads:
            if r in self.lastw:
                deps.append(self.lastw[r])
        for wv in writes:
            if wv in self.lastw:
                deps.append(self.lastw[wv])
            deps.extend(self.readers.get(wv, {}).values())
        return deps

    def _commit(self, tok, reads, writes):
        for r in reads:
            d = self.readers.setdefault(r, {})
            if tok[0] == "c":
                d[("c", tok[1])] = tok
            elif tok[0] == "x":
                d[tok] = tok
            else:
                d[("d", tok[1], tok[2], tok[3])] = tok
        for wv in writes:
            self.lastw[wv] = tok
            self.readers[wv] = {}

    @staticmethod
    def _excl(reads, writes):
        ps = [r for r in reads if isinstance(r, str) and r.startswith("ps")]
        if not ps:
            return reads, writes
        return [r for r in reads if r not in ps], list(writes) + ps

    def op(self, e, fn, reads=(), writes=()):
        reads, writes = self._excl(reads, writes)
        for t in self._deps(reads, writes):
            self._emit_wait(e, t)
        self.cnt[e] += 1
        n = self.cnt[e]
        idx = n - 1
        fns = fn if isinstance(fn, (list, tuple)) else [fn]
        ins = None
        for f in fns:
            ins = f(self.eng[e])
        ins.then_inc(self._csem(e, idx // self.EPOCH), 1)
        tok = ("c", e, n)
        self._commit(tok, reads, writes)
        return tok

    def dma(self, q, out, in_, reads=(), writes=(), after=()):
        slot = self.dslot_next[q]
        self.dslot_next[q] = (slot + 1) % self.NSLOT
        gen = self.dgen.get((q, slot), 0)
        k = self.dcount.get((q, slot, gen), 0)
        if k > 0:
            self._emit_wait(q, ("d", q, slot, gen, k))
        if k >= self.DMA_EPOCH:
            gen += 1
            self.dgen[(q, slot)] = gen
            k = 0
        if (q, slot, gen) not in self.dsem:
            self.dsem[(q, slot, gen)] = self._newsem(f"d_{q}_{slot}_{gen}")
        for t in self._deps(reads, writes):
            self._emit_wait(q, t)
        for t in after:
            self._emit_wait(q, t)
        k += 1
        self.dcount[(q, slot, gen)] = k
        self.eng[q].dma_start(out=out, in_=in_).then_inc(self.dsem[(q, slot, gen)], 16)
        tok = ("d", q, slot, gen, k)
        self._commit(tok, reads, writes)
        self.live_dma[(q, slot)] = tok
        return tok

    def collective(self, kind, in_ap, out_ap, reads=(), writes=()):
        e = "pool"
        for t in self._deps(reads, writes):
            self._emit_wait(e, t)
        self.ncoll = getattr(self, "ncoll", 0) + 1
        if not hasattr(self, "xsem"):
            self.xsem = {}
        sem = self._newsem(f"coll{self.ncoll}")
        self.xsem[self.ncoll] = sem
        self.nc.gpsimd.collective_compute(kind, ALU.bypass, replica_groups=[list(range(NCORES))],
                                          ins=[in_ap.opt()], outs=[out_ap.opt()]).then_inc(sem, 1)
        tok = ("x", self.ncoll)
        self._commit(tok, reads, writes)
        return tok

    def barrier(self):
        for x in self.eng:
            for (q_, _s), t in self.live_dma.items():
                if q_ == "pool" and getattr(self, "bg_pool", False):
                    continue
                self._emit_wait(x, t)
            for ci in range(1, getattr(self, "ncoll", 0) + 1):
                self._emit_wait(x, ("x", ci))
            for y in self.eng:
                if y != x and self.cnt[y] > 0:
                    self._emit_wait(x, ("c", y, self.cnt[y]))
        for x in ("act", "dve", "pool"):
            if self.cnt[x] > 0 and self.same_engine_sync:
                self._emit_wait(x, ("c", x, self.cnt[x]))

    def finish(self, e="sp"):
        for t in self.live_dma.values():
            self._emit_wait(e, t)
        for x in self.eng:
            if self.cnt[x] > 0 and x != e:
                self._emit_wait(e, ("c", x, self.cnt[x]))


def MM(out, lhsT, rhs, start=True, stop=True):
    return lambda e: e.matmul(out, lhsT=lhsT, rhs=rhs, start=start, stop=stop)


def TR(out, in_, ident):
    return lambda e: e.transpose(out, in_, ident)


def ACT(out, in_, func, **kw):
    return lambda e: e.activation(out=out, in_=in_, func=func, **kw)


def TT(out, a, b, op):
    return lambda e: e.tensor_tensor(out=out, in0=a, in1=b, op=op)


def TS(out, a, s1, op0, s2=None, op1=None):
    if op1 is None:
        return lambda e: e.tensor_scalar(out=out, in0=a, scalar1=s1, scalar2=None, op0=op0)
    return lambda e: e.tensor_scalar(out=out, in0=a, scalar1=s1, scalar2=s2, op0=op0, op1=op1)


def STT(out, a, s, b, op0, op1):
    return lambda e: e.scalar_tensor_tensor(out=out, in0=a, scalar=s, in1=b, op0=op0, op1=op1)


def CP(out, in_):
    return lambda e: e.tensor_copy(out, in_)


def SCAN(out, d0, d1, init):
    return lambda e: e.tensor_tensor_scan(out=out, data0=d0, data1=d1, initial=init,
                                          op0=ALU.mult, op1=ALU.add)


def RECIP(out, in_):
    return lambda e: e.reciprocal(out, in_)


def MEMSET(ap, v):
    return lambda e: e.memset(ap, v)


def _rope_tables():
    pos = np.zeros(G, np.float32)
    pos[NPAD:G - 128] = np.arange(G - 128 - NPAD, dtype=np.float32)
    pos[G - 128:] = 8192.0
    half = 8
    inv = np.power(np.float32(500000.0), -np.arange(half, dtype=np.float32) / np.float32(half)).astype(np.float32)
    ang = pos[None, :] * inv[:, None]
    cos8, sin8 = np.cos(ang).astype(np.float32), np.sin(ang).astype(np.float32)
    tab = np.zeros((2, 128, G), np.float32)
    tab[0] = 1.0
    for base in (0, 64):
        tab[0, base:base + 8] = cos8
        tab[0, base + 8:base + 16] = cos8
        tab[1, base:base + 8] = sin8
        tab[1, base + 8:base + 16] = sin8
    pm = np.zeros((128, 128), np.float32)
    for base in (0, 64):
        for d in range(8):
            pm[base + d + 8, base + d] = -1.0
            pm[base + d, base + d + 8] = 1.0
    return tab, pm


def _swa_masks():
    s = np.arange(128)[:, None]
    q = np.arange(128)[None, :]
    m = np.zeros((3, 128, 2, 128), np.float32)
    prev = (s >= q).astype(np.float32)
    cur = (s <= q).astype(np.float32)
    real = (s >= NPAD).astype(np.float32)
    m[0, :, 0], m[0, :, 1] = prev, cur
    m[1, :, 0], m[1, :, 1] = 0.0, cur * real
    m[2, :, 0], m[2, :, 1] = prev * real, cur
    return np.ascontiguousarray(m.transpose(1, 0, 2, 3).reshape(128, 3, 256))


def _prep_inputs(inp):
    f = lambda a: np.ascontiguousarray(np.asarray(a, dtype=np.float32))
    d = {}
    xall = np.zeros((G, D), np.float32)
    xall[NPAD:128] = inp["meta_tokens"]
    xall[128:G - 128] = inp["x_prompt"][0]
    xall[G - 128:] = inp["x_sample"][:, 0]
    d["xT"] = f(xall.T)
    vec16 = lambda v: f(np.asarray(v).reshape(16, 128).T)
    d["ln_in"] = f(np.stack([vec16(inp["ln_in_g"]), vec16(inp["ln_in_b"])]))
    w_in = np.asarray(inp["w_in"])
    win_all = np.zeros((L, NHG, D, NWC), np.float32)
    for hg in range(NHG):
        kvh, gh, half = hg // 2, hg // 2, hg % 2
        cols = np.concatenate([
            np.arange(128 * hg, 128 * hg + 128),
            np.arange(1024 + 64 * kvh, 1024 + 64 * kvh + 64),
            np.arange(1024 + 64 * kvh, 1024 + 64 * kvh + 64),
            np.arange(1280 + 64 * kvh, 1280 + 64 * kvh + 64),
            np.arange(1536 + 64 * hg, 1536 + 64 * hg + 64),
            np.arange(2048 + 64 * gh, 2048 + 64 * gh + 64),
            np.arange(2304 + 64 * gh, 2304 + 64 * gh + 64),
            np.arange(2560 + 128 * gh + 64 * half, 2560 + 128 * gh + 64 * half + 64),
            np.arange(3072 + 128 * gh + 64 * half, 3072 + 128 * gh + 64 * half + 64),
            np.arange(3584, 3600),
        ])
        win_all[:, hg] = w_in[:, :, cols]
    d["win_all"] = win_all
    tab, pm = _rope_tables()
    d["rope"] = tab
    d["pm"] = pm
    d["swamask"] = _swa_masks()
    d["ident"] = np.eye(128, dtype=np.float32)
    sink = np.asarray(inp["attn_sink"])
    d["sink"] = f(np.broadcast_to(sink.reshape(L, NHG, 1, 2), (L, NHG, 128, 2)))
    lr, li, ldt = (np.asarray(inp[k]) for k in ("s5_lam_re", "s5_lam_im", "s5_log_dt"))
    bre, bim = np.asarray(inp["s5_b_re"]), np.asarray(inp["s5_b_im"])
    cre, cim = np.asarray(inp["s5_c_re"]), np.asarray(inp["s5_c_im"])
    dsk = np.asarray(inp["s5_d"])
    s5col = np.zeros((L, NHG, 2, 128, 3), np.float32)
    s5row = np.zeros((L, NHG, 2, 64, 3, 128), np.float32)
    s5bz = np.zeros((L, NHG, 2, 64, 2, 128), np.float32)
    s5cz = np.zeros((L, NHG, 2, 128, 2, 64), np.float32)
    s5dz = np.zeros((L, NHG, 64, 64), np.float32)
    for l in range(L):
        for hg in range(NHG):
            for k in range(2):
                for j in range(2):
                    g = 4 * hg + 2 * k + j
                    gl = 2 * k + j
                    sl = slice(64 * j, 64 * j + 64)
                    s5col[l, hg, k, sl, 0] = lr[l, g]
                    s5col[l, hg, k, sl, 1] = li[l, g]
                    s5col[l, hg, k, sl, 2] = ldt[l, g]
                    s5row[l, hg, k, :, 0, sl] = lr[l, g][None, :]
                    s5row[l, hg, k, :, 1, sl] = li[l, g][None, :]
                    s5row[l, hg, k, :, 2, sl] = ldt[l, g]
                    s5bz[l, hg, k, 16 * gl:16 * gl + 16, 0, sl] = bre[l, g].T
                    s5bz[l, hg, k, 16 * gl:16 * gl + 16, 1, sl] = bim[l, g].T
                    s5cz[l, hg, k, sl, 0, 16 * gl:16 * gl + 16] = cre[l, g].T
                    s5cz[l, hg, k, sl, 1, 16 * gl:16 * gl + 16] = cim[l, g].T
            s5dz[l, hg][np.arange(64), np.arange(64)] = dsk[l, 4 * hg:4 * hg + 4].reshape(64)
    d["s5col"], d["s5row"], d["s5bz"], d["s5cz"], d["s5dz"] = s5col, s5row, s5bz, s5cz, s5dz
    wa2 = np.asarray(inp["gla_w_a2"])
    ba = np.asarray(inp["gla_b_a"])
    d["wa2"] = f(np.stack([[wa2[l][:, 64 * (hg // 2):64 * (hg // 2) + 64] for hg in range(NHG)] for l in range(L)]))
    d["ba"] = f(np.stack([[ba[l][64 * (hg // 2):64 * (hg // 2) + 64].reshape(64, 1) for hg in range(NHG)] for l in range(L)]))
    ck, cv = np.asarray(inp["cache_swa_k"]), np.asarray(inp["cache_swa_v"])
    d["ckT"] = f(ck.transpose(0, 3, 4, 1, 2))
    d["cvS"] = f(cv.transpose(0, 3, 2, 1, 4))
    d["ckN"] = f(ck.transpose(0, 3, 1, 2, 4))
    d["cvN"] = f(cv.transpose(0, 3, 1, 2, 4))
    sre, sim = np.asarray(inp["state_ssm_re"]), np.asarray(inp["state_ssm_im"])
    ssm0 = np.stack([sre, sim], 1)
    d["ssm0"] = f(ssm0.reshape(L, 2, 128, 16, 128).transpose(0, 1, 3, 4, 2))
    gst = np.asarray(inp["state_gla"])
    d["gst"] = f(gst.reshape(L, 128, 4, 64, 2, 64).transpose(0, 2, 4, 3, 1, 5).reshape(L, NHG, 64, 128, 64))
    cpv = np.asarray(inp["state_conv"])
    d["cprevT"] = f(cpv.transpose(0, 2, 3, 1))
    d["wout"] = f(inp["w_out"])
    d["wup"] = f(inp["ffn_w_up"])
    d["wdown"] = f(inp["ffn_w_down"])
    d["wglu"] = f(inp["s5_w_glu"])
    d["ln1"] = f(np.stack([np.stack([vec16(inp["ln1_g"][l]), vec16(inp["ln1_b"][l])]) for l in range(L)]))
    d["ln2"] = f(np.stack([np.stack([vec16(inp["ln2_g"][l]), vec16(inp["ln2_b"][l])]) for l in range(L)]))
    cw, cb = np.asarray(inp["ffn_conv_w"]), np.asarray(inp["ffn_conv_b"])
    d["convw"] = f(np.stack([np.stack([cw[l, j].reshape(NFC, 128).T for j in range(3)] + [cb[l].reshape(NFC, 128).T]) for l in range(L)]))
    d["bglu"] = f(np.stack([np.asarray(inp["s5_b_glu"][l]).reshape(4, 128).T for l in range(L)]))
    d["gnorm"] = f(np.stack([np.asarray(inp["gla_norm_g"][l]).reshape(4, 128).T for l in range(L)]))
    return d


INPUT_SHAPES = None


def chunked(ap2d):
    return ap2d.rearrange("(c p) n -> p c n", p=128)


class Builder:
    def __init__(self, shapes, debug=False):
        self.debug = debug
        self.nc = nc = bass.Bass("TRN2", target_bir_lowering=False)
        self.I = {}
        for name, shp in shapes.items():
            self.I[name] = nc.dram_tensor(name, list(shp), F32, kind="ExternalInput").ap()
        O = self.O = {}

        def out(name, shp):
            O[name] = nc.dram_tensor(name, list(shp), F32, kind="ExternalOutput").ap()
        out("y_out", [CPC, D])
        out("ks_out", [L, 1, 128, 128, 64])
        out("vs_out", [L, 1, 128, 128, 64])
        out("kp_out", [L, 1, 128, 64])
        out("vp_out", [L, 1, 128, 64])
        out("ssmp_out", [L, 2, 2, 128])
        out("ssms_out", [L, 2, 2, 128, 128])
        out("glap_out", [L, 1, 64, 64])
        out("glas_out", [L, 1, 64, 128, 64])
        out("convp_out", [L, 128, NFC, 2])
        out("convs_out", [L, 2, DFF, 128])
        kind = "ExternalOutput" if debug else "Internal"
        self.xn_in = nc.dram_tensor("xn_in", [D, CPC], BF16).ap()
        self.xg = [nc.dram_tensor(f"xg{l}", [NCORES * D, CPC], BF16).ap() for l in range(L)]
        self.xres_loc = nc.dram_tensor("xres_loc", [D, CPC], F32).ap()
        self.mb_in = nc.dram_tensor("mb_in", [BROWS, G + 2], BF16).ap()
        self.mb_loc = nc.dram_tensor("mb_loc", [NHG * BROWS, CPC + 2], BF16).ap()
        self.mbufs = [nc.dram_tensor(f"mbuf{l}", [NHG * BROWS, G + 2], BF16, kind=kind).ap() for l in range(L)]
        self.vs_scr = nc.dram_tensor("vs_scr", [128, 64], F32).ap()
        self.CONV = ["ident", "pm", "win_all", "s5dz", "wa2", "ckT", "cvS", "wglu", "wout", "wup", "wdown"]
        self.Wb = {k: nc.dram_tensor(k + "_bf", list(shapes[k]), BF16).ap() for k in self.CONV}

    def T(self, es, name, shape, dt):
        self._n = getattr(self, "_n", 0) + 1
        return es.enter_context(self.nc.sbuf_tensor(f"{name}_{self._n}", shape, dt))

    def build(self, n_layers=L, stop_after=None):
        nc = self.nc
        with ExitStack() as es:
            self.P = P = Prog(nc, es)
            self.es = es
            self.ps = [es.enter_context(nc.psum_tensor(f"ps{i}", [128, 512], F32)) for i in range(6)]
            self.psb = [es.enter_context(nc.psum_tensor(f"psb{i}", [128, 1024], BF16)) for i in range(2)]
            self.ps_i = 0
            self.psb_i = 0
            self.pid = nc.sync.partition_id()
            self.convert_weights(0)
            self.setup_consts()
            self.p0()
            self.convert_weights(("dense", 0))
            for l in range(n_layers):
                self.cur_l = l
                self.p1(l, 0)
                P.collective("AllGather", self.mb_in, self.mbufs[l], reads=["mb_in"], writes=[f"mbuf{l}"])
                if l + 1 < n_layers:
                    self.convert_weights(("dense", l + 1))
                if isinstance(stop_after, tuple) and stop_after[:2] == ("p1", l):
                    break
                self.p2(l, last=(l == n_layers - 1))
            P.barrier()
            P.finish("sp")
        return nc

    def convert_weights(self, stage):
        P = self.P
        early = ["ident", "pm", "win_all", "s5dz", "wa2", "ckT", "cvS"]
        self.wb_keys = getattr(self, "wb_keys", {})
        self.pending = getattr(self, "pending", [])
        if stage == 0:
            jobs = [(k, self.I[k], self.Wb[k], None) for k in early]
        else:
            l = stage[1]
            jobs = [(k, self.I[k][l], self.Wb[k][l], l) for k in self.CONV if k not in early]
        for k, src, dst, l in jobs:
            nd = len(src.shape)
            names = " ".join(f"d{i}" for i in range(nd))
            pat = f"{names} -> ({' '.join(f'd{i}' for i in range(nd - 1))}) d{nd - 1}"
            s2, d2 = src.rearrange(pat), dst.rearrange(pat)
            rows = s2.shape[0]
            step = max(8, min(rows, ((1 << 22) if stage == 0 else (1 << 21)) // max(1, s2.shape[1])) // 8 * 8)
            keys = []
            for r0 in range(0, rows, step):
                r1 = min(rows, r0 + step)
                key = ("wb", k, l, r0)
                if stage != 0:
                    self.pending.append((d2[r0:r1, :], s2[r0:r1, :], key))
                else:
                    P.dma("pool", d2[r0:r1, :], s2[r0:r1, :], writes=[key])
                keys.append(key)
            self.wb_keys[(k, l)] = keys
        if stage == 0:
            P.barrier()
        else:
            P.bg_pool = True
            self.pump_total = len(self.pending)

    def pump(self, frac, pace_key):
        cnt = len(self.pending) if frac >= 1.0 else min(len(self.pending), max(1, int(math.ceil(self.pump_total * frac))))
        for _ in range(cnt):
            d_, s_, key = self.pending.pop(0)
            self.P.dma("pool", d_, s_, writes=[key], after=[pace_key] if pace_key else [])

    def bank(self):
        i = self.ps_i
        self.ps_i = (i + 1) % len(self.ps)
        return self.ps[i], f"ps{i}"

    def bankb(self):
        i = self.psb_i
        self.psb_i = (i + 1) % len(self.psb)
        return self.psb[i], f"psb{i}"

    def setup_consts(self):
        P, es, I = self.P, self.es, self.I
        T = lambda n, s, d: self.T(es, n, s, d)
        self.ident_f = T("ident_f", [128, 128], F32)
        self.ident_b = T("ident_b", [128, 128], BF16)
        self.pm_b = T("pm_b", [128, 128], BF16)
        self.mask_f = T("mask_f", [128, 3, 256], F32)
        self.ones_f = T("ones_f", [128, 512], F32)
        self.ones_b = T("ones_b", [128, 128], BF16)
        self.iota1 = T("iota1", [128, 512], F32)
        self.iota_i = T("iota_i", [128, 512], I32)
        self.resetm = T("resetm", [128, 512], F32)
        self.blk2 = T("blk2", [128, 2], F32)
        self.sel65 = T("sel65", [128, 64], F32)
        P.dma("sp", self.ident_f[:], I["ident"][:, :], writes=["ident_f"])
        P.dma("sp", self.ident_b[:], self.Wb["ident"][:, :], writes=["ident_b"])
        P.dma("sp", self.pm_b[:], self.Wb["pm"][:, :], writes=["pm_b"])
        P.dma("sp", self.mask_f[:], I["swamask"][:, :, :], writes=["mask_f"])
        P.op("dve", MEMSET(self.ones_f[:], 1.0), writes=["ones_f"])
        P.op("dve", MEMSET(self.ones_b[:], 1.0), writes=["ones_b"])
        P.op("pool", lambda e: e.iota(self.iota_i[:], pattern=[[1, 512]], base=1, channel_multiplier=0), writes=["iota_i"])
        P.op("dve", CP(self.iota1[:], self.iota_i[:]), reads=["iota_i"], writes=["iota1"])
        P.op("dve", MEMSET(self.resetm[:], 1.0), writes=["resetm"])
        for c in range(4):
            P.op("dve", MEMSET(self.resetm[:, c * 128:c * 128 + 1], 0.0), writes=["resetm"])
        P.op("dve", MEMSET(self.blk2[:], 0.0), writes=["blk2"])
        P.op("dve", MEMSET(self.blk2[0:64, 0:1], 1.0), writes=["blk2"])
        P.op("dve", MEMSET(self.blk2[64:128, 1:2], 1.0), writes=["blk2"])
        self.cmask = T("cmask", [128, CPC + 2], F32)
        self.nsm = T("nsm", [128, 1], F32)
        self.zero_b = T("zero_b", [128, 2], BF16)
        P.dma("sp", self.cmask[:], I["cmask"][0:1, :].partition_broadcast(128), writes=["cmask"])
        P.dma("sp", self.nsm[:], I["nsm"][:, :], writes=["nsm"])
        P.op("dve", MEMSET(self.zero_b[:], 0.0), writes=["zero_b"])
        P.dma("sp", self.mb_in[0:128, 0:2], self.zero_b[:], reads=["zero_b"], writes=["mb_in"])
        P.dma("sp", self.mb_in[128:256, 0:2], self.zero_b[:], reads=["zero_b"], writes=["mb_in"])
        P.dma("sp", self.mb_in[256:320, 0:2], self.zero_b[0:64, :], reads=["zero_b"], writes=["mb_in"])
        P.op("dve", MEMSET(self.sel65[:], 0.0), writes=["sel65"])
        P.op("dve", MEMSET(self.sel65[64:65, :], 1.0), writes=["sel65"])

    def sincos(self, ang, ang_key, sin_out, cos_out, out_key, tmp_f, tmp_i, tmp_key, np_, nf):
        P = self.P
        a = ang
        for shift, dst in ((0.0, sin_out), (math.pi / 2, cos_out)):
            if dst is None:
                continue
            P.op("dve", TS(tmp_i, a, 1.0 / TWO_PI, ALU.mult, shift / TWO_PI, ALU.add), reads=[ang_key], writes=[tmp_key + "i"])
            P.op("dve", CP(tmp_f, tmp_i), reads=[tmp_key + "i"], writes=[tmp_key + "f"])
            P.op("dve", STT(dst, tmp_f, -C1, a, ALU.mult, ALU.add), reads=[tmp_key + "f", ang_key], writes=[out_key])
            P.op("dve", STT(dst, tmp_f, -C2, dst, ALU.mult, ALU.add), reads=[tmp_key + "f", out_key], writes=[out_key])
            if shift:
                P.op("dve", TS(dst, dst, shift, ALU.add), reads=[out_key], writes=[out_key])
            P.op("dve", TS(dst, dst, -math.pi, ALU.max, math.pi, ALU.min), reads=[out_key], writes=[out_key])
            P.op("act", ACT(dst, dst, AF.Sin), reads=[out_key], writes=[out_key])

    def layernorm(self, es, xr, xr_key, ncols, gb, gb_key, xb=None, xb_key=None, mask=None):
        P = self.P
        T = lambda n, s, d: self.T(es, n, s, d)
        LW = CPC + 2
        sqs = [T("ln_sq", [128, 512], F32) for _ in range(2)]
        mean = T("ln_mean", [128, LW], F32)
        rstd = T("ln_rstd", [128, LW], F32)
        shift = T("ln_shift", [128, LW], F32)
        tmp = T("ln_tmp", [128, LW], F32)
        nblk = (ncols + 511) // 512
        for b in range(nblk):
            c0, c1 = b * 512, min(ncols, b * 512 + 512)
            n = c1 - c0
            ps_s, ks = self.bank()
            ps_q, kq = self.bank()
            fs, fq = [], []
            for k in range(16):
                sq, sk = sqs[k % 2], f"ln_sq{k % 2}"
                P.op("act", ACT(sq[:, 0:n], xr[:, k, c0:c1], AF.Square), reads=[xr_key], writes=[sk])
                P.op("pe", [MM(ps_s[:, 0:n], self.ones_f[:, 0:128], xr[:, k, c0:c1], start=(k == 0), stop=(k == 15)),
                            MM(ps_q[:, 0:n], self.ones_f[:, 0:128], sq[:, 0:n], start=(k == 0), stop=(k == 15))],
                     reads=["ones_f", xr_key, sk], writes=[ks, kq])
            P.op("act", ACT(mean[:, c0:c1], ps_s[:, 0:n], AF.Copy, scale=1.0 / D), reads=[ks], writes=["ln_mean"])
            P.op("dve", TT(tmp[:, c0:c1], mean[:, c0:c1], mean[:, c0:c1], ALU.mult), reads=["ln_mean"], writes=["ln_tmp"])
            P.op("dve", STT(tmp[:, c0:c1], ps_q[:, 0:n], 1.0 / D, tmp[:, c0:c1], ALU.mult, ALU.subtract), reads=[kq, "ln_tmp"], writes=["ln_tmp"])
            P.op("dve", TS(tmp[:, c0:c1], tmp[:, c0:c1], EPS, ALU.add), reads=["ln_tmp"], writes=["ln_tmp"])
            P.op("act", ACT(tmp[:, c0:c1], tmp[:, c0:c1], AF.Sqrt), reads=["ln_tmp"], writes=["ln_tmp"])
            P.op("dve", RECIP(rstd[:, c0:c1], tmp[:, c0:c1]), reads=["ln_tmp"], writes=["ln_rstd"])
            P.op("dve", STT(shift[:, c0:c1], mean[:, c0:c1], -1.0, rstd[:, c0:c1], ALU.mult, ALU.mult), reads=["ln_mean", "ln_rstd"], writes=["ln_shift"])
        for k in range(16):
            xk = xr[:, k, 0:ncols]
            P.op("dve", TT(xk, xk, rstd[:, 0:ncols], ALU.mult), reads=[xr_key, "ln_rstd"], writes=[xr_key])
            P.op("dve", TT(xk, xk, shift[:, 0:ncols], ALU.add), reads=[xr_key, "ln_shift"], writes=[xr_key])
            P.op("act", ACT(xk, xk, AF.Identity, scale=gb[:, 0, k:k + 1], bias=gb[:, 1, k:k + 1]), reads=[xr_key, gb_key], writes=[xr_key])
            if xb is not None:
                P.op("dve", TT(xb[:, k, 0:ncols], xk, mask, ALU.mult), reads=[xr_key, "cmask"], writes=[xb_key])

    def p0(self):
        P, I = self.P, self.I
        with ExitStack() as es:
            T = lambda n, s, d: self.T(es, n, s, d)
            xr = T("p0_xr", [128, 16, CPC], F32)
            xb = T("p0_xb", [128, 16, CPC], BF16)
            gb = T("p0_gb", [128, 2, 16], F32)
            P.dma("sp", gb[:], I["ln_in"].rearrange("t p k -> p t k"), writes=["p0_gb"])
            P.dma("sp", xr[:], chunked(I["xT"]), writes=["p0_xr"])
            with ExitStack() as es2:
                self.layernorm(es2, xr, "p0_xr", CPC, gb, "p0_gb", xb, "p0_xb", mask=self.cmask[:, 2:CPC + 2])
                P.dma("sp", chunked(self.xres_loc), xr[:], reads=["p0_xr"], writes=["xres_loc"])
                P.dma("sp", chunked(self.xn_in), xb[:], reads=["p0_xb"], writes=["xn_in"])
                P.barrier()
            P.collective("AllGather", self.xn_in, self.xg[0], reads=["xn_in"], writes=["xg0"])
            P.barrier()

    def p1(self, l, hg):
        P, I, O = self.P, self.I, self.O
        kvh = hg // 2
        NT = (G + W1 - 1) // W1
        with ExitStack() as es:
            T = lambda n, s, d: self.T(es, n, s, d)
            win = T("win", [128, 16, 768], BF16)
            P.dma("sp", win[:, :, 0:NWC], chunked(self.Wb["win_all"][l, hg]), writes=["win"])
            sinkn = T("sinkn", [128, 2], F32)
            P.dma("sp", sinkn[:], I["sink"][l, hg], writes=["sinkn"])
            P.op("dve", TS(sinkn[:], sinkn[:], -1.0, ALU.mult), reads=["sinkn"], writes=["sinkn"])
            Ec, Es, magt, bbre, bbim, czre, nczim, arc, aic, naic = [], [], [], [], [], [], [], [], [], []
            pre = {}
            for k in range(2):
                pre[("arc", k)], pre[("aic", k)], pre[("naic", k)] = T("arc", [128, 1], F32), T("aic", [128, 1], F32), T("naic", [128, 1], F32)
                pre[("Ec", k)], pre[("Es", k)], pre[("magt", k)] = T("Ec", [128, 512], F32), T("Es", [128, 512], F32), T("magt", [128, 512], F32)
                pre[("bbre", k)], pre[("bbim", k)] = T("bbre", [128, 128], BF16), T("bbim", [128, 128], BF16)
                pre[("czre", k)], pre[("nczim", k)] = T("czre", [128, 64], BF16), T("nczim", [128, 64], BF16)
            dz_b = T("dz_b", [128, 64], BF16)
            wa2_b = T("wa2_b", [16, 64], BF16)
            nba = T("nba", [64, 1], F32)
            with ExitStack() as esp:
                Tp = lambda n, s, d: self.T(esp, n, s, d)
                ang = Tp("ang", [128, 512], F32)
                tf = Tp("sc_tf", [128, 512], F32)
                ti = Tp("sc_ti", [128, 512], I32)
                for k in range(0 if 's5prep' in DBG_SKIP else 2):
                    cp = Tp("cp", [128, 3], F32)
                    sm = Tp("sm", [128, 12], F32)
                    P.dma("sp", cp[:], I["s5col"][l, hg, k], writes=["cp"])
                    dtc, thc, lrdt, magc, sc_, cc_ = (sm[:, j:j + 1] for j in range(6))
                    P.op("act", ACT(dtc, cp[:, 2:3], AF.Exp), reads=["cp"], writes=["sm"])
                    P.op("dve", TT(thc, cp[:, 1:2], dtc, ALU.mult), reads=["cp", "sm"], writes=["sm"])
                    P.op("dve", TT(lrdt, cp[:, 0:1], dtc, ALU.mult), reads=["cp", "sm"], writes=["sm"])
                    P.op("act", ACT(magc, lrdt, AF.Exp), reads=["sm"], writes=["sm"])
                    sct = Tp("sct", [128, 2], F32)
                    self.sincos(thc, "sm", sct[:, 0:1], sct[:, 1:2], "sct", tf[:, 0:1], ti[:, 0:1], "sc_t", 128, 1)
                    a_r, a_i, na_i = pre[("arc", k)], pre[("aic", k)], pre[("naic", k)]
                    P.op("dve", TT(a_r[:], magc, sct[:, 1:2], ALU.mult), reads=["sm", "sct"], writes=["arc"])
                    P.op("dve", TT(a_i[:], magc, sct[:, 0:1], ALU.mult), reads=["sm", "sct"], writes=["aic"])
                    P.op("dve", TS(na_i[:], a_i[:], -1.0, ALU.mult), reads=["aic"], writes=["naic"])
                    arc.append(a_r); aic.append(a_i); naic.append(na_i)
                    ec, esn, mt = pre[("Ec", k)], pre[("Es", k)], pre[("magt", k)]
                    P.op("dve", TS(ang[:], self.iota1[:], thc, ALU.mult), reads=["iota1", "sm"], writes=["ang"])
                    self.sincos(ang[:], "ang", esn[:], ec[:], f"EcEs{k}", tf[:], ti[:], "sc_t", 128, 512)
                    P.op("dve", TS(mt[:], self.ones_f[:], magc, ALU.mult), reads=["ones_f", "sm"], writes=[f"magt{k}"])
                    Ec.append(ec); Es.append(esn); magt.append(mt)
                    rp = Tp("rp", [128, 3, 128], F32)
                    bz = Tp("bz", [128, 2, 128], F32)
                    r = Tp("rwork", [128, 12, 128], F32)
                    P.dma("sp", rp[64:128], I["s5row"][l, hg, k], writes=["rp"])
                    P.dma("sp", bz[64:128], I["s5bz"][l, hg, k], writes=["bz"])
                    h = slice(64, 128)
                    lrr, lir, ldr = rp[h, 0, :], rp[h, 1, :], rp[h, 2, :]
                    R_ = lambda j: r[h, j, :]
                    rk = ["rp", "rwork"]
                    P.op("act", ACT(R_(0), ldr, AF.Exp), reads=rk, writes=["rwork"])
                    P.op("dve", TT(R_(1), lir, R_(0), ALU.mult), reads=rk, writes=["rwork"])
                    P.op("dve", TT(R_(2), lrr, R_(0), ALU.mult), reads=rk, writes=["rwork"])
                    P.op("act", ACT(R_(2), R_(2), AF.Exp), reads=rk, writes=["rwork"])
                    self.sincos(R_(1), "rwork", R_(3), R_(4), "rwork", tf[h, 0:128], ti[h, 0:128], "sc_t", 64, 128)
                    P.op("dve", TT(R_(4), R_(4), R_(2), ALU.mult), reads=rk, writes=["rwork"])
                    P.op("dve", TT(R_(3), R_(3), R_(2), ALU.mult), reads=rk, writes=["rwork"])
                    P.op("dve", TS(R_(4), R_(4), -1.0, ALU.add), reads=rk, writes=["rwork"])
                    P.op("dve", TT(R_(5), lrr, lrr, ALU.mult), reads=rk, writes=["rwork"])
                    P.op("dve", TT(R_(6), lir, lir, ALU.mult), reads=rk, writes=["rwork"])
                    P.op("dve", TT(R_(5), R_(5), R_(6), ALU.add), reads=rk, writes=["rwork"])
                    P.op("dve", RECIP(R_(5), R_(5)), reads=rk, writes=["rwork"])
                    P.op("dve", TT(R_(6), R_(4), lrr, ALU.mult), reads=rk, writes=["rwork"])
                    P.op("dve", TT(R_(7), R_(3), lir, ALU.mult), reads=rk, writes=["rwork"])
                    P.op("dve", TT(R_(6), R_(6), R_(7), ALU.add), reads=rk, writes=["rwork"])
                    P.op("dve", TT(R_(6), R_(6), R_(5), ALU.mult), reads=rk, writes=["rwork"])
                    P.op("dve", TT(R_(7), R_(3), lrr, ALU.mult), reads=rk, writes=["rwork"])
                    P.op("dve", TT(R_(8), R_(4), lir, ALU.mult), reads=rk, writes=["rwork"])
                    P.op("dve", TT(R_(7), R_(7), R_(8), ALU.subtract), reads=rk, writes=["rwork"])
                    P.op("dve", TT(R_(7), R_(7), R_(5), ALU.mult), reads=rk, writes=["rwork"])
                    b_re, b_im = pre[("bbre", k)], pre[("bbim", k)]
                    rb = ["rwork", "bz"]
                    P.op("dve", TT(R_(8), R_(6), bz[h, 0, :], ALU.mult), reads=rb, writes=["rwork"])
                    P.op("dve", TT(R_(9), R_(7), bz[h, 1, :], ALU.mult), reads=rb, writes=["rwork"])
                    P.op("dve", TT(b_re[h, :], R_(8), R_(9), ALU.subtract), reads=rb, writes=[f"bb{k}"])
                    P.op("dve", TT(R_(8), R_(6), bz[h, 1, :], ALU.mult), reads=rb, writes=["rwork"])
                    P.op("dve", TT(R_(9), R_(7), bz[h, 0, :], ALU.mult), reads=rb, writes=["rwork"])
                    P.op("dve", TT(b_im[h, :], R_(8), R_(9), ALU.add), reads=rb, writes=[f"bb{k}"])
                    bbre.append(b_re); bbim.append(b_im)
                    czf = Tp("czf", [128, 2, 64], F32)
                    P.dma("sp", czf[:], I["s5cz"][l, hg, k], writes=["czf"])
                    c_re, nc_im = pre[("czre", k)], pre[("nczim", k)]
                    P.op("dve", CP(c_re[:], czf[:, 0, :]), reads=["czf"], writes=[f"cz{k}"])
                    P.op("dve", TS(nc_im[:], czf[:, 1, :], -1.0, ALU.mult), reads=["czf"], writes=[f"cz{k}"])
                    czre.append(c_re); nczim.append(nc_im)
                P.dma("sp", dz_b[64:128, :], self.Wb["s5dz"][l, hg], writes=["dz_b"])
                P.dma("sp", wa2_b[:], self.Wb["wa2"][l, hg], writes=["wa2_b"])
                P.dma("sp", nba[:], I["ba"][l, hg], writes=["nba"])
                P.op("dve", TS(nba[:], nba[:], -1.0, ALU.mult), reads=["nba"], writes=["nba"])
                P.barrier()
            xt = [T("xt", [128, 16, W1], BF16) for _ in range(2)]
            ropet = [T("ropet", [128, 2, W1], F32) for _ in range(2)]
            qT = T("qT", [128, W1], BF16)
            qb = T("qb", [128, W1], BF16)
            kT = T("kT", [128, 128 + W1], BF16)
            kf = T("kf", [128, 256], F32)
            vf = T("vf", [128, 256], F32)
            t1 = T("t1", [128, W1], F32)
            t2 = T("t2", [128, W1], F32)
            t3 = T("t3", [128, W1], F32)
            t4 = T("t4", [128, W1], F32)
            gre = T("gre", [128, W1], F32)
            gim = T("gim", [128, W1], F32)
            VUb = T("VUb", [128, W1], BF16)
            Vtm = T("Vtm", [128, 5, 65], BF16)
            oa2 = T("oa2", [128, 128], BF16)
            oaT_st = T("oaT_st", [128, W1], BF16)
            pexp = [T("pexp", [128, 256], F32) for _ in range(2)]
            PTt = [T("PT", [128, 256], BF16) for _ in range(2)]
            rden = T("rden", [128, 4], F32)
            hprev = T("hprev", [128, 4], F32)
            hre_b = [T("hre_b", [128, W1], BF16) for _ in range(2)]
            him_b = [T("him_b", [128, W1], BF16) for _ in range(2)]
            z_st = T("z_st", [64, W1], BF16)
            qg_f = T("qg_f", [64, W1], F32)
            kg_f = T("kg_f", [64, W1], F32)
            VGb = T("VGb", [128, W1], BF16)
            vgf = T("vgf", [64, 128], F32)
            acb = T("acb", [16, W1], BF16)
            lg = T("lg", [64, W1], F32)
            cs = T("cs", [64, W1], F32)
            Ep = T("Ep", [64, W1], F32)
            Em = T("Em", [64, W1], F32)
            qin_b = T("qin_b", [64, W1], BF16)
            kout_f = T("kout_f", [64, W1], F32)
            kout_b = T("kout_b", [64, W1], BF16)
            kend_b = T("kend_b", [64, W1], BF16)
            attT_b = [T("attT_b", [128, 128], BF16) for _ in range(2)]
            vtm_b = [T("vtm_b", [128, 64], BF16) for _ in range(2)]
            kend_tm = [T("kend_tm", [128, 64], BF16) for _ in range(2)]
            S_f = T("S_f", [64, 64], F32)
            S_b = T("S_b", [64, 64], BF16)
            o_st = T("o_st", [64, W1], BF16)
            if 'initoa' in DBG_SKIP:
                P.op("dve", MEMSET(oaT_st[:], 0.0), writes=["oaT_st"])
            P.op("dve", MEMSET(kT[:, 0:128], 0.0), writes=["kT"])
            P.op("dve", MEMSET(Vtm[:], 0.0), writes=["Vtm"])
            P.op("dve", MEMSET(Vtm[:, :, 64:65], 1.0), writes=["Vtm"])
            P.op("dve", MEMSET(hprev[:], 0.0), writes=["hprev"])
            P.op("dve", MEMSET(S_f[:], 0.0), writes=["S_f"])
            P.op("dve", MEMSET(S_b[:], 0.0), writes=["S_b"])

            def load_tile(t):
                c0 = t * W1
                n = min(W1, G - c0)
                xg3 = self.xg[l].rearrange("(r c p) n -> r p c n", r=NCORES, p=128)
                a = c0
                while a < c0 + n:
                    r = a // CPC
                    b = min(c0 + n, (r + 1) * CPC)
                    P.dma("sp", xt[t % 2][:, :, a - c0:b - c0], xg3[r][:, :, a - r * CPC:b - r * CPC], reads=[f"xg{l}"], writes=[f"xt{t % 2}"])
                    a = b
                if 'ropedma' not in DBG_SKIP:
                    P.dma("sp", ropet[t % 2][:, :, 0:n], I["rope"][:, :, c0:c0 + n].rearrange("t p n -> p t n"), writes=[f"ropet{t % 2}"])

            if 'tiles' in DBG_SKIP:
                NT = 0
            else:
                load_tile(0)
            for t in range(NT):
                c0 = t * W1
                n = min(W1, G - c0)
                nblk = n // 128
                last = (t == NT - 1)
                if t + 1 < NT:
                    load_tile(t + 1)
                X, xk_ = xt[t % 2], f"xt{t % 2}"
                RT, rk_ = ropet[t % 2], f"ropet{t % 2}"

                def proj(c_lo, c_hi, m):
                    ps, key = self.bank()
                    P.op("pe", [MM(ps[0:m, 0:n], win[:, k, c_lo:c_hi], X[:, k, 0:n], start=(k == 0), stop=(k == 15)) for k in range(16)],
                         reads=["win", xk_], writes=[key])
                    return ps, key

                def roped(c_lo, dst, dst_key, f32_dst=None):
                    ps, key = proj(c_lo, c_lo + 128, 128)
                    P.op("act", ACT(qb[:, 0:n], ps[:, 0:n], AF.Copy), reads=[key], writes=["qb"])
                    ps2, key2 = self.bank()
                    P.op("pe", MM(ps2[:, 0:n], self.pm_b[:], qb[:, 0:n]), reads=["pm_b", "qb"], writes=[key2])
                    P.op("dve", TT(t1[:, 0:n], ps[:, 0:n], RT[:, 0, 0:n], ALU.mult), reads=[key, rk_], writes=["t1"])
                    P.op("dve", TT(t2[:, 0:n], ps2[:, 0:n], RT[:, 1, 0:n], ALU.mult), reads=[key2, rk_], writes=["t2"])
                    P.op("dve", TT(dst, t1[:, 0:n], t2[:, 0:n], ALU.add), reads=["t1", "t2"], writes=[dst_key])
                    if f32_dst is not None:
                        P.op("dve", TT(f32_dst, t1[:, 0:n], t2[:, 0:n], ALU.add), reads=["t1", "t2"], writes=["kf"])

                if self.pending and l == 0:
                    self.pump(1.0 / (NT - 2) if t < NT - 1 else 1.0, getattr(self, "pace_tok", None))
                if 'rope' not in DBG_SKIP:
                    roped(0, qT[:, 0:n], "qT")
                    roped(128, kT[:, 128:128 + n], "kT", f32_dst=(kf[:, 0:n] if last else None))
                if 'vu' in DBG_SKIP:
                    continue
                ps_vu, kvu = proj(256, 384, 128)
                self.pace_tok = P.op("act", ACT(VUb[:, 0:n], ps_vu[:, 0:n], AF.Copy), reads=[kvu], writes=["VUb"])
                if last and 'novf' not in DBG_SKIP:
                    P.op("act", ACT(vf[0:64, 0:n], ps_vu[0:64, 0:n], AF.Copy), reads=[kvu], writes=["vf"])
                for i in range(0 if 'vtr' in DBG_SKIP else nblk):
                    pb_, kb_ = self.bankb()
                    P.op("pe", TR(pb_[:, 0:64], VUb[0:64, i * 128:(i + 1) * 128], self.ident_b[0:64, 0:64]), reads=["VUb", "ident_b"], writes=[kb_])
                    P.op("act", ACT(Vtm[:, i + 1, 0:64], pb_[:, 0:64], AF.Copy), reads=[kb_], writes=["Vtm"])
                for i in range(0 if 'swa' in DBG_SKIP else nblk):
                    gb = 4 * t + i
                    if gb == NB - 1:
                        continue
                    mv = 1 if gb == 0 else (2 if gb == 1 else 0)
                    for hh in range(2):
                        pb = 64 * hh
                        ps_s, ks_ = self.bank()
                        qs = qT[pb:pb + 64, i * 128:(i + 1) * 128]
                        P.op("pe", [MM(ps_s[:, 0:128], kT[pb:pb + 64, i * 128:(i + 1) * 128], qs),
                                    MM(ps_s[:, 128:256], kT[pb:pb + 64, (i + 1) * 128:(i + 2) * 128], qs)],
                             reads=["kT", "qT"], writes=[ks_])
                        P.op("act", ACT(pexp[hh][:], ps_s[:, 0:256], AF.Exp, scale=0.125, bias=sinkn[:, hh:hh + 1]), reads=[ks_, "sinkn"], writes=[f"pexp{hh}"])
                        P.op("dve", TT(PTt[hh][:], pexp[hh][:], self.mask_f[:, mv, :], ALU.mult), reads=[f"pexp{hh}", "mask_f"], writes=[f"PT{hh}"])
                        ps_o, ko_ = self.bank()
                        P.op("pe", [MM(ps_o[:, 0:65], PTt[hh][:, 0:128], Vtm[:, i, :], start=True, stop=False),
                                    MM(ps_o[:, 0:65], PTt[hh][:, 128:256], Vtm[:, i + 1, :], start=False, stop=True)],
                             reads=[f"PT{hh}", "Vtm"], writes=[ko_])
                        P.op("dve", TS(rden[:, hh:hh + 1], ps_o[:, 64:65], 1.0, ALU.add), reads=[ko_], writes=["rden"])
                        P.op("dve", RECIP(rden[:, hh:hh + 1], rden[:, hh:hh + 1]), reads=["rden"], writes=["rden"])
                        P.op("act", ACT(oa2[:, pb:pb + 64], ps_o[:, 0:64], AF.Copy, scale=rden[:, hh:hh + 1]), reads=[ko_, "rden"], writes=["oa2"])
                    pb_, kb_ = self.bankb()
                    P.op("pe", TR(pb_[:, 0:128], oa2[:], self.ident_b[:]), reads=["oa2", "ident_b"], writes=[kb_])
                    P.op("act", ACT(oaT_st[:, i * 128:(i + 1) * 128], pb_[:, 0:128], AF.Copy), reads=[kb_], writes=["oaT_st"])
                npr = n - 128 if last else n
                if 'nombuf' not in DBG_SKIP:
                    P.dma("sp", self.mb_in[0:128, 2 + c0:2 + c0 + npr], oaT_st[:, 0:npr], reads=["oaT_st"])
                for k in range(0 if 's5' in DBG_SKIP else 2):
                    ps_re, kre = self.bank()
                    ps_im, kim = self.bank()
                    P.op("pe", MM(ps_re[:, 0:n], bbre[k][64:128, :], VUb[64:128, 0:n]), reads=[f"bb{k}", "VUb"], writes=[kre])
                    P.op("pe", MM(ps_im[:, 0:n], bbim[k][64:128, :], VUb[64:128, 0:n]), reads=[f"bb{k}", "VUb"], writes=[kim])
                    ek = f"EcEs{k}"
                    ec, esn = Ec[k][:, 0:n], Es[k][:, 0:n]
                    P.op("dve", TT(t1[:, 0:n], ps_re[:, 0:n], ec, ALU.mult), reads=[kre, ek], writes=["t1"])
                    P.op("dve", TT(t2[:, 0:n], ps_im[:, 0:n], esn, ALU.mult), reads=[kim, ek], writes=["t2"])
                    P.op("dve", TT(t1[:, 0:n], t1[:, 0:n], t2[:, 0:n], ALU.add), reads=["t1", "t2"], writes=["t1"])
                    P.op("dve", TT(t3[:, 0:n], ps_im[:, 0:n], ec, ALU.mult), reads=[kim, ek], writes=["t3"])
                    P.op("dve", TT(t4[:, 0:n], ps_re[:, 0:n], esn, ALU.mult), reads=[kre, ek], writes=["t4"])
                    P.op("dve", TT(t3[:, 0:n], t3[:, 0:n], t4[:, 0:n], ALU.subtract), reads=["t3", "t4"], writes=["t3"])
                    P.op("dve", SCAN(gre[:, 0:n], magt[k][:, 0:n], t1[:, 0:n], hprev[:, 2 * k:2 * k + 1]), reads=[f"magt{k}", "t1", "hprev"], writes=["gre"])
                    P.op("dve", SCAN(gim[:, 0:n], magt[k][:, 0:n], t3[:, 0:n], hprev[:, 2 * k + 1:2 * k + 2]), reads=[f"magt{k}", "t3", "hprev"], writes=["gim"])
                    P.op("dve", TT(t1[:, 0:n], gre[:, 0:n], ec, ALU.mult), reads=["gre", ek], writes=["t1"])
                    P.op("dve", TT(t2[:, 0:n], gim[:, 0:n], esn, ALU.mult), reads=["gim", ek], writes=["t2"])
                    P.op("dve", TT(hre_b[k][:, 0:n], t1[:, 0:n], t2[:, 0:n], ALU.subtract), reads=["t1", "t2"], writes=[f"hre{k}"])
                    P.op("dve", TT(t3[:, 0:n], gre[:, 0:n], esn, ALU.mult), reads=["gre", ek], writes=["t3"])
                    P.op("dve", TT(t4[:, 0:n], gim[:, 0:n], ec, ALU.mult), reads=["gim", ek], writes=["t4"])
                    P.op("dve", TT(him_b[k][:, 0:n], t3[:, 0:n], t4[:, 0:n], ALU.add), reads=["t3", "t4"], writes=[f"him{k}"])
                    cc = npr - 1
                    cs1 = slice(cc, cc + 1)
                    P.op("dve", TT(hprev[:, 2 * k:2 * k + 1], t1[:, cs1], t2[:, cs1], ALU.subtract), reads=["t1", "t2"], writes=["hprev"])
                    P.op("dve", TT(hprev[:, 2 * k + 1:2 * k + 2], t3[:, cs1], t4[:, cs1], ALU.add), reads=["t3", "t4"], writes=["hprev"])
                    if last:
                        for ri in range(2):
                            P.dma("sp", O["ssmp_out"][l, ri, 2 * hg + k].rearrange("(p o) -> p o", o=1), hprev[:, 2 * k + ri:2 * k + ri + 1], reads=["hprev"])
                        h0 = T("h0", [128, 2, 128], F32)
                        hs = T("hs", [128, 2, 128], F32)
                        for ri in range(2):
                            P.dma("sp", h0[:, ri, :], I["ssm0"][l, ri, 2 * hg + k], writes=["h0"])
                        sc = slice(128, 256)
                        P.op("dve", TS(t1[:, 0:128], h0[:, 0, :], arc[k][:], ALU.mult), reads=["h0", "arc"], writes=["t1"])
                        P.op("dve", STT(t1[:, 0:128], h0[:, 1, :], naic[k][:], t1[:, 0:128], ALU.mult, ALU.add), reads=["h0", "naic", "t1"], writes=["t1"])
                        P.op("dve", TT(hs[:, 0, :], t1[:, 0:128], ps_re[:, sc], ALU.add), reads=["t1", kre], writes=["hs"])
                        P.op("dve", TS(t2[:, 0:128], h0[:, 1, :], arc[k][:], ALU.mult), reads=["h0", "arc"], writes=["t2"])
                        P.op("dve", STT(t2[:, 0:128], h0[:, 0, :], aic[k][:], t2[:, 0:128], ALU.mult, ALU.add), reads=["h0", "aic", "t2"], writes=["t2"])
                        P.op("dve", TT(hs[:, 1, :], t2[:, 0:128], ps_im[:, sc], ALU.add), reads=["t2", kim], writes=["hs"])
                        P.op("dve", CP(hre_b[k][:, sc], hs[:, 0, :]), reads=["hs"], writes=[f"hre{k}"])
                        P.op("dve", CP(him_b[k][:, sc], hs[:, 1, :]), reads=["hs"], writes=[f"him{k}"])
                        for ri in range(2):
                            pst, kst = self.bank()
                            P.op("pe", TR(pst[:, 0:128], hs[:, ri, :], self.ident_f[:]), reads=["hs", "ident_f"], writes=[kst])
                            hst = T("hst", [128, 128], F32)
                            P.op("act", ACT(hst[:], pst[:, 0:128], AF.Copy), reads=[kst], writes=["hst"])
                            P.dma("sp", O["ssms_out"][l, ri, 2 * hg + k], hst[:], reads=["hst"])
                if 's5' in DBG_SKIP:
                    if not last and 'carry' not in DBG_SKIP:
                        P.op("dve", CP(kT[:, 0:128], kT[:, n:n + 128]), reads=["kT"], writes=["kT"])
                        P.op("dve", CP(Vtm[:, 0, :], Vtm[:, nblk, :]), reads=["Vtm"], writes=["Vtm"])
                    continue
                ps_y, ky = self.bank()
                fy = []
                for k in range(2):
                    fy.append(MM(ps_y[0:64, 0:n], czre[k][:], hre_b[k][:, 0:n], start=(k == 0), stop=False))
                    fy.append(MM(ps_y[0:64, 0:n], nczim[k][:], him_b[k][:, 0:n], start=False, stop=False))
                fy.append(MM(ps_y[0:64, 0:n], dz_b[64:128, :], VUb[64:128, 0:n], start=False, stop=True))
                P.op("pe", fy, reads=["cz0", "cz1", "hre0", "hre1", "him0", "him1", "dz_b", "VUb"], writes=[ky])
                P.op("act", ACT(z_st[:, 0:n], ps_y[0:64, 0:n], AF.Gelu), reads=[ky], writes=["z_st"])
                P.dma("sp", self.mb_in[128:192, 2 + c0:2 + c0 + n], z_st[:, 0:n], reads=["z_st"])
                ps_gq, kgq = proj(384, 448, 64)
                P.op("act", ACT(qg_f[:, 0:n], ps_gq[0:64, 0:n], AF.Copy, scale=0.125), reads=[kgq], writes=["qg_f"])
                ps_gk, kgk = proj(448, 512, 64)
                P.op("act", ACT(kg_f[:, 0:n], ps_gk[0:64, 0:n], AF.Copy), reads=[kgk], writes=["kg_f"])
                ps_vg, kvg = proj(512, 640, 128)
                P.op("act", ACT(VGb[:, 0:n], ps_vg[:, 0:n], AF.Copy), reads=[kvg], writes=["VGb"])
                if last:
                    P.op("act", ACT(vgf[:, :], ps_vg[0:64, 128:256], AF.Copy), reads=[kvg], writes=["vgf"])
                ps_ac, kac = proj(640, 656, 16)
                P.op("act", ACT(acb[:, 0:n], ps_ac[0:16, 0:n], AF.Copy), reads=[kac], writes=["acb"])
                ps_g, kg_ = self.bank()
                P.op("pe", MM(ps_g[0:64, 0:n], wa2_b[:, :], acb[:, 0:n]), reads=["wa2_b", "acb"], writes=[kg_])
                P.op("act", ACT(lg[:, 0:n], ps_g[0:64, 0:n], AF.Exp, scale=-1.0, bias=nba[:, 0:1]), reads=[kg_, "nba"], writes=["lg"])
                P.op("act", ACT(lg[:, 0:n], lg[:, 0:n], AF.Ln, bias=1.0), reads=["lg"], writes=["lg"])
                P.op("dve", SCAN(cs[:, 0:n], self.resetm[0:64, 0:n], lg[:, 0:n], 0.0), reads=["resetm", "lg"], writes=["cs"])
                P.op("act", ACT(Ep[:, 0:n], cs[:, 0:n], AF.Exp, scale=-1.0 / 16.0), reads=["cs"], writes=["Ep"])
                P.op("act", ACT(Em[:, 0:n], cs[:, 0:n], AF.Exp, scale=1.0 / 16.0), reads=["cs"], writes=["Em"])
                P.op("dve", TT(qin_b[:, 0:n], qg_f[:, 0:n], Ep[:, 0:n], ALU.mult), reads=["qg_f", "Ep"], writes=["qin_b"])
                P.op("dve", TT(kout_f[:, 0:n], kg_f[:, 0:n], Em[:, 0:n], ALU.mult), reads=["kg_f", "Em"], writes=["kout_f"])
                P.op("act", ACT(kout_b[:, 0:n], kout_f[:, 0:n], AF.Copy), reads=["kout_f"], writes=["kout_b"])
                for i in range(nblk):
                    gb = 4 * t + i
                    if gb == NB - 1:
                        continue
                    ch = slice(i * 128, (i + 1) * 128)
                    eend = Ep[:, i * 128 + 127:i * 128 + 128]
                    j = i % 2
                    P.op("dve", TS(kend_b[:, ch], kout_f[:, ch], eend, ALU.mult), reads=["kout_f", "Ep"], writes=["kend_b"])
                    ps_a, ka_ = self.bank()
                    P.op("pe", MM(ps_a[:, 0:128], kout_b[:, ch], qin_b[:, ch]), reads=["kout_b", "qin_b"], writes=[ka_])
                    P.op("dve", TT(attT_b[j][:], ps_a[:, 0:128], self.mask_f[:, 0, 128:256], ALU.mult), reads=[ka_, "mask_f"], writes=[f"attT{j}"])
                    pb1, kb1 = self.bankb()
                    P.op("pe", TR(pb1[:, 0:64], VGb[0:64, ch], self.ident_b[0:64, 0:64]), reads=["VGb", "ident_b"], writes=[kb1])
                    P.op("act", ACT(vtm_b[j][:], pb1[:, 0:64], AF.Copy), reads=[kb1], writes=[f"vtm{j}"])
                    pb2, kb2 = self.bankb()
                    P.op("pe", TR(pb2[:, 0:64], kend_b[:, ch], self.ident_b[0:64, 0:64]), reads=["kend_b", "ident_b"], writes=[kb2])
                    P.op("dve", CP(kend_tm[j][:], pb2[:, 0:64]), reads=[kb2], writes=[f"kendtm{j}"])
                    ps_o, ko_ = self.bank()
                    P.op("pe", [MM(ps_o[0:64, 0:128], vtm_b[j][:], attT_b[j][:], start=True, stop=False),
                                MM(ps_o[0:64, 0:128], S_b[:], qin_b[:, ch], start=False, stop=True)],
                         reads=[f"vtm{j}", f"attT{j}", "S_b", "qin_b"], writes=[ko_])
                    P.op("act", ACT(o_st[:, ch], ps_o[0:64, 0:128], AF.Copy), reads=[ko_], writes=["o_st"])
                    ps_kv, kkv = self.bank()
                    P.op("pe", MM(ps_kv[0:64, 0:64], kend_tm[j][:], vtm_b[j][:]), reads=[f"kendtm{j}", f"vtm{j}"], writes=[kkv])
                    P.op("dve", STT(S_f[:], S_f[:], eend, ps_kv[0:64, 0:64], ALU.mult, ALU.add), reads=["S_f", "Ep", kkv], writes=["S_f"])
                    P.op("act", ACT(S_b[:], S_f[:], AF.Copy), reads=["S_f"], writes=["S_b"])
                P.dma("sp", self.mb_in[192:256, 2 + c0:2 + c0 + npr], o_st[:, 0:npr], reads=["o_st"])
                P.dma("sp", self.mb_in[256:320, 2 + c0:2 + c0 + n], VGb[64:128, 0:n], reads=["VGb"])
                if last:
                    P.dma("sp", O["glap_out"][l, hg], S_f[:], reads=["S_f"])
                    self.p1_samples(es, l, hg, qT, kT, kf, vf, Vtm, sinkn, qg_f, kg_f, lg, vgf)
                else:
                    if 'carry' in DBG_SKIP:
                        continue
                    P.op("dve", CP(kT[:, 0:128], kT[:, n:n + 128]), reads=["kT"], writes=["kT"])
                    P.op("dve", CP(Vtm[:, 0, :], Vtm[:, nblk, :]), reads=["Vtm"], writes=["Vtm"])
            P.barrier()

    def p1_samples(self, es, l, hg, qT, kT, kf, vf, Vtm, sinkn, qg_f, kg_f, lg, vgf):
        P, I, O = self.P, self.I, self.O
        kvh = hg // 2
        T = lambda n, s, d: self.T(es, n, s, d)
        sc = slice(128, 256)
        GS = G - 128
        prod = T("s_prod", [128, 128], F32)
        pself = T("s_pself", [128, 2], F32)
        Dg = T("s_Dg", [128, 2, 128], BF16)
        Ps = T("s_Ps", [128, 256], BF16)
        Kc = [T("s_Kc", [128, 8, 128], BF16) for _ in range(2)]
        Vc = [T("s_Vc", [128, 8, 65], BF16) for _ in range(2)]
        sC = T("s_sC", [65, 256], F32)
        tot = T("s_tot", [65, 256], F32)
        rd = T("s_rd", [64, 256], F32)
        oas = T("s_oas", [64, 256], BF16)
        P.op("dve", TT(prod[:], qT[:, sc], kT[:, 256:384], ALU.mult), reads=["qT", "kT"], writes=["s_prod"])
        ps_ss, kss = self.bank()
        P.op("pe", MM(ps_ss[:, 0:2], prod[:], self.blk2[:]), reads=["s_prod", "blk2"], writes=[kss])
        for hh in range(2):
            P.op("act", ACT(pself[:, hh:hh + 1], ps_ss[:, hh:hh + 1], AF.Exp, scale=0.125, bias=sinkn[:, hh:hh + 1]), reads=[kss, "sinkn"], writes=["s_pself"])
            P.op("dve", TS(Dg[:, hh, :], self.ident_f[:], pself[:, hh:hh + 1], ALU.mult), reads=["ident_f", "s_pself"], writes=["s_Dg"])
        for j in range(2):
            P.op("dve", MEMSET(Vc[j][:, :, 64:65], 1.0), writes=[f"s_Vc{j}"])
        ps_A, kA = self.bank()
        for g8 in range(16):
            j = g8 % 2
            b0 = g8 * 8
            for half in range(2):
                P.dma("sp", Kc[j][64 * half:64 * half + 64], self.Wb["ckT"][l, kvh][:, b0:b0 + 8, :], writes=[f"s_Kc{j}"])
            fns = []
            for bi in range(8):
                b = b0 + bi
                for hh in range(2):
                    pb = 64 * hh
                    fns.append(MM(ps_A[:, hh * 128 + b:hh * 128 + b + 1], Kc[j][pb:pb + 64, bi, :], qT[pb:pb + 64, 128 + b:129 + b]))
            P.op("pe", fns, reads=[f"s_Kc{j}", "qT"], writes=[kA])
        for hh in range(2):
            P.op("act", ACT(Ps[:, hh * 128:(hh + 1) * 128], ps_A[:, hh * 128:(hh + 1) * 128], AF.Exp, scale=0.125, bias=sinkn[:, hh:hh + 1]), reads=[kA, "sinkn"], writes=["s_Ps"])
        ps_B, kB = self.bank()
        for g8 in range(16):
            j = g8 % 2
            b0 = g8 * 8
            P.dma("sp", Vc[j][:, :, 0:64], self.Wb["cvS"][l, kvh][:, b0:b0 + 8, :], writes=[f"s_Vc{j}"])
            fns = [MM(ps_B[0:65, b0 + bi:256:128], Vc[j][:, bi, :], Ps[:, b0 + bi:256:128]) for bi in range(8)]
            P.op("pe", fns, reads=[f"s_Vc{j}", "s_Ps"], writes=[kB])
        ps_C, kC = self.bank()
        P.op("pe", [MM(ps_C[0:65, hh * 128:(hh + 1) * 128], Vtm[:, 2, :], Dg[:, hh, :]) for hh in range(2)], reads=["Vtm", "s_Dg"], writes=[kC])
        P.op("act", ACT(sC[:], ps_C[0:65, 0:256], AF.Copy), reads=[kC], writes=["s_sC"])
        P.op("dve", TT(tot[:], ps_B[0:65, 0:256], sC[:], ALU.add), reads=[kB, "s_sC"], writes=["s_tot"])
        ps_D, kD = self.bank()
        P.op("pe", MM(ps_D[0:64, 0:256], self.sel65[0:65, :], tot[:]), reads=["sel65", "s_tot"], writes=[kD])
        P.op("dve", TS(rd[:], ps_D[0:64, 0:256], 1.0, ALU.add), reads=[kD], writes=["s_rd"])
        P.op("dve", RECIP(rd[:], rd[:]), reads=["s_rd"], writes=["s_rd"])
        P.op("dve", TT(oas[:], tot[0:64, :], rd[:], ALU.mult), reads=["s_tot", "s_rd"], writes=["s_oas"])
        for hh in range(2):
            P.dma("sp", self.mb_in[64 * hh:64 * hh + 64, 2 + GS:2 + G], oas[:, hh * 128:(hh + 1) * 128], reads=["s_oas"])
        if hg % 2 == 0:
            ktm = T("s_ktm", [128, 4, 64], F32)
            for vi, (src, skey, nm) in enumerate(((kf, "kf", "k"), (vf, "vf", "v"))):
                for i in range(2):
                    pst, kst = self.bank()
                    P.op("pe", TR(pst[:, 0:64], src[0:64, i * 128:(i + 1) * 128], self.ident_f[0:64, 0:64]), reads=[skey, "ident_f"], writes=[kst])
                    P.op("act", ACT(ktm[:, 2 * vi + i, :], pst[:, 0:64], AF.Copy), reads=[kst], writes=["s_ktm"])
                P.dma("sp", O[f"{nm}p_out"][l, kvh], ktm[:, 2 * vi, :], reads=["s_ktm"])
                P.dma("sp", O[f"{nm}s_out"][l, kvh, :, 127, :], ktm[:, 2 * vi + 1, :], reads=["s_ktm"])
                P.dma("sp", O[f"{nm}s_out"][l, kvh, :, 0:127, :], I["ckN" if nm == "k" else "cvN"][l, kvh, :, 1:128, :])
        Eps = T("g_Eps", [64, 128], F32)
        qins = T("g_qins", [64, 128], BF16)
        prodg = T("g_prodg", [64, 128], F32)
        vts = T("g_vts", [128, 64], F32)
        S0f = [T("g_S0f", [64, 16, 64], F32) for _ in range(2)]
        S0b = [T("g_S0b", [64, 16, 64], BF16) for _ in range(2)]
        Vrep = [T("g_Vrep", [64, 16, 64], F32) for _ in range(2)]
        Snew = [T("g_Snew", [64, 16, 64], F32) for _ in range(2)]
        tkv = T("g_tkv", [64, 64], F32)
        osb = T("g_osb", [64, 128], BF16)
        otmp = T("g_otmp", [64, 128], F32)
        P.op("act", ACT(Eps[:], lg[:, sc], AF.Exp, scale=-1.0 / 16.0), reads=["lg"], writes=["g_Eps"])
        P.op("dve", TT(qins[:], qg_f[:, sc], Eps[:], ALU.mult), reads=["qg_f", "g_Eps"], writes=["g_qins"])
        P.op("dve", TT(prodg[:], qg_f[:, sc], kg_f[:, sc], ALU.mult), reads=["qg_f", "kg_f"], writes=["g_prodg"])
        ps_qk, kqk = self.bank()
        P.op("pe", MM(ps_qk[0:64, 0:128], self.ones_f[0:64, 0:64], prodg[:]), reads=["ones_f", "g_prodg"], writes=[kqk])
        pst, kst = self.bank()
        P.op("pe", TR(pst[:, 0:64], vgf[:, :], self.ident_f[0:64, 0:64]), reads=["vgf", "ident_f"], writes=[kst])
        P.op("act", ACT(vts[:], pst[:, 0:64], AF.Copy), reads=[kst], writes=["g_vts"])
        P.dma("sp", self.vs_scr[:, :], vts[:], reads=["g_vts"], writes=["vs_scr"])
        ps_os, kos = self.bank()
        NQ = 16
        for qd in range(128 // NQ):
            j = qd % 2
            bq = slice(qd * NQ, qd * NQ + NQ)
            P.dma("sp", S0f[j][:], I["gst"][l, hg][:, bq, :], writes=[f"g_S0f{j}"])
            P.dma("sp", Vrep[j][:].rearrange("p b e -> p (b e)"),
                  self.vs_scr.rearrange("(o b) e -> o (b e)", o=128 // NQ)[qd:qd + 1, :].partition_broadcast(64),
                  reads=["vs_scr"], writes=[f"g_Vrep{j}"])
            P.op("act", ACT(S0b[j][:], S0f[j][:], AF.Copy), reads=[f"g_S0f{j}"], writes=[f"g_S0b{j}"])
            P.op("pe", [MM(ps_os[0:64, qd * NQ + b:qd * NQ + b + 1], S0b[j][:, b, :], qins[:, qd * NQ + b:qd * NQ + b + 1]) for b in range(NQ)],
                 reads=[f"g_S0b{j}", "g_qins"], writes=[kos])
            for b in range(NQ):
                col = 128 + qd * NQ + b
                P.op("dve", TS(tkv[:], Vrep[j][:, b, :], kg_f[:, col:col + 1], ALU.mult), reads=[f"g_Vrep{j}", "kg_f"], writes=["g_tkv"])
                P.op("dve", STT(Snew[j][:, b, :], S0f[j][:, b, :], Eps[:, qd * NQ + b:qd * NQ + b + 1], tkv[:], ALU.mult, ALU.add),
                     reads=[f"g_S0f{j}", "g_Eps", "g_tkv"], writes=[f"g_Snew{j}"])
            P.dma("sp", O["glas_out"][l, hg][:, bq, :], Snew[j][:], reads=[f"g_Snew{j}"])
        P.op("dve", TT(otmp[:], vgf[:, :], ps_qk[0:64, 0:128], ALU.mult), reads=["vgf", kqk], writes=["g_otmp"])
        P.op("dve", TT(osb[:], otmp[:], ps_os[0:64, 0:128], ALU.add), reads=["g_otmp", kos], writes=["g_osb"])
        P.dma("sp", self.mb_in[192:256, 2 + GS:2 + G], osb[:], reads=["g_osb"])

    @staticmethod
    def blocks(total, off=0):
        nb = (total + 511) // 512
        base, rem = total // nb, total % nb
        out, a = [], off
        for i in range(nb):
            w = base + (1 if i < rem else 0)
            out.append((a, a + w))
            a += w
        return out

    def p2(self, l, last):
        P, I, O = self.P, self.I, self.O
        n = CPC + 2
        if l == 0 and self.pending:
            self.pump(1.0, None)
        blksH = self.blocks(n)
        blksO = self.blocks(CPC)
        pid = self.nc.sync.partition_id()
        P.dma("sp", self.mb_loc[:, :], self.mbufs[l][:, bass.ds(pid * CPC, n)], reads=[f"mbuf{l}"], writes=["mb_loc"])
        mb = self.mb_loc.rearrange("(r q) c -> r q c", r=NHG)
        mkey = "mb_loc"
        with ExitStack() as es:
            T = lambda nm, sh, d: self.T(es, nm, sh, d)
            xr = T("xr", [128, 16, n], F32)
            hb = T("halo_b", [128, 16, 2], BF16)
            P.dma("sp", xr[:, :, 2:n], chunked(self.xres_loc), reads=["xres_loc"], writes=["xr"])
            xg2 = self.xg[l].rearrange("(rk p) n -> p rk n", p=128)
            P.dma("sp", hb[:], xg2[:, bass.ds(((pid + 7) % 8) * 16, 16), CPC - 2:CPC], reads=[f"xg{l}"], writes=["halo_b"])
            P.op("act", ACT(xr[:, :, 0:2], hb[:], AF.Copy), reads=["halo_b"], writes=["xr"])
            gb1 = T("gb1", [128, 2, 16], F32)
            gb2 = T("gb2", [128, 2, 16], F32)
            cw = T("cw", [128, 4, NFC], F32)
            bgl = T("bgl", [128, 4], F32)
            gno = T("gno", [128, 4], F32)
            P.dma("sp", gb1[:], I["ln1"][l].rearrange("t p k -> p t k"), writes=["gb1"])
            P.dma("sp", gb2[:], I["ln2"][l].rearrange("t p k -> p t k"), writes=["gb2"])
            P.dma("sp", cw[:], I["convw"][l].rearrange("t p k -> p t k"), writes=["cw"])
            P.dma("sp", bgl[:], I["bglu"][l], writes=["bgl"])
            P.dma("sp", gno[:], I["gnorm"][l], writes=["gno"])
            P.dma("sp", O["convs_out"][l, 0], I["cprevT"][l, 1])
            with ExitStack() as esa:
                Ta = lambda nm, sh, d: self.T(esa, nm, sh, d)
                oaT = Ta("oaT", [128, 8, n], BF16)
                zT = Ta("zT", [128, 4, n], BF16)
                oT = Ta("oT", [128, 4, n], BF16)
                gT = Ta("gT", [128, 4, n], BF16)
                obT = Ta("obT", [128, 4, n], BF16)
                wgl = Ta("wgl", [128, 4, 512], BF16)
                tA = Ta("tA", [128, 512], F32)
                tB = Ta("tB", [128, 512], F32)
                tCb = Ta("tCb", [128, 512], BF16)
                wo = [Ta("wo", [128, 16, 128], BF16) for _ in range(2)]
                cols = slice(0, n)
                for r in range(NHG):
                    hp = slice(64 * (r % 2), 64 * (r % 2) + 64)
                    P.dma("sp", oaT[:, r, :], mb[r, 0:128, cols], reads=[mkey], writes=["oaT"])
                    P.dma("sp", zT[hp, r // 2, :], mb[r, 128:192, cols], reads=[mkey], writes=["zT"])
                    P.dma("sp", oT[hp, r // 2, :], mb[r, 192:256, cols], reads=[mkey], writes=["oT"])
                    P.dma("sp", gT[hp, r // 2, :], mb[r, 256:320, cols], reads=[mkey], writes=["gT"])
                P.dma("sp", wgl[:], chunked(self.Wb["wglu"][l]), reads=self.wb_keys[("wglu", l)], writes=["wgl"])
                for oc in range(4):
                    for (b0, b1) in blksH:
                        nb = b1 - b0
                        ps, kp = self.bank()
                        P.op("pe", [MM(ps[:, 0:nb], wgl[:, k, oc * 128:(oc + 1) * 128], zT[:, k, b0:b1], start=(k == 0), stop=(k == 3)) for k in range(4)],
                             reads=["wgl", "zT"], writes=[kp])
                        P.op("act", ACT(tA[:, 0:nb], ps[:, 0:nb], AF.Sigmoid, bias=bgl[:, oc:oc + 1]), reads=[kp, "bgl"], writes=["tA"])
                        P.op("dve", TT(obT[:, oc, b0:b1], zT[:, oc, b0:b1], tA[:, 0:nb], ALU.mult), reads=["zT", "tA"], writes=["obT"])
                for h in range(4):
                    for (b0, b1) in blksH:
                        nb = b1 - b0
                        P.op("act", ACT(tCb[:, 0:nb], oT[:, h, b0:b1], AF.Square), reads=["oT"], writes=["tCb"])
                        ps, kp = self.bank()
                        P.op("pe", MM(ps[:, 0:nb], self.ones_b[:], tCb[:, 0:nb]), reads=["ones_b", "tCb"], writes=[kp])
                        P.op("act", ACT(tA[:, 0:nb], ps[:, 0:nb], AF.Sqrt, scale=1.0 / 128.0, bias=EPS), reads=[kp], writes=["tA"])
                        P.op("dve", RECIP(tA[:, 0:nb], tA[:, 0:nb]), reads=["tA"], writes=["tA"])
                        P.op("dve", TT(tA[:, 0:nb], oT[:, h, b0:b1], tA[:, 0:nb], ALU.mult), reads=["oT", "tA"], writes=["tA"])
                        P.op("act", ACT(tB[:, 0:nb], gT[:, h, b0:b1], AF.Silu), reads=["gT"], writes=["tB"])
                        P.op("dve", STT(oT[:, h, b0:b1], tA[:, 0:nb], gno[:, h:h + 1], tB[:, 0:nb], ALU.mult, ALU.mult), reads=["tA", "gno", "tB"], writes=["oT"])
                for oc in range(16):
                    w = wo[oc % 2]
                    wk = f"wo{oc % 2}"
                    P.dma("sp", w[:], chunked(self.Wb["wout"][l][:, oc * 128:(oc + 1) * 128]), reads=self.wb_keys[("wout", l)], writes=[wk])
                    for (b0, b1) in blksH:
                        nb = b1 - b0
                        ps, kp = self.bank()
                        fns = []
                        for k in range(16):
                            rhs = oaT[:, k, b0:b1] if k < 8 else (obT[:, k - 8, b0:b1] if k < 12 else oT[:, k - 12, b0:b1])
                            fns.append(MM(ps[:, 0:nb], w[:, k, :], rhs, start=(k == 0), stop=(k == 15)))
                        P.op("pe", fns, reads=[wk, "oaT", "obT", "oT"], writes=[kp])
                        P.op("dve", STT(xr[:, oc, b0:b1], xr[:, oc, b0:b1], ALPHA, ps[:, 0:nb], ALU.mult, ALU.add), reads=["xr", kp], writes=["xr"])
                P.barrier()
            x1b = T("x1b", [128, 16, n], BF16)
            with ExitStack() as esl:
                self.layernorm(esl, xr, "xr", n, gb1, "gb1", x1b, "x1b", mask=self.cmask[:, 0:n])
                P.barrier()
            for k in range(16):
                P.op("act", ACT(xr[:, k, :], xr[:, k, :], AF.Copy, scale=ALPHA), reads=["xr"], writes=["xr"])
            with ExitStack() as esf:
                Tf = lambda nm, sh, d: self.T(esf, nm, sh, d)
                hT = Tf("hT", [128, 11, CPC], BF16)
                U = [Tf("U", [128, n], F32) for _ in range(2)]
                cc = [Tf("cc", [128, CPC], F32) for _ in range(2)]
                e01 = [Tf("e01", [128, 2, 128], F32) for _ in range(2)]
                wu = [Tf("wu", [128, 16, 128], BF16) for _ in range(2)]
                wg = [Tf("wg", [128, 16, 128], BF16) for _ in range(2)]
                wd = [Tf("wd", [128, 11, 128], BF16) for _ in range(2)]
                cpv = [Tf("cpv", [128, 2, 128], F32) for _ in range(2)]
                npz = CPC - 128
                sc = slice(npz, CPC)
                for grp in range(4):
                    for fi in range(11):
                        f = grp * 11 + fi
                        j = f % 2
                        P.dma("sp", wu[j][:], chunked(self.Wb["wup"][l][:, f * 128:(f + 1) * 128]), reads=self.wb_keys[("wup", l)], writes=[f"wu{j}"])
                        P.dma("sp", wg[j][:], chunked(self.Wb["wup"][l][:, DFF + f * 128:DFF + (f + 1) * 128]), reads=self.wb_keys[("wup", l)], writes=[f"wg{j}"])
                        P.dma("sp", cpv[j][:], I["cprevT"][l][:, f * 128:(f + 1) * 128, :].rearrange("t p b -> p t b"), writes=[f"cpv{j}"])
                        Uj, uk = U[j], f"U{j}"
                        cj, ck_ = cc[j], f"cc{j}"
                        if self.pending:
                            self.pump(1.0 / 40 if f < NFC - 1 else 1.0, getattr(self, "pace_tok", None))
                        for (b0, b1) in blksH:
                            nb = b1 - b0
                            ps, kp = self.bank()
                            P.op("pe", [MM(ps[:, 0:nb], wu[j][:, k, :], x1b[:, k, b0:b1], start=(k == 0), stop=(k == 15)) for k in range(16)],
                                 reads=[f"wu{j}", "x1b"], writes=[kp])
                            P.op("act", ACT(Uj[:, b0:b1], ps[:, 0:nb], AF.Copy), reads=[kp], writes=[uk])
                        gps = []
                        for (b0, b1) in blksO:
                            nb = b1 - b0
                            ps2, kp2 = self.bank()
                            P.op("pe", [MM(ps2[:, 0:nb], wg[j][:, k, :], x1b[:, k, 2 + b0:2 + b1], start=(k == 0), stop=(k == 15)) for k in range(16)],
                                 reads=[f"wg{j}", "x1b"], writes=[kp2])
                            gps.append((ps2, kp2))
                        w0, w1, w2, cb = (cw[:, t_, f:f + 1] for t_ in range(4))
                        P.op("dve", TS(cj[:, 0:CPC], Uj[:, 0:CPC], w0, ALU.mult, cb, ALU.add), reads=[uk, "cw"], writes=[ck_])
                        P.op("dve", STT(cj[:, 0:CPC], Uj[:, 1:CPC + 1], w1, cj[:, 0:CPC], ALU.mult, ALU.add), reads=[uk, "cw", ck_], writes=[ck_])
                        P.op("dve", STT(cj[:, 0:CPC], Uj[:, 2:CPC + 2], w2, cj[:, 0:CPC], ALU.mult, ALU.add), reads=[uk, "cw", ck_], writes=[ck_])
                        ej, ek_ = e01[j], f"e01{j}"
                        P.op("dve", STT(ej[:, 0, :], Uj[:, npz:npz + 128], self.nsm[:, 0:1], cpv[j][:, 0, :], ALU.mult, ALU.add), reads=[uk, "nsm", f"cpv{j}"], writes=[ek_])
                        P.op("dve", STT(ej[:, 1, :], Uj[:, npz + 1:npz + 129], self.nsm[:, 0:1], cpv[j][:, 1, :], ALU.mult, ALU.add), reads=[uk, "nsm", f"cpv{j}"], writes=[ek_])
                        P.op("dve", TS(cj[:, sc], ej[:, 0, :], w0, ALU.mult, cb, ALU.add), reads=[ek_, "cw", ck_], writes=[ck_])
                        P.op("dve", STT(cj[:, sc], ej[:, 1, :], w1, cj[:, sc], ALU.mult, ALU.add), reads=[ek_, "cw", ck_], writes=[ck_])
                        P.op("dve", STT(cj[:, sc], Uj[:, 2 + npz:2 + CPC], w2, cj[:, sc], ALU.mult, ALU.add), reads=[uk, "cw", ck_], writes=[ck_])
                        P.dma("sp", O["convp_out"][l, :, f, :], Uj[:, npz:npz + 2], reads=[uk])
                        P.dma("sp", O["convs_out"][l, 1, f * 128:(f + 1) * 128, :], Uj[:, 2 + npz:2 + CPC], reads=[uk])
                        self.pace_tok = P.op("act", ACT(cj[:, 0:CPC], cj[:, 0:CPC], AF.Gelu), reads=[ck_], writes=[ck_])
                        for bi, (b0, b1) in enumerate(blksO):
                            ps2, kp2 = gps[bi]
                            P.op("dve", TT(hT[:, fi, b0:b1], cj[:, b0:b1], ps2[:, 0:b1 - b0], ALU.mult), reads=[ck_, kp2], writes=["hT"])
                    for oc in range(16):
                        j = oc % 2
                        P.dma("sp", wd[j][:], chunked(self.Wb["wdown"][l][grp * 1408:(grp + 1) * 1408, oc * 128:(oc + 1) * 128]), reads=self.wb_keys[("wdown", l)], writes=[f"wd{j}"])
                        for (b0, b1) in blksO:
                            nb = b1 - b0
                            ps, kp = self.bank()
                            P.op("pe", [MM(ps[:, 0:nb], wd[j][:, fi, :], hT[:, fi, b0:b1], start=(fi == 0), stop=(fi == 10)) for fi in range(11)],
                                 reads=[f"wd{j}", "hT"], writes=[kp])
                            P.op("dve", TT(xr[:, oc, 2 + b0:2 + b1], xr[:, oc, 2 + b0:2 + b1], ps[:, 0:nb], ALU.add), reads=["xr", kp], writes=["xr"])
                P.barrier()
            xo = xr[:, :, 2:n]
            with ExitStack() as eso:
                To = lambda nm, sh, d: self.T(eso, nm, sh, d)
                if not last:
                    self.layernorm(eso, xo, "xr", CPC, gb2, "gb2", x1b, "x1b", mask=self.cmask[:, 2:n])
                    P.dma("sp", chunked(self.xres_loc), xo, reads=["xr"], writes=["xres_loc"])
                    P.dma("sp", chunked(self.xn_in), x1b[:, :, 0:CPC], reads=["x1b"], writes=["xn_in"])
                    P.barrier()
                    P.collective("AllGather", self.xn_in, self.xg[l + 1], reads=["xn_in"], writes=[f"xg{l + 1}"])
                else:
                    self.layernorm(eso, xo, "xr", CPC, gb2, "gb2")
                    yt = [To("yt", [128, D], F32) for _ in range(2)]
                    nfull = CPC // 128
                    for jb in range((CPC + 127) // 128):
                        w = min(128, CPC - jb * 128)
                        y_, yk = yt[jb % 2], f"yt{jb % 2}"
                        for kg in range(4):
                            ps, kp = self.bank()
                            for kk in range(4):
                                P.op("pe", TR(ps[0:w, kk * 128:(kk + 1) * 128], xo[:, 4 * kg + kk, jb * 128:jb * 128 + w], self.ident_f[:]),
                                     reads=["xr", "ident_f"], writes=[kp])
                            P.op("act", ACT(y_[0:w, kg * 512:(kg + 1) * 512], ps[0:w, :], AF.Copy), reads=[kp], writes=[yk])
                        P.dma("sp", O["y_out"][jb * 128:jb * 128 + w, :], y_[0:w, :], reads=[yk])
                P.barrier()


SHARED = ["ln_in", "rope", "pm", "swamask", "ident", "wout", "wup", "wdown", "wglu", "ln1", "ln2", "convw", "bglu", "gnorm"]


def _percore(d, c):
    f = np.ascontiguousarray
    e = {k: d[k] for k in SHARED}
    e["xT"] = f(d["xT"][:, c * CPC:(c + 1) * CPC])
    for k in ("win_all", "sink", "s5col", "s5row", "s5bz", "s5cz", "s5dz", "wa2", "ba", "gst"):
        e[k] = f(d[k][:, c:c + 1])
    for k in ("ckT", "cvS", "ckN", "cvN"):
        e[k] = f(d[k][:, c // 2:c // 2 + 1])
    e["ssm0"] = f(d["ssm0"][:, :, 2 * c:2 * c + 2])
    e["cprevT"] = d["cprevT"] if c == NCORES - 1 else np.zeros_like(d["cprevT"])
    cm = np.zeros((1, CPC + 2), np.float32)
    g = c * CPC - 2 + np.arange(CPC + 2)
    cm[0, g >= NPAD] = 1.0
    e["cmask"] = cm
    e["nsm"] = np.full((128, 1), 0.0 if c == NCORES - 1 else 1.0, np.float32)
    return e


def _assemble(rs, inp):
    B = 128
    cat = lambda k, ax: np.concatenate([r[k] for r in rs], axis=ax)
    y = cat("y_out", 0)
    y_prompt = y[128:G - 128][None]
    y_sample = y[G - 128:][:, None, :]
    ev = [rs[c] for c in range(0, NCORES, 2)]
    kp = np.concatenate([r["kp_out"] for r in ev], 1).transpose(0, 2, 1, 3)[:, None]
    vp = np.concatenate([r["vp_out"] for r in ev], 1).transpose(0, 2, 1, 3)[:, None]
    ks = np.concatenate([r["ks_out"] for r in ev], 1).transpose(0, 2, 3, 1, 4)
    vs = np.concatenate([r["vs_out"] for r in ev], 1).transpose(0, 2, 3, 1, 4)
    sp = cat("ssmp_out", 2).reshape(L, 2, 16, 2, 64).reshape(L, 2, 32, 64)
    ssm_re_p, ssm_im_p = sp[:, 0][:, None], sp[:, 1][:, None]
    ss = cat("ssms_out", 2).transpose(0, 1, 3, 2, 4).reshape(L, 2, B, 32, 64)
    ssm_re_s, ssm_im_s = ss[:, 0], ss[:, 1]
    gp = cat("glap_out", 1).reshape(L, 4, 2, 64, 64).transpose(0, 1, 3, 2, 4).reshape(L, 4, 64, 128)[:, None]
    gs = cat("glas_out", 1).reshape(L, 4, 2, 64, B, 64).transpose(0, 4, 1, 3, 2, 5).reshape(L, B, 4, 64, 128)
    rl = rs[NCORES - 1]
    conv_p = rl["convp_out"].transpose(0, 3, 2, 1).reshape(L, 2, DFF)[:, None]
    conv_s = rl["convs_out"].transpose(0, 3, 1, 2)
    outs = (y_prompt, y_sample, kp, vp, ssm_re_p, ssm_im_p, gp, conv_p, ks, vs, ssm_re_s, ssm_im_s, gs, conv_s)
    return tuple(np.ascontiguousarray(o, dtype=np.float32) for o in outs)


def kernel(**inp):
    d = _prep_inputs(inp)
    in_maps = [_percore(d, c) for c in range(NCORES)]
    shapes = {k: v.shape for k, v in in_maps[0].items()}
    b = Builder(shapes)
    nc = b.build()
    res = run_bass_kernel_spmd(nc, in_maps, core_ids=list(range(NCORES)))
    return _assemble(res.results, inp)
```

```python
import math
from contextlib import ExitStack
import numpy as np
import ml_dtypes
import concourse.bass as bass
import concourse.mybir as mybir
from concourse.bass_utils import run_bass_kernel_spmd

F32 = mybir.dt.float32
BF16 = mybir.dt.bfloat16
I32 = mybir.dt.int32
AF = mybir.ActivationFunctionType
ALU = mybir.AluOpType

D = 2048
L = 2
NB = 66
G = NB * 128
NPAD = 112
NCORES = 8
DFF = 5632
NFC = DFF // 128
NHG = 8
NWC = 656
W1 = 512
W2 = 1024
CPC = G // NCORES
ALPHA = float((2 * L) ** 0.25)
EPS = 1e-5
TWO_PI = 2.0 * math.pi
C1 = 6.28125
C2 = TWO_PI - C1
BROWS = 320
import os
DBG_SKIP = set(os.environ.get('DBG_SKIP', '').split(','))


class Prog:
    EPOCH = 4000
    NSLOT = 8
    DMA_EPOCH = 1800

    def __init__(self, nc, es, same_engine_sync=True):
        self.nc = nc
        self.es = es
        self.eng = {"pe": nc.tensor, "act": nc.scalar, "dve": nc.vector,
                    "pool": nc.gpsimd, "sp": nc.sync}
        self.cnt = {e: 0 for e in self.eng}
        self.csem = {}
        self.dsem = {}
        self.dcount = {}
        self.dgen = {}
        self.dslot_next = {e: 0 for e in self.eng}
        self.waited = {e: {} for e in self.eng}
        self.lastw = {}
        self.readers = {}
        self.same_engine_sync = same_engine_sync
        self.live_dma = {}

    def _newsem(self, name):
        return self.es.enter_context(self.nc.semaphore(name))

    def _csem(self, e, ep):
        k = (e, ep)
        if k not in self.csem:
            self.csem[k] = self._newsem(f"c_{e}_{ep}")
        return self.csem[k]

    def _emit_wait(self, x, tok):
        w = self.waited[x]
        if tok[0] == "x":
            if w.get(tok, 0):
                return
            w[tok] = 1
            self.eng[x].wait_ge(self.xsem[tok[1]], 1)
            return
        if tok[0] == "c":
            _, e, n = tok
            if e == x and (e == "pe" or not self.same_engine_sync):
                return
            key = ("c", e)
            if w.get(key, 0) >= n:
                return
            w[key] = n
            idx = n - 1
            self.eng[x].wait_ge(self._csem(e, idx // self.EPOCH), idx % self.EPOCH + 1)
        else:
            _, q, slot, gen, k = tok
            key = ("d", q, slot, gen)
            if w.get(key, 0) >= k:
                return
            w[key] = k
            self.eng[x].wait_ge(self.dsem[(q, slot, gen)], 16 * k)

    def _deps(self, reads, writes):
        deps = []
        for r in reads:
            if r in self.lastw:
                deps.append(self.lastw[r])
        for wv in writes:
            if wv in self.lastw:
                deps.append(self.lastw[wv])
            deps.extend(self.readers.get(wv, {}).values())
        return deps

    def _commit(self, tok, reads, writes):
        for r in reads:
            d = self.readers.setdefault(r, {})
            if tok[0] == "c":
                d[("c", tok[1])] = tok
            elif tok[0] == "x":
                d[tok] = tok
            else:
                d[("d", tok[1], tok[2], tok[3])] = tok
        for wv in writes:
            self.lastw[wv] = tok
            self.readers[wv] = {}

    @staticmethod
    def _excl(reads, writes):
        ps = [r for r in reads if isinstance(r, str) and r.startswith("ps")]
        if not ps:
            return reads, writes
        return [r for r in reads if r not in ps], list(writes) + ps

    def op(self, e, fn, reads=(), writes=()):
        reads, writes = self._excl(reads, writes)
        for t in self._deps(reads, writes):
            self._emit_wait(e, t)
        self.cnt[e] += 1
        n = self.cnt[e]
        idx = n - 1
        fns = fn if isinstance(fn, (list, tuple)) else [fn]
        ins = None
        for f in fns:
            ins = f(self.eng[e])
        ins.then_inc(self._csem(e, idx // self.EPOCH), 1)
        tok = ("c", e, n)
        self._commit(tok, reads, writes)
        return tok

    def dma(self, q, out, in_, reads=(), writes=(), after=()):
        slot = self.dslot_next[q]
        self.dslot_next[q] = (slot + 1) % self.NSLOT
        gen = self.dgen.get((q, slot), 0)
        k = self.dcount.get((q, slot, gen), 0)
        if k > 0:
            self._emit_wait(q, ("d", q, slot, gen, k))
        if k >= self.DMA_EPOCH:
            gen += 1
            self.dgen[(q, slot)] = gen
            k = 0
        if (q, slot, gen) not in self.dsem:
            self.dsem[(q, slot, gen)] = self._newsem(f"d_{q}_{slot}_{gen}")
        for t in self._deps(reads, writes):
            self._emit_wait(q, t)
        for t in after:
            self._emit_wait(q, t)
        k += 1
        self.dcount[(q, slot, gen)] = k
        self.eng[q].dma_start(out=out, in_=in_).then_inc(self.dsem[(q, slot, gen)], 16)
        tok = ("d", q, slot, gen, k)
        self._commit(tok, reads, writes)
        self.live_dma[(q, slot)] = tok
        return tok

    def collective(self, kind, in_ap, out_ap, reads=(), writes=()):
        e = "pool"
        for t in self._deps(reads, writes):
            self._emit_wait(e, t)
        self.ncoll = getattr(self, "ncoll", 0) + 1
        if not hasattr(self, "xsem"):
            self.xsem = {}
        sem = self._newsem(f"coll{self.ncoll}")
        self.xsem[self.ncoll] = sem
        self.nc.gpsimd.collective_compute(kind, ALU.bypass, replica_groups=[list(range(NCORES))],
                                          ins=[in_ap.opt()], outs=[out_ap.opt()]).then_inc(sem, 1)
        tok = ("x", self.ncoll)
        self._commit(tok, reads, writes)
        return tok

    def barrier(self):
        for x in self.eng:
            for (q_, _s), t in self.live_dma.items():
                if q_ == "pool" and getattr(self, "bg_pool", False):
                    continue
                self._emit_wait(x, t)
            for ci in range(1, getattr(self, "ncoll", 0) + 1):
                self._emit_wait(x, ("x", ci))
            for y in self.eng:
                if y != x and self.cnt[y] > 0:
                    self._emit_wait(x, ("c", y, self.cnt[y]))
        for x in ("act", "dve", "pool"):
            if self.cnt[x] > 0 and self.same_engine_sync:
                self._emit_wait(x, ("c", x, self.cnt[x]))

    def finish(self, e="sp"):
        for t in self.live_dma.values():
            self._emit_wait(e, t)
        for x in self.eng:
            if self.cnt[x] > 0 and x != e:
                self._emit_wait(e, ("c", x, self.cnt[x]))


def MM(out, lhsT, rhs, start=True, stop=True):
    return lambda e: e.matmul(out, lhsT=lhsT, rhs=rhs, start=start, stop=stop)


def TR(out, in_, ident):
    return lambda e: e.transpose(out, in_, ident)


def ACT(out, in_, func, **kw):
    return lambda e: e.activation(out=out, in_=in_, func=func, **kw)


def TT(out, a, b, op):
    return lambda e: e.tensor_tensor(out=out, in0=a, in1=b, op=op)


def TS(out, a, s1, op0, s2=None, op1=None):
    if op1 is None:
        return lambda e: e.tensor_scalar(out=out, in0=a, scalar1=s1, scalar2=None, op0=op0)
    return lambda e: e.tensor_scalar(out=out, in0=a, scalar1=s1, scalar2=s2, op0=op0, op1=op1)


def STT(out, a, s, b, op0, op1):
    return lambda e: e.scalar_tensor_tensor(out=out, in0=a, scalar=s, in1=b, op0=op0, op1=op1)


def CP(out, in_):
    return lambda e: e.tensor_copy(out, in_)


def SCAN(out, d0, d1, init):
    return lambda e: e.tensor_tensor_scan(out=out, data0=d0, data1=d1, initial=init,
                                          op0=ALU.mult, op1=ALU.add)


def RECIP(out, in_):
    return lambda e: e.reciprocal(out, in_)


def MEMSET(ap, v):
    return lambda e: e.memset(ap, v)


def _rope_tables():
    pos = np.zeros(G, np.float32)
    pos[NPAD:G - 128] = np.arange(G - 128 - NPAD, dtype=np.float32)
    pos[G - 128:] = 8192.0
    half = 8
    inv = np.power(np.float32(500000.0), -np.arange(half, dtype=np.float32) / np.float32(half)).astype(np.float32)
    ang = pos[None, :] * inv[:, None]
    cos8, sin8 = np.cos(ang).astype(np.float32), np.sin(ang).astype(np.float32)
    tab = np.zeros((2, 128, G), np.float32)
    tab[0] = 1.0
    for base in (0, 64):
        tab[0, base:base + 8] = cos8
        tab[0, base + 8:base + 16] = cos8
        tab[1, base:base + 8] = sin8
        tab[1, base + 8:base + 16] = sin8
    pm = np.zeros((128, 128), np.float32)
    for base in (0, 64):
        for d in range(8):
            pm[base + d + 8, base + d] = -1.0
            pm[base + d, base + d + 8] = 1.0
    return tab, pm


def _swa_masks():
    s = np.arange(128)[:, None]
    q = np.arange(128)[None, :]
    m = np.zeros((3, 128, 2, 128), np.float32)
    prev = (s >= q).astype(np.float32)
    cur = (s <= q).astype(np.float32)
    real = (s >= NPAD).astype(np.float32)
    m[0, :, 0], m[0, :, 1] = prev, cur
    m[1, :, 0], m[1, :, 1] = 0.0, cur * real
    m[2, :, 0], m[2, :, 1] = prev * real, cur
    return np.ascontiguousarray(m.transpose(1, 0, 2, 3).reshape(128, 3, 256))


def _prep_inputs(inp):
    f = lambda a: np.ascontiguousarray(np.asarray(a, dtype=np.float32))
    d = {}
    xall = np.zeros((G, D), np.float32)
    xall[NPAD:128] = inp["meta_tokens"]
    xall[128:G - 128] = inp["x_prompt"][0]
    xall[G - 128:] = inp["x_sample"][:, 0]
    d["xT"] = f(xall.T)
    vec16 = lambda v: f(np.asarray(v).reshape(16, 128).T)
    d["ln_in"] = f(np.stack([vec16(inp["ln_in_g"]), vec16(inp["ln_in_b"])]))
    w_in = np.asarray(inp["w_in"])
    win_all = np.zeros((L, NHG, D, NWC), np.float32)
    for hg in range(NHG):
        kvh, gh, half = hg // 2, hg // 2, hg % 2
        cols = np.concatenate([
            np.arange(128 * hg, 128 * hg + 128),
            np.arange(1024 + 64 * kvh, 1024 + 64 * kvh + 64),
            np.arange(1024 + 64 * kvh, 1024 + 64 * kvh + 64),
            np.arange(1280 + 64 * kvh, 1280 + 64 * kvh + 64),
            np.arange(1536 + 64 * hg, 1536 + 64 * hg + 64),
            np.arange(2048 + 64 * gh, 2048 + 64 * gh + 64),
            np.arange(2304 + 64 * gh, 2304 + 64 * gh + 64),
            np.arange(2560 + 128 * gh + 64 * half, 2560 + 128 * gh + 64 * half + 64),
            np.arange(3072 + 128 * gh + 64 * half, 3072 + 128 * gh + 64 * half + 64),
            np.arange(3584, 3600),
        ])
        win_all[:, hg] = w_in[:, :, cols]
    d["win_all"] = win_all
    tab, pm = _rope_tables()
    d["rope"] = tab
    d["pm"] = pm
    d["swamask"] = _swa_masks()
    d["ident"] = np.eye(128, dtype=np.float32)
    sink = np.asarray(inp["attn_sink"])
    d["sink"] = f(np.broadcast_to(sink.reshape(L, NHG, 1, 2), (L, NHG, 128, 2)))
    lr, li, ldt = (np.asarray(inp[k]) for k in ("s5_lam_re", "s5_lam_im", "s5_log_dt"))
    bre, bim = np.asarray(inp["s5_b_re"]), np.asarray(inp["s5_b_im"])
    cre, cim = np.asarray(inp["s5_c_re"]), np.asarray(inp["s5_c_im"])
    dsk = np.asarray(inp["s5_d"])
    s5col = np.zeros((L, NHG, 2, 128, 3), np.float32)
    s5row = np.zeros((L, NHG, 2, 64, 3, 128), np.float32)
    s5bz = np.zeros((L, NHG, 2, 64, 2, 128), np.float32)
    s5cz = np.zeros((L, NHG, 2, 128, 2, 64), np.float32)
    s5dz = np.zeros((L, NHG, 64, 64), np.float32)
    for l in range(L):
        for hg in range(NHG):
            for k in range(2):
                for j in range(2):
                    g = 4 * hg + 2 * k + j
                    gl = 2 * k + j
                    sl = slice(64 * j, 64 * j + 64)
                    s5col[l, hg, k, sl, 0] = lr[l, g]
                    s5col[l, hg, k, sl, 1] = li[l, g]
                    s5col[l, hg, k, sl, 2] = ldt[l, g]
                    s5row[l, hg, k, :, 0, sl] = lr[l, g][None, :]
                    s5row[l, hg, k, :, 1, sl] = li[l, g][None, :]
                    s5row[l, hg, k, :, 2, sl] = ldt[l, g]
                    s5bz[l, hg, k, 16 * gl:16 * gl + 16, 0, sl] = bre[l, g].T
                    s5bz[l, hg, k, 16 * gl:16 * gl + 16, 1, sl] = bim[l, g].T
                    s5cz[l, hg, k, sl, 0, 16 * gl:16 * gl + 16] = cre[l, g].T
                    s5cz[l, hg, k, sl, 1, 16 * gl:16 * gl + 16] = cim[l, g].T
            s5dz[l, hg][np.arange(64), np.arange(64)] = dsk[l, 4 * hg:4 * hg + 4].reshape(64)
    d["s5col"], d["s5row"], d["s5bz"], d["s5cz"], d["s5dz"] = s5col, s5row, s5bz, s5cz, s5dz
    wa2 = np.asarray(inp["gla_w_a2"])
    ba = np.asarray(inp["gla_b_a"])
    d["wa2"] = f(np.stack([[wa2[l][:, 64 * (hg // 2):64 * (hg // 2) + 64] for hg in range(NHG)] for l in range(L)]))
    d["ba"] = f(np.stack([[ba[l][64 * (hg // 2):64 * (hg // 2) + 64].reshape(64, 1) for hg in range(NHG)] for l in range(L)]))
    ck, cv = np.asarray(inp["cache_swa_k"]), np.asarray(inp["cache_swa_v"])
    d["ckT"] = f(ck.transpose(0, 3, 4, 1, 2))
    d["cvS"] = f(cv.transpose(0, 3, 2, 1, 4))
    d["ckN"] = f(ck.transpose(0, 3, 1, 2, 4))
    d["cvN"] = f(cv.transpose(0, 3, 1, 2, 4))
    sre, sim = np.asarray(inp["state_ssm_re"]), np.asarray(inp["state_ssm_im"])
    ssm0 = np.stack([sre, sim], 1)
    d["ssm0"] = f(ssm0.reshape(L, 2, 128, 16, 128).transpose(0, 1, 3, 4, 2))
    gst = np.asarray(inp["state_gla"])
    d["gst"] = f(gst.reshape(L, 128, 4, 64, 2, 64).transpose(0, 2, 4, 3, 1, 5).reshape(L, NHG, 64, 128, 64))
    cpv = np.asarray(inp["state_conv"])
    d["cprevT"] = f(cpv.transpose(0, 2, 3, 1))
    d["wout"] = f(inp["w_out"])
    d["wup"] = f(inp["ffn_w_up"])
    d["wdown"] = f(inp["ffn_w_down"])
    d["wglu"] = f(inp["s5_w_glu"])
    d["ln1"] = f(np.stack([np.stack([vec16(inp["ln1_g"][l]), vec16(inp["ln1_b"][l])]) for l in range(L)]))
    d["ln2"] = f(np.stack([np.stack([vec16(inp["ln2_g"][l]), vec16(inp["ln2_b"][l])]) for l in range(L)]))
    cw, cb = np.asarray(inp["ffn_conv_w"]), np.asarray(inp["ffn_conv_b"])
    d["convw"] = f(np.stack([np.stack([cw[l, j].reshape(NFC, 128).T for j in range(3)] + [cb[l].reshape(NFC, 128).T]) for l in range(L)]))
    d["bglu"] = f(np.stack([np.asarray(inp["s5_b_glu"][l]).reshape(4, 128).T for l in range(L)]))
    d["gnorm"] = f(np.stack([np.asarray(inp["gla_norm_g"][l]).reshape(4, 128).T for l in range(L)]))
    return d


INPUT_SHAPES = None


def chunked(ap2d):
    return ap2d.rearrange("(c p) n -> p c n", p=128)


class Builder:
    def __init__(self, shapes, debug=False):
        self.debug = debug
        self.nc = nc = bass.Bass("TRN2", target_bir_lowering=False)
        self.I = {}
        for name, shp in shapes.items():
            self.I[name] = nc.dram_tensor(name, list(shp), F32, kind="ExternalInput").ap()
        O = self.O = {}

        def out(name, shp):
            O[name] = nc.dram_tensor(name, list(shp), F32, kind="ExternalOutput").ap()
        out("y_out", [CPC, D])
        out("ks_out", [L, 1, 128, 128, 64])
        out("vs_out", [L, 1, 128, 128, 64])
        out("kp_out", [L, 1, 128, 64])
        out("vp_out", [L, 1, 128, 64])
        out("ssmp_out", [L, 2, 2, 128])
        out("ssms_out", [L, 2, 2, 128, 128])
        out("glap_out", [L, 1, 64, 64])
        out("glas_out", [L, 1, 64, 128, 64])
        out("convp_out", [L, 128, NFC, 2])
        out("convs_out", [L, 2, DFF, 128])
        kind = "ExternalOutput" if debug else "Internal"
        self.xn_in = nc.dram_tensor("xn_in", [D, CPC], BF16).ap()
        self.xg = [nc.dram_tensor(f"xg{l}", [NCORES * D, CPC], BF16).ap() for l in range(L)]
        self.xres_loc = nc.dram_tensor("xres_loc", [D, CPC], F32).ap()
        self.mb_in = nc.dram_tensor("mb_in", [BROWS, G + 2], BF16).ap()
        self.mb_loc = nc.dram_tensor("mb_loc", [NHG * BROWS, CPC + 2], BF16).ap()
        self.mbufs = [nc.dram_tensor(f"mbuf{l}", [NHG * BROWS, G + 2], BF16, kind=kind).ap() for l in range(L)]
        self.vs_scr = nc.dram_tensor("vs_scr", [128, 64], F32).ap()
        self.CONV = ["ident", "pm", "win_all", "s5dz", "wa2", "ckT", "cvS", "wglu", "wout", "wup", "wdown"]
        self.Wb = {k: nc.dram_tensor(k + "_bf", list(shapes[k]), BF16).ap() for k in self.CONV}

    def T(self, es, name, shape, dt):
        self._n = getattr(self, "_n", 0) + 1
        return es.enter_context(self.nc.sbuf_tensor(f"{name}_{self._n}", shape, dt))

    def build(self, n_layers=L, stop_after=None):
        nc = self.nc
        with ExitStack() as es:
            self.P = P = Prog(nc, es)
            self.es = es
            self.ps = [es.enter_context(nc.psum_tensor(f"ps{i}", [128, 512], F32)) for i in range(6)]
            self.psb = [es.enter_context(nc.psum_tensor(f"psb{i}", [128, 1024], BF16)) for i in range(2)]
            self.ps_i = 0
            self.psb_i = 0
            self.pid = nc.sync.partition_id()
            self.convert_weights(0)
            self.setup_consts()
            self.p0()
            self.convert_weights(("dense", 0))
            for l in range(n_layers):
                self.cur_l = l
                self.p1(l, 0)
                P.collective("AllGather", self.mb_in, self.mbufs[l], reads=["mb_in"], writes=[f"mbuf{l}"])
                if l + 1 < n_layers:
                    self.convert_weights(("dense", l + 1))
                if isinstance(stop_after, tuple) and stop_after[:2] == ("p1", l):
                    break
                self.p2(l, last=(l == n_layers - 1))
            P.barrier()
            P.finish("sp")
        return nc

    def convert_weights(self, stage):
        P = self.P
        early = ["ident", "pm", "win_all", "s5dz", "wa2", "ckT", "cvS"]
        self.wb_keys = getattr(self, "wb_keys", {})
        self.pending = getattr(self, "pending", [])
        if stage == 0:
            jobs = [(k, self.I[k], self.Wb[k], None) for k in early]
        else:
            l = stage[1]
            jobs = [(k, self.I[k][l], self.Wb[k][l], l) for k in self.CONV if k not in early]
        for k, src, dst, l in jobs:
            nd = len(src.shape)
            names = " ".join(f"d{i}" for i in range(nd))
            pat = f"{names} -> ({' '.join(f'd{i}' for i in range(nd - 1))}) d{nd - 1}"
            s2, d2 = src.rearrange(pat), dst.rearrange(pat)
            rows = s2.shape[0]
            step = max(8, min(rows, ((1 << 22) if stage == 0 else (1 << 21)) // max(1, s2.shape[1])) // 8 * 8)
            keys = []
            for r0 in range(0, rows, step):
                r1 = min(rows, r0 + step)
                key = ("wb", k, l, r0)
                if stage != 0:
                    self.pending.append((d2[r0:r1, :], s2[r0:r1, :], key))
                else:
                    P.dma("pool", d2[r0:r1, :], s2[r0:r1, :], writes=[key])
                keys.append(key)
            self.wb_keys[(k, l)] = keys
        if stage == 0:
            P.barrier()
        else:
            P.bg_pool = True
            self.pump_total = len(self.pending)

    def pump(self, frac, pace_key):
        cnt = len(self.pending) if frac >= 1.0 else min(len(self.pending), max(1, int(math.ceil(self.pump_total * frac))))
        for _ in range(cnt):
            d_, s_, key = self.pending.pop(0)
            self.P.dma("pool", d_, s_, writes=[key], after=[pace_key] if pace_key else [])

    def bank(self):
        i = self.ps_i
        self.ps_i = (i + 1) % len(self.ps)
        return self.ps[i], f"ps{i}"

    def bankb(self):
        i = self.psb_i
        self.psb_i = (i + 1) % len(self.psb)
        return self.psb[i], f"psb{i}"

    def setup_consts(self):
        P, es, I = self.P, self.es, self.I
        T = lambda n, s, d: self.T(es, n, s, d)
        self.ident_f = T("ident_f", [128, 128], F32)
        self.ident_b = T("ident_b", [128, 128], BF16)
        self.pm_b = T("pm_b", [128, 128], BF16)
        self.mask_f = T("mask_f", [128, 3, 256], F32)
        self.ones_f = T("ones_f", [128, 512], F32)
        self.ones_b = T("ones_b", [128, 128], BF16)
        self.iota1 = T("iota1", [128, 512], F32)
        self.iota_i = T("iota_i", [128, 512], I32)
        self.resetm = T("resetm", [128, 512], F32)
        self.blk2 = T("blk2", [128, 2], F32)
        self.sel65 = T("sel65", [128, 64], F32)
        P.dma("sp", self.ident_f[:], I["ident"][:, :], writes=["ident_f"])
        P.dma("sp", self.ident_b[:], self.Wb["ident"][:, :], writes=["ident_b"])
        P.dma("sp", self.pm_b[:], self.Wb["pm"][:, :], writes=["pm_b"])
        P.dma("sp", self.mask_f[:], I["swamask"][:, :, :], writes=["mask_f"])
        P.op("dve", MEMSET(self.ones_f[:], 1.0), writes=["ones_f"])
        P.op("dve", MEMSET(self.ones_b[:], 1.0), writes=["ones_b"])
        P.op("pool", lambda e: e.iota(self.iota_i[:], pattern=[[1, 512]], base=1, channel_multiplier=0), writes=["iota_i"])
        P.op("dve", CP(self.iota1[:], self.iota_i[:]), reads=["iota_i"], writes=["iota1"])
        P.op("dve", MEMSET(self.resetm[:], 1.0), writes=["resetm"])
        for c in range(4):
            P.op("dve", MEMSET(self.resetm[:, c * 128:c * 128 + 1], 0.0), writes=["resetm"])
        P.op("dve", MEMSET(self.blk2[:], 0.0), writes=["blk2"])
        P.op("dve", MEMSET(self.blk2[0:64, 0:1], 1.0), writes=["blk2"])
        P.op("dve", MEMSET(self.blk2[64:128, 1:2], 1.0), writes=["blk2"])
        self.cmask = T("cmask", [128, CPC + 2], F32)
        self.nsm = T("nsm", [128, 1], F32)
        self.zero_b = T("zero_b", [128, 2], BF16)
        P.dma("sp", self.cmask[:], I["cmask"][0:1, :].partition_broadcast(128), writes=["cmask"])
        P.dma("sp", self.nsm[:], I["nsm"][:, :], writes=["nsm"])
        P.op("dve", MEMSET(self.zero_b[:], 0.0), writes=["zero_b"])
        P.dma("sp", self.mb_in[0:128, 0:2], self.zero_b[:], reads=["zero_b"], writes=["mb_in"])
        P.dma("sp", self.mb_in[128:256, 0:2], self.zero_b[:], reads=["zero_b"], writes=["mb_in"])
        P.dma("sp", self.mb_in[256:320, 0:2], self.zero_b[0:64, :], reads=["zero_b"], writes=["mb_in"])
        P.op("dve", MEMSET(self.sel65[:], 0.0), writes=["sel65"])
        P.op("dve", MEMSET(self.sel65[64:65, :], 1.0), writes=["sel65"])

    def sincos(self, ang, ang_key, sin_out, cos_out, out_key, tmp_f, tmp_i, tmp_key, np_, nf):
        P = self.P
        a = ang
        for shift, dst in ((0.0, sin_out), (math.pi / 2, cos_out)):
            if dst is None:
                continue
            P.op("dve", TS(tmp_i, a, 1.0 / TWO_PI, ALU.mult, shift / TWO_PI, ALU.add), reads=[ang_key], writes=[tmp_key + "i"])
            P.op("dve", CP(tmp_f, tmp_i), reads=[tmp_key + "i"], writes=[tmp_key + "f"])
            P.op("dve", STT(dst, tmp_f, -C1, a, ALU.mult, ALU.add), reads=[tmp_key + "f", ang_key], writes=[out_key])
            P.op("dve", STT(dst, tmp_f, -C2, dst, ALU.mult, ALU.add), reads=[tmp_key + "f", out_key], writes=[out_key])
            if shift:
                P.op("dve", TS(dst, dst, shift, ALU.add), reads=[out_key], writes=[out_key])
            P.op("dve", TS(dst, dst, -math.pi, ALU.max, math.pi, ALU.min), reads=[out_key], writes=[out_key])
            P.op("act", ACT(dst, dst, AF.Sin), reads=[out_key], writes=[out_key])

    def layernorm(self, es, xr, xr_key, ncols, gb, gb_key, xb=None, xb_key=None, mask=None):
        P = self.P
        T = lambda n, s, d: self.T(es, n, s, d)
        LW = CPC + 2
        sqs = [T("ln_sq", [128, 512], F32) for _ in range(2)]
        mean = T("ln_mean", [128, LW], F32)
        rstd = T("ln_rstd", [128, LW], F32)
        shift = T("ln_shift", [128, LW], F32)
        tmp = T("ln_tmp", [128, LW], F32)
        nblk = (ncols + 511) // 512
        for b in range(nblk):
            c0, c1 = b * 512, min(ncols, b * 512 + 512)
            n = c1 - c0
            ps_s, ks = self.bank()
            ps_q, kq = self.bank()
            fs, fq = [], []
            for k in range(16):
                sq, sk = sqs[k % 2], f"ln_sq{k % 2}"
                P.op("act", ACT(sq[:, 0:n], xr[:, k, c0:c1], AF.Square), reads=[xr_key], writes=[sk])
                P.op("pe", [MM(ps_s[:, 0:n], self.ones_f[:, 0:128], xr[:, k, c0:c1], start=(k == 0), stop=(k == 15)),
                            MM(ps_q[:, 0:n], self.ones_f[:, 0:128], sq[:, 0:n], start=(k == 0), stop=(k == 15))],
                     reads=["ones_f", xr_key, sk], writes=[ks, kq])
            P.op("act", ACT(mean[:, c0:c1], ps_s[:, 0:n], AF.Copy, scale=1.0 / D), reads=[ks], writes=["ln_mean"])
            P.op("dve", TT(tmp[:, c0:c1], mean[:, c0:c1], mean[:, c0:c1], ALU.mult), reads=["ln_mean"], writes=["ln_tmp"])
            P.op("dve", STT(tmp[:, c0:c1], ps_q[:, 0:n], 1.0 / D, tmp[:, c0:c1], ALU.mult, ALU.subtract), reads=[kq, "ln_tmp"], writes=["ln_tmp"])
            P.op("dve", TS(tmp[:, c0:c1], tmp[:, c0:c1], EPS, ALU.add), reads=["ln_tmp"], writes=["ln_tmp"])
            P.op("act", ACT(tmp[:, c0:c1], tmp[:, c0:c1], AF.Sqrt), reads=["ln_tmp"], writes=["ln_tmp"])
            P.op("dve", RECIP(rstd[:, c0:c1], tmp[:, c0:c1]), reads=["ln_tmp"], writes=["ln_rstd"])
            P.op("dve", STT(shift[:, c0:c1], mean[:, c0:c1], -1.0, rstd[:, c0:c1], ALU.mult, ALU.mult), reads=["ln_mean", "ln_rstd"], writes=["ln_shift"])
        for k in range(16):
            xk = xr[:, k, 0:ncols]
            P.op("dve", TT(xk, xk, rstd[:, 0:ncols], ALU.mult), reads=[xr_key, "ln_rstd"], writes=[xr_key])
            P.op("dve", TT(xk, xk, shift[:, 0:ncols], ALU.add), reads=[xr_key, "ln_shift"], writes=[xr_key])
            P.op("act", ACT(xk, xk, AF.Identity, scale=gb[:, 0, k:k + 1], bias=gb[:, 1, k:k + 1]), reads=[xr_key, gb_key], writes=[xr_key])
            if xb is not None:
                P.op("dve", TT(xb[:, k, 0:ncols], xk, mask, ALU.mult), reads=[xr_key, "cmask"], writes=[xb_key])

    def p0(self):
        P, I = self.P, self.I
        with ExitStack() as es:
            T = lambda n, s, d: self.T(es, n, s, d)
            xr = T("p0_xr", [128, 16, CPC], F32)
            xb = T("p0_xb", [128, 16, CPC], BF16)
            gb = T("p0_gb", [128, 2, 16], F32)
            P.dma("sp", gb[:], I["ln_in"].rearrange("t p k -> p t k"), writes=["p0_gb"])
            P.dma("sp", xr[:], chunked(I["xT"]), writes=["p0_xr"])
            with ExitStack() as es2:
                self.layernorm(es2, xr, "p0_xr", CPC, gb, "p0_gb", xb, "p0_xb", mask=self.cmask[:, 2:CPC + 2])
                P.dma("sp", chunked(self.xres_loc), xr[:], reads=["p0_xr"], writes=["xres_loc"])
                P.dma("sp", chunked(self.xn_in), xb[:], reads=["p0_xb"], writes=["xn_in"])
                P.barrier()
            P.collective("AllGather", self.xn_in, self.xg[0], reads=["xn_in"], writes=["xg0"])
            P.barrier()

    def p1(self, l, hg):
        P, I, O = self.P, self.I, self.O
        kvh = hg // 2
        NT = (G + W1 - 1) // W1
        with ExitStack() as es:
            T = lambda n, s, d: self.T(es, n, s, d)
            win = T("win", [128, 16, 768], BF16)
            P.dma("sp", win[:, :, 0:NWC], chunked(self.Wb["win_all"][l, hg]), writes=["win"])
            sinkn = T("sinkn", [128, 2], F32)
            P.dma("sp", sinkn[:], I["sink"][l, hg], writes=["sinkn"])
            P.op("dve", TS(sinkn[:], sinkn[:], -1.0, ALU.mult), reads=["sinkn"], writes=["sinkn"])
            Ec, Es, magt, bbre, bbim, czre, nczim, arc, aic, naic = [], [], [], [], [], [], [], [], [], []
            pre = {}
            for k in range(2):
                pre[("arc", k)], pre[("aic", k)], pre[("naic", k)] = T("arc", [128, 1], F32), T("aic", [128, 1], F32), T("naic", [128, 1], F32)
                pre[("Ec", k)], pre[("Es", k)], pre[("magt", k)] = T("Ec", [128, 512], F32), T("Es", [128, 512], F32), T("magt", [128, 512], F32)
                pre[("bbre", k)], pre[("bbim", k)] = T("bbre", [128, 128], BF16), T("bbim", [128, 128], BF16)
                pre[("czre", k)], pre[("nczim", k)] = T("czre", [128, 64], BF16), T("nczim", [128, 64], BF16)
            dz_b = T("dz_b", [128, 64], BF16)
            wa2_b = T("wa2_b", [16, 64], BF16)
            nba = T("nba", [64, 1], F32)
            with ExitStack() as esp:
                Tp = lambda n, s, d: self.T(esp, n, s, d)
                ang = Tp("ang", [128, 512], F32)
                tf = Tp("sc_tf", [128, 512], F32)
                ti = Tp("sc_ti", [128, 512], I32)
                for k in range(0 if 's5prep' in DBG_SKIP else 2):
                    cp = Tp("cp", [128, 3], F32)
                    sm = Tp("sm", [128, 12], F32)
                    P.dma("sp", cp[:], I["s5col"][l, hg, k], writes=["cp"])
                    dtc, thc, lrdt, magc, sc_, cc_ = (sm[:, j:j + 1] for j in range(6))
                    P.op("act", ACT(dtc, cp[:, 2:3], AF.Exp), reads=["cp"], writes=["sm"])
                    P.op("dve", TT(thc, cp[:, 1:2], dtc, ALU.mult), reads=["cp", "sm"], writes=["sm"])
                    P.op("dve", TT(lrdt, cp[:, 0:1], dtc, ALU.mult), reads=["cp", "sm"], writes=["sm"])
                    P.op("act", ACT(magc, lrdt, AF.Exp), reads=["sm"], writes=["sm"])
                    sct = Tp("sct", [128, 2], F32)
                    self.sincos(thc, "sm", sct[:, 0:1], sct[:, 1:2], "sct", tf[:, 0:1], ti[:, 0:1], "sc_t", 128, 1)
                    a_r, a_i, na_i = pre[("arc", k)], pre[("aic", k)], pre[("naic", k)]
                    P.op("dve", TT(a_r[:], magc, sct[:, 1:2], ALU.mult), reads=["sm", "sct"], writes=["arc"])
                    P.op("dve", TT(a_i[:], magc, sct[:, 0:1], ALU.mult), reads=["sm", "sct"], writes=["aic"])
                    P.op("dve", TS(na_i[:], a_i[:], -1.0, ALU.mult), reads=["aic"], writes=["naic"])
                    arc.append(a_r); aic.append(a_i); naic.append(na_i)
                    ec, esn, mt = pre[("Ec", k)], pre[("Es", k)], pre[("magt", k)]
                    P.op("dve", TS(ang[:], self.iota1[:], thc, ALU.mult), reads=["iota1", "sm"], writes=["ang"])
                    self.sincos(ang[:], "ang", esn[:], ec[:], f"EcEs{k}", tf[:], ti[:], "sc_t", 128, 512)
                    P.op("dve", TS(mt[:], self.ones_f[:], magc, ALU.mult), reads=["ones_f", "sm"], writes=[f"magt{k}"])
                    Ec.append(ec); Es.append(esn); magt.append(mt)
                    rp = Tp("rp", [128, 3, 128], F32)
                    bz = Tp("bz", [128, 2, 128], F32)
                    r = Tp("rwork", [128, 12, 128], F32)
                    P.dma("sp", rp[64:128], I["s5row"][l, hg, k], writes=["rp"])
                    P.dma("sp", bz[64:128], I["s5bz"][l, hg, k], writes=["bz"])
                    h = slice(64, 128)
                    lrr, lir, ldr = rp[h, 0, :], rp[h, 1, :], rp[h, 2, :]
                    R_ = lambda j: r[h, j, :]
                    rk = ["rp", "rwork"]
                    P.op("act", ACT(R_(0), ldr, AF.Exp), reads=rk, writes=["rwork"])
                    P.op("dve", TT(R_(1), lir, R_(0), ALU.mult), reads=rk, writes=["rwork"])
                    P.op("dve", TT(R_(2), lrr, R_(0), ALU.mult), reads=rk, writes=["rwork"])
                    P.op("act", ACT(R_(2), R_(2), AF.Exp), reads=rk, writes=["rwork"])
                    self.sincos(R_(1), "rwork", R_(3), R_(4), "rwork", tf[h, 0:128], ti[h, 0:128], "sc_t", 64, 128)
                    P.op("dve", TT(R_(4), R_(4), R_(2), ALU.mult), reads=rk, writes=["rwork"])
                    P.op("dve", TT(R_(3), R_(3), R_(2), ALU.mult), reads=rk, writes=["rwork"])
                    P.op("dve", TS(R_(4), R_(4), -1.0, ALU.add), reads=rk, writes=["rwork"])
                    P.op("dve", TT(R_(5), lrr, lrr, ALU.mult), reads=rk, writes=["rwork"])
                    P.op("dve", TT(R_(6), lir, lir, ALU.mult), reads=rk, writes=["rwork"])
                    P.op("dve", TT(R_(5), R_(5), R_(6), ALU.add), reads=rk, writes=["rwork"])
                    P.op("dve", RECIP(R_(5), R_(5)), reads=rk, writes=["rwork"])
                    P.op("dve", TT(R_(6), R_(4), lrr, ALU.mult), reads=rk, writes=["rwork"])
                    P.op("dve", TT(R_(7), R_(3), lir, ALU.mult), reads=rk, writes=["rwork"])
                    P.op("dve", TT(R_(6), R_(6), R_(7), ALU.add), reads=rk, writes=["rwork"])
                    P.op("dve", TT(R_(6), R_(6), R_(5), ALU.mult), reads=rk, writes=["rwork"])
                    P.op("dve", TT(R_(7), R_(3), lrr, ALU.mult), reads=rk, writes=["rwork"])
                    P.op("dve", TT(R_(8), R_(4), lir, ALU.mult), reads=rk, writes=["rwork"])
                    P.op("dve", TT(R_(7), R_(7), R_(8), ALU.subtract), reads=rk, writes=["rwork"])
                    P.op("dve", TT(R_(7), R_(7), R_(5), ALU.mult), reads=rk, writes=["rwork"])
                    b_re, b_im = pre[("bbre", k)], pre[("bbim", k)]
                    rb = ["rwork", "bz"]
                    P.op("dve", TT(R_(8), R_(6), bz[h, 0, :], ALU.mult), reads=rb, writes=["rwork"])
                    P.op("dve", TT(R_(9), R_(7), bz[h, 1, :], ALU.mult), reads=rb, writes=["rwork"])
                    P.op("dve", TT(b_re[h, :], R_(8), R_(9), ALU.subtract), reads=rb, writes=[f"bb{k}"])
                    P.op("dve", TT(R_(8), R_(6), bz[h, 1, :], ALU.mult), reads=rb, writes=["rwork"])
                    P.op("dve", TT(R_(9), R_(7), bz[h, 0, :], ALU.mult), reads=rb, writes=["rwork"])
                    P.op("dve", TT(b_im[h, :], R_(8), R_(9), ALU.add), reads=rb, writes=[f"bb{k}"])
                    bbre.append(b_re); bbim.append(b_im)
                    czf = Tp("czf", [128, 2, 64], F32)
                    P.dma("sp", czf[:], I["s5cz"][l, hg, k], writes=["czf"])
                    c_re, nc_im = pre[("czre", k)], pre[("nczim", k)]
                    P.op("dve", CP(c_re[:], czf[:, 0, :]), reads=["czf"], writes=[f"cz{k}"])
                    P.op("dve", TS(nc_im[:], czf[:, 1, :], -1.0, ALU.mult), reads=["czf"], writes=[f"cz{k}"])
                    czre.append(c_re); nczim.append(nc_im)
                P.dma("sp", dz_b[64:128, :], self.Wb["s5dz"][l, hg], writes=["dz_b"])
                P.dma("sp", wa2_b[:], self.Wb["wa2"][l, hg], writes=["wa2_b"])
                P.dma("sp", nba[:], I["ba"][l, hg], writes=["nba"])
                P.op("dve", TS(nba[:], nba[:], -1.0, ALU.mult), reads=["nba"], writes=["nba"])
                P.barrier()
            xt = [T("xt", [128, 16, W1], BF16) for _ in range(2)]
            ropet = [T("ropet", [128, 2, W1], F32) for _ in range(2)]
            qT = T("qT", [128, W1], BF16)
            qb = T("qb", [128, W1], BF16)
            kT = T("kT", [128, 128 + W1], BF16)
            kf = T("kf", [128, 256], F32)
            vf = T("vf", [128, 256], F32)
            t1 = T("t1", [128, W1], F32)
            t2 = T("t2", [128, W1], F32)
            t3 = T("t3", [128, W1], F32)
            t4 = T("t4", [128, W1], F32)
            gre = T("gre", [128, W1], F32)
            gim = T("gim", [128, W1], F32)
            VUb = T("VUb", [128, W1], BF16)
            Vtm = T("Vtm", [128, 5, 65], BF16)
            oa2 = T("oa2", [128, 128], BF16)
            oaT_st = T("oaT_st", [128, W1], BF16)
            pexp = [T("pexp", [128, 256], F32) for _ in range(2)]
            PTt = [T("PT", [128, 256], BF16) for _ in range(2)]
            rden = T("rden", [128, 4], F32)
            hprev = T("hprev", [128, 4], F32)
            hre_b = [T("hre_b", [128, W1], BF16) for _ in range(2)]
            him_b = [T("him_b", [128, W1], BF16) for _ in range(2)]
            z_st = T("z_st", [64, W1], BF16)
            qg_f = T("qg_f", [64, W1], F32)
            kg_f = T("kg_f", [64, W1], F32)
            VGb = T("VGb", [128, W1], BF16)
            vgf = T("vgf", [64, 128], F32)
            acb = T("acb", [16, W1], BF16)
            lg = T("lg", [64, W1], F32)
            cs = T("cs", [64, W1], F32)
            Ep = T("Ep", [64, W1], F32)
            Em = T("Em", [64, W1], F32)
            qin_b = T("qin_b", [64, W1], BF16)
            kout_f = T("kout_f", [64, W1], F32)
            kout_b = T("kout_b", [64, W1], BF16)
            kend_b = T("kend_b", [64, W1], BF16)
            attT_b = [T("attT_b", [128, 128], BF16) for _ in range(2)]
            vtm_b = [T("vtm_b", [128, 64], BF16) for _ in range(2)]
            kend_tm = [T("kend_tm", [128, 64], BF16) for _ in range(2)]
            S_f = T("S_f", [64, 64], F32)
            S_b = T("S_b", [64, 64], BF16)
            o_st = T("o_st", [64, W1], BF16)
            if 'initoa' in DBG_SKIP:
                P.op("dve", MEMSET(oaT_st[:], 0.0), writes=["oaT_st"])
            P.op("dve", MEMSET(kT[:, 0:128], 0.0), writes=["kT"])
            P.op("dve", MEMSET(Vtm[:], 0.0), writes=["Vtm"])
            P.op("dve", MEMSET(Vtm[:, :, 64:65], 1.0), writes=["Vtm"])
            P.op("dve", MEMSET(hprev[:], 0.0), writes=["hprev"])
            P.op("dve", MEMSET(S_f[:], 0.0), writes=["S_f"])
            P.op("dve", MEMSET(S_b[:], 0.0), writes=["S_b"])

            def load_tile(t):
                c0 = t * W1
                n = min(W1, G - c0)
                xg3 = self.xg[l].rearrange("(r c p) n -> r p c n", r=NCORES, p=128)
                a = c0
                while a < c0 + n:
                    r = a // CPC
                    b = min(c0 + n, (r + 1) * CPC)
                    P.dma("sp", xt[t % 2][:, :, a - c0:b - c0], xg3[r][:, :, a - r * CPC:b - r * CPC], reads=[f"xg{l}"], writes=[f"xt{t % 2}"])
                    a = b
                if 'ropedma' not in DBG_SKIP:
                    P.dma("sp", ropet[t % 2][:, :, 0:n], I["rope"][:, :, c0:c0 + n].rearrange("t p n -> p t n"), writes=[f"ropet{t % 2}"])

            if 'tiles' in DBG_SKIP:
                NT = 0
            else:
                load_tile(0)
            for t in range(NT):
                c0 = t * W1
                n = min(W1, G - c0)
                nblk = n // 128
                last = (t == NT - 1)
                if t + 1 < NT:
                    load_tile(t + 1)
                X, xk_ = xt[t % 2], f"xt{t % 2}"
                RT, rk_ = ropet[t % 2], f"ropet{t % 2}"

                def proj(c_lo, c_hi, m):
                    ps, key = self.bank()
                    P.op("pe", [MM(ps[0:m, 0:n], win[:, k, c_lo:c_hi], X[:, k, 0:n], start=(k == 0), stop=(k == 15)) for k in range(16)],
                         reads=["win", xk_], writes=[key])
                    return ps, key

                def roped(c_lo, dst, dst_key, f32_dst=None):
                    ps, key = proj(c_lo, c_lo + 128, 128)
                    P.op("act", ACT(qb[:, 0:n], ps[:, 0:n], AF.Copy), reads=[key], writes=["qb"])
                    ps2, key2 = self.bank()
                    P.op("pe", MM(ps2[:, 0:n], self.pm_b[:], qb[:, 0:n]), reads=["pm_b", "qb"], writes=[key2])
                    P.op("dve", TT(t1[:, 0:n], ps[:, 0:n], RT[:, 0, 0:n], ALU.mult), reads=[key, rk_], writes=["t1"])
                    P.op("dve", TT(t2[:, 0:n], ps2[:, 0:n], RT[:, 1, 0:n], ALU.mult), reads=[key2, rk_], writes=["t2"])
                    P.op("dve", TT(dst, t1[:, 0:n], t2[:, 0:n], ALU.add), reads=["t1", "t2"], writes=[dst_key])
                    if f32_dst is not None:
                        P.op("dve", TT(f32_dst, t1[:, 0:n], t2[:, 0:n], ALU.add), reads=["t1", "t2"], writes=["kf"])

                if self.pending and l == 0:
                    self.pump(1.0 / (NT - 2) if t < NT - 1 else 1.0, getattr(self, "pace_tok", None))
                if 'rope' not in DBG_SKIP:
                    roped(0, qT[:, 0:n], "qT")
                    roped(128, kT[:, 128:128 + n], "kT", f32_dst=(kf[:, 0:n] if last else None))
                if 'vu' in DBG_SKIP:
                    continue
                ps_vu, kvu = proj(256, 384, 128)
                self.pace_tok = P.op("act", ACT(VUb[:, 0:n], ps_vu[:, 0:n], AF.Copy), reads=[kvu], writes=["VUb"])
                if last and 'novf' not in DBG_SKIP:
                    P.op("act", ACT(vf[0:64, 0:n], ps_vu[0:64, 0:n], AF.Copy), reads=[kvu], writes=["vf"])
                for i in range(0 if 'vtr' in DBG_SKIP else nblk):
                    pb_, kb_ = self.bankb()
                    P.op("pe", TR(pb_[:, 0:64], VUb[0:64, i * 128:(i + 1) * 128], self.ident_b[0:64, 0:64]), reads=["VUb", "ident_b"], writes=[kb_])
                    P.op("act", ACT(Vtm[:, i + 1, 0:64], pb_[:, 0:64], AF.Copy), reads=[kb_], writes=["Vtm"])
                for i in range(0 if 'swa' in DBG_SKIP else nblk):
                    gb = 4 * t + i
                    if gb == NB - 1:
                        continue
                    mv = 1 if gb == 0 else (2 if gb == 1 else 0)
                    for hh in range(2):
                        pb = 64 * hh
                        ps_s, ks_ = self.bank()
                        qs = qT[pb:pb + 64, i * 128:(i + 1) * 128]
                        P.op("pe", [MM(ps_s[:, 0:128], kT[pb:pb + 64, i * 128:(i + 1) * 128], qs),
                                    MM(ps_s[:, 128:256], kT[pb:pb + 64, (i + 1) * 128:(i + 2) * 128], qs)],
                             reads=["kT", "qT"], writes=[ks_])
                        P.op("act", ACT(pexp[hh][:], ps_s[:, 0:256], AF.Exp, scale=0.125, bias=sinkn[:, hh:hh + 1]), reads=[ks_, "sinkn"], writes=[f"pexp{hh}"])
                        P.op("dve", TT(PTt[hh][:], pexp[hh][:], self.mask_f[:, mv, :], ALU.mult), reads=[f"pexp{hh}", "mask_f"], writes=[f"PT{hh}"])
                        ps_o, ko_ = self.bank()
                        P.op("pe", [MM(ps_o[:, 0:65], PTt[hh][:, 0:128], Vtm[:, i, :], start=True, stop=False),
                                    MM(ps_o[:, 0:65], PTt[hh][:, 128:256], Vtm[:, i + 1, :], start=False, stop=True)],
                             reads=[f"PT{hh}", "Vtm"], writes=[ko_])
                        P.op("dve", TS(rden[:, hh:hh + 1], ps_o[:, 64:65], 1.0, ALU.add), reads=[ko_], writes=["rden"])
                        P.op("dve", RECIP(rden[:, hh:hh + 1], rden[:, hh:hh + 1]), reads=["rden"], writes=["rden"])
                        P.op("act", ACT(oa2[:, pb:pb + 64], ps_o[:, 0:64], AF.Copy, scale=rden[:, hh:hh + 1]), reads=[ko_, "rden"], writes=["oa2"])
                    pb_, kb_ = self.bankb()
                    P.op("pe", TR(pb_[:, 0:128], oa2[:], self.ident_b[:]), reads=["oa2", "ident_b"], writes=[kb_])
                    P.op("act", ACT(oaT_st[:, i * 128:(i + 1) * 128], pb_[:, 0:128], AF.Copy), reads=[kb_], writes=["oaT_st"])
                npr = n - 128 if last else n
                if 'nombuf' not in DBG_SKIP:
                    P.dma("sp", self.mb_in[0:128, 2 + c0:2 + c0 + npr], oaT_st[:, 0:npr], reads=["oaT_st"])
                for k in range(0 if 's5' in DBG_SKIP else 2):
                    ps_re, kre = self.bank()
                    ps_im, kim = self.bank()
                    P.op("pe", MM(ps_re[:, 0:n], bbre[k][64:128, :], VUb[64:128, 0:n]), reads=[f"bb{k}", "VUb"], writes=[kre])
                    P.op("pe", MM(ps_im[:, 0:n], bbim[k][64:128, :], VUb[64:128, 0:n]), reads=[f"bb{k}", "VUb"], writes=[kim])
                    ek = f"EcEs{k}"
                    ec, esn = Ec[k][:, 0:n], Es[k][:, 0:n]
                    P.op("dve", TT(t1[:, 0:n], ps_re[:, 0:n], ec, ALU.mult), reads=[kre, ek], writes=["t1"])
                    P.op("dve", TT(t2[:, 0:n], ps_im[:, 0:n], esn, ALU.mult), reads=[kim, ek], writes=["t2"])
                    P.op("dve", TT(t1[:, 0:n], t1[:, 0:n], t2[:, 0:n], ALU.add), reads=["t1", "t2"], writes=["t1"])
                    P.op("dve", TT(t3[:, 0:n], ps_im[:, 0:n], ec, ALU.mult), reads=[kim, ek], writes=["t3"])
                    P.op("dve", TT(t4[:, 0:n], ps_re[:, 0:n], esn, ALU.mult), reads=[kre, ek], writes=["t4"])
                    P.op("dve", TT(t3[:, 0:n], t3[:, 0:n], t4[:, 0:n], ALU.subtract), reads=["t3", "t4"], writes=["t3"])
                    P.op("dve", SCAN(gre[:, 0:n], magt[k][:, 0:n], t1[:, 0:n], hprev[:, 2 * k:2 * k + 1]), reads=[f"magt{k}", "t1", "hprev"], writes=["gre"])
                    P.op("dve", SCAN(gim[:, 0:n], magt[k][:, 0:n], t3[:, 0:n], hprev[:, 2 * k + 1:2 * k + 2]), reads=[f"magt{k}", "t3", "hprev"], writes=["gim"])
                    P.op("dve", TT(t1[:, 0:n], gre[:, 0:n], ec, ALU.mult), reads=["gre", ek], writes=["t1"])
                    P.op("dve", TT(t2[:, 0:n], gim[:, 0:n], esn, ALU.mult), reads=["gim", ek], writes=["t2"])
                    P.op("dve", TT(hre_b[k][:, 0:n], t1[:, 0:n], t2[:, 0:n], ALU.subtract), reads=["t1", "t2"], writes=[f"hre{k}"])
                    P.op("dve", TT(t3[:, 0:n], gre[:, 0:n], esn, ALU.mult), reads=["gre", ek], writes=["t3"])
                    P.op("dve", TT(t4[:, 0:n], gim[:, 0:n], ec, ALU.mult), reads=["gim", ek], writes=["t4"])
                    P.op("dve", TT(him_b[k][:, 0:n], t3[:, 0:n], t4[:, 0:n], ALU.add), reads=["t3", "t4"], writes=[f"him{k}"])
                    cc = npr - 1
                    cs1 = slice(cc, cc + 1)
                    P.op("dve", TT(hprev[:, 2 * k:2 * k + 1], t1[:, cs1], t2[:, cs1], ALU.subtract), reads=["t1", "t2"], writes=["hprev"])
                    P.op("dve", TT(hprev[:, 2 * k + 1:2 * k + 2], t3[:, cs1], t4[:, cs1], ALU.add), reads=["t3", "t4"], writes=["hprev"])
                    if last:
                        for ri in range(2):
                            P.dma("sp", O["ssmp_out"][l, ri, 2 * hg + k].rearrange("(p o) -> p o", o=1), hprev[:, 2 * k + ri:2 * k + ri + 1], reads=["hprev"])
                        h0 = T("h0", [128, 2, 128], F32)
                        hs = T("hs", [128, 2, 128], F32)
                        for ri in range(2):
                            P.dma("sp", h0[:, ri, :], I["ssm0"][l, ri, 2 * hg + k], writes=["h0"])
                        sc = slice(128, 256)
                        P.op("dve", TS(t1[:, 0:128], h0[:, 0, :], arc[k][:], ALU.mult), reads=["h0", "arc"], writes=["t1"])
                        P.op("dve", STT(t1[:, 0:128], h0[:, 1, :], naic[k][:], t1[:, 0:128], ALU.mult, ALU.add), reads=["h0", "naic", "t1"], writes=["t1"])
                        P.op("dve", TT(hs[:, 0, :], t1[:, 0:128], ps_re[:, sc], ALU.add), reads=["t1", kre], writes=["hs"])
                        P.op("dve", TS(t2[:, 0:128], h0[:, 1, :], arc[k][:], ALU.mult), reads=["h0", "arc"], writes=["t2"])
                        P.op("dve", STT(t2[:, 0:128], h0[:, 0, :], aic[k][:], t2[:, 0:128], ALU.mult, ALU.add), reads=["h0", "aic", "t2"], writes=["t2"])
                        P.op("dve", TT(hs[:, 1, :], t2[:, 0:128], ps_im[:, sc], ALU.add), reads=["t2", kim], writes=["hs"])
                        P.op("dve", CP(hre_b[k][:, sc], hs[:, 0, :]), reads=["hs"], writes=[f"hre{k}"])
                        P.op("dve", CP(him_b[k][:, sc], hs[:, 1, :]), reads=["hs"], writes=[f"him{k}"])
                        for ri in range(2):
                            pst, kst = self.bank()
                            P.op("pe", TR(pst[:, 0:128], hs[:, ri, :], self.ident_f[:]), reads=["hs", "ident_f"], writes=[kst])
                            hst = T("hst", [128, 128], F32)
                            P.op("act", ACT(hst[:], pst[:, 0:128], AF.Copy), reads=[kst], writes=["hst"])
                            P.dma("sp", O["ssms_out"][l, ri, 2 * hg + k], hst[:], reads=["hst"])
                if 's5' in DBG_SKIP:
                    if not last and 'carry' not in DBG_SKIP:
                        P.op("dve", CP(kT[:, 0:128], kT[:, n:n + 128]), reads=["kT"], writes=["kT"])
                        P.op("dve", CP(Vtm[:, 0, :], Vtm[:, nblk, :]), reads=["Vtm"], writes=["Vtm"])
                    continue
                ps_y, ky = self.bank()
                fy = []
                for k in range(2):
                    fy.append(MM(ps_y[0:64, 0:n], czre[k][:], hre_b[k][:, 0:n], start=(k == 0), stop=False))
                    fy.append(MM(ps_y[0:64, 0:n], nczim[k][:], him_b[k][:, 0:n], start=False, stop=False))
                fy.append(MM(ps_y[0:64, 0:n], dz_b[64:128, :], VUb[64:128, 0:n], start=False, stop=True))
                P.op("pe", fy, reads=["cz0", "cz1", "hre0", "hre1", "him0", "him1", "dz_b", "VUb"], writes=[ky])
                P.op("act", ACT(z_st[:, 0:n], ps_y[0:64, 0:n], AF.Gelu), reads=[ky], writes=["z_st"])
                P.dma("sp", self.mb_in[128:192, 2 + c0:2 + c0 + n], z_st[:, 0:n], reads=["z_st"])
                ps_gq, kgq = proj(384, 448, 64)
                P.op("act", ACT(qg_f[:, 0:n], ps_gq[0:64, 0:n], AF.Copy, scale=0.125), reads=[kgq], writes=["qg_f"])
                ps_gk, kgk = proj(448, 512, 64)
                P.op("act", ACT(kg_f[:, 0:n], ps_gk[0:64, 0:n], AF.Copy), reads=[kgk], writes=["kg_f"])
                ps_vg, kvg = proj(512, 640, 128)
                P.op("act", ACT(VGb[:, 0:n], ps_vg[:, 0:n], AF.Copy), reads=[kvg], writes=["VGb"])
                if last:
                    P.op("act", ACT(vgf[:, :], ps_vg[0:64, 128:256], AF.Copy), reads=[kvg], writes=["vgf"])
                ps_ac, kac = proj(640, 656, 16)
                P.op("act", ACT(acb[:, 0:n], ps_ac[0:16, 0:n], AF.Copy), reads=[kac], writes=["acb"])
                ps_g, kg_ = self.bank()
                P.op("pe", MM(ps_g[0:64, 0:n], wa2_b[:, :], acb[:, 0:n]), reads=["wa2_b", "acb"], writes=[kg_])
                P.op("act", ACT(lg[:, 0:n], ps_g[0:64, 0:n], AF.Exp, scale=-1.0, bias=nba[:, 0:1]), reads=[kg_, "nba"], writes=["lg"])
                P.op("act", ACT(lg[:, 0:n], lg[:, 0:n], AF.Ln, bias=1.0), reads=["lg"], writes=["lg"])
                P.op("dve", SCAN(cs[:, 0:n], self.resetm[0:64, 0:n], lg[:, 0:n], 0.0), reads=["resetm", "lg"], writes=["cs"])
                P.op("act", ACT(Ep[:, 0:n], cs[:, 0:n], AF.Exp, scale=-1.0 / 16.0), reads=["cs"], writes=["Ep"])
                P.op("act", ACT(Em[:, 0:n], cs[:, 0:n], AF.Exp, scale=1.0 / 16.0), reads=["cs"], writes=["Em"])
                P.op("dve", TT(qin_b[:, 0:n], qg_f[:, 0:n], Ep[:, 0:n], ALU.mult), reads=["qg_f", "Ep"], writes=["qin_b"])
                P.op("dve", TT(kout_f[:, 0:n], kg_f[:, 0:n], Em[:, 0:n], ALU.mult), reads=["kg_f", "Em"], writes=["kout_f"])
                P.op("act", ACT(kout_b[:, 0:n], kout_f[:, 0:n], AF.Copy), reads=["kout_f"], writes=["kout_b"])
                for i in range(nblk):
                    gb = 4 * t + i
                    if gb == NB - 1:
                        continue
                    ch = slice(i * 128, (i + 1) * 128)
                    eend = Ep[:, i * 128 + 127:i * 128 + 128]
                    j = i % 2
                    P.op("dve", TS(kend_b[:, ch], kout_f[:, ch], eend, ALU.mult), reads=["kout_f", "Ep"], writes=["kend_b"])
                    ps_a, ka_ = self.bank()
                    P.op("pe", MM(ps_a[:, 0:128], kout_b[:, ch], qin_b[:, ch]), reads=["kout_b", "qin_b"], writes=[ka_])
                    P.op("dve", TT(attT_b[j][:], ps_a[:, 0:128], self.mask_f[:, 0, 128:256], ALU.mult), reads=[ka_, "mask_f"], writes=[f"attT{j}"])
                    pb1, kb1 = self.bankb()
                    P.op("pe", TR(pb1[:, 0:64], VGb[0:64, ch], self.ident_b[0:64, 0:64]), reads=["VGb", "ident_b"], writes=[kb1])
                    P.op("act", ACT(vtm_b[j][:], pb1[:, 0:64], AF.Copy), reads=[kb1], writes=[f"vtm{j}"])
                    pb2, kb2 = self.bankb()
                    P.op("pe", TR(pb2[:, 0:64], kend_b[:, ch], self.ident_b[0:64, 0:64]), reads=["kend_b", "ident_b"], writes=[kb2])
                    P.op("dve", CP(kend_tm[j][:], pb2[:, 0:64]), reads=[kb2], writes=[f"kendtm{j}"])
                    ps_o, ko_ = self.bank()
                    P.op("pe", [MM(ps_o[0:64, 0:128], vtm_b[j][:], attT_b[j][:], start=True, stop=False),
                                MM(ps_o[0:64, 0:128], S_b[:], qin_b[:, ch], start=False, stop=True)],
                         reads=[f"vtm{j}", f"attT{j}", "S_b", "qin_b"], writes=[ko_])
                    P.op("act", ACT(o_st[:, ch], ps_o[0:64, 0:128], AF.Copy), reads=[ko_], writes=["o_st"])
                    ps_kv, kkv = self.bank()
                    P.op("pe", MM(ps_kv[0:64, 0:64], kend_tm[j][:], vtm_b[j][:]), reads=[f"kendtm{j}", f"vtm{j}"], writes=[kkv])
                    P.op("dve", STT(S_f[:], S_f[:], eend, ps_kv[0:64, 0:64], ALU.mult, ALU.add), reads=["S_f", "Ep", kkv], writes=["S_f"])
                    P.op("act", ACT(S_b[:], S_f[:], AF.Copy), reads=["S_f"], writes=["S_b"])
                P.dma("sp", self.mb_in[192:256, 2 + c0:2 + c0 + npr], o_st[:, 0:npr], reads=["o_st"])
                P.dma("sp", self.mb_in[256:320, 2 + c0:2 + c0 + n], VGb[64:128, 0:n], reads=["VGb"])
                if last:
                    P.dma("sp", O["glap_out"][l, hg], S_f[:], reads=["S_f"])
                    self.p1_samples(es, l, hg, qT, kT, kf, vf, Vtm, sinkn, qg_f, kg_f, lg, vgf)
                else:
                    if 'carry' in DBG_SKIP:
                        continue
                    P.op("dve", CP(kT[:, 0:128], kT[:, n:n + 128]), reads=["kT"], writes=["kT"])
                    P.op("dve", CP(Vtm[:, 0, :], Vtm[:, nblk, :]), reads=["Vtm"], writes=["Vtm"])
            P.barrier()

    def p1_samples(self, es, l, hg, qT, kT, kf, vf, Vtm, sinkn, qg_f, kg_f, lg, vgf):
        P, I, O = self.P, self.I, self.O
        kvh = hg // 2
        T = lambda n, s, d: self.T(es, n, s, d)
        sc = slice(128, 256)
        GS = G - 128
        prod = T("s_prod", [128, 128], F32)
        pself = T("s_pself", [128, 2], F32)
        Dg = T("s_Dg", [128, 2, 128], BF16)
        Ps = T("s_Ps", [128, 256], BF16)
        Kc = [T("s_Kc", [128, 8, 128], BF16) for _ in range(2)]
        Vc = [T("s_Vc", [128, 8, 65], BF16) for _ in range(2)]
        sC = T("s_sC", [65, 256], F32)
        tot = T("s_tot", [65, 256], F32)
        rd = T("s_rd", [64, 256], F32)
        oas = T("s_oas", [64, 256], BF16)
        P.op("dve", TT(prod[:], qT[:, sc], kT[:, 256:384], ALU.mult), reads=["qT", "kT"], writes=["s_prod"])
        ps_ss, kss = self.bank()
        P.op("pe", MM(ps_ss[:, 0:2], prod[:], self.blk2[:]), reads=["s_prod", "blk2"], writes=[kss])
        for hh in range(2):
            P.op("act", ACT(pself[:, hh:hh + 1], ps_ss[:, hh:hh + 1], AF.Exp, scale=0.125, bias=sinkn[:, hh:hh + 1]), reads=[kss, "sinkn"], writes=["s_pself"])
            P.op("dve", TS(Dg[:, hh, :], self.ident_f[:], pself[:, hh:hh + 1], ALU.mult), reads=["ident_f", "s_pself"], writes=["s_Dg"])
        for j in range(2):
            P.op("dve", MEMSET(Vc[j][:, :, 64:65], 1.0), writes=[f"s_Vc{j}"])
        ps_A, kA = self.bank()
        for g8 in range(16):
            j = g8 % 2
            b0 = g8 * 8
            for half in range(2):
                P.dma("sp", Kc[j][64 * half:64 * half + 64], self.Wb["ckT"][l, kvh][:, b0:b0 + 8, :], writes=[f"s_Kc{j}"])
            fns = []
            for bi in range(8):
                b = b0 + bi
                for hh in range(2):
                    pb = 64 * hh
                    fns.append(MM(ps_A[:, hh * 128 + b:hh * 128 + b + 1], Kc[j][pb:pb + 64, bi, :], qT[pb:pb + 64, 128 + b:129 + b]))
            P.op("pe", fns, reads=[f"s_Kc{j}", "qT"], writes=[kA])
        for hh in range(2):
            P.op("act", ACT(Ps[:, hh * 128:(hh + 1) * 128], ps_A[:, hh * 128:(hh + 1) * 128], AF.Exp, scale=0.125, bias=sinkn[:, hh:hh + 1]), reads=[kA, "sinkn"], writes=["s_Ps"])
        ps_B, kB = self.bank()
        for g8 in range(16):
            j = g8 % 2
            b0 = g8 * 8
            P.dma("sp", Vc[j][:, :, 0:64], self.Wb["cvS"][l, kvh][:, b0:b0 + 8, :], writes=[f"s_Vc{j}"])
            fns = [MM(ps_B[0:65, b0 + bi:256:128], Vc[j][:, bi, :], Ps[:, b0 + bi:256:128]) for bi in range(8)]
            P.op("pe", fns, reads=[f"s_Vc{j}", "s_Ps"], writes=[kB])
        ps_C, kC = self.bank()
        P.op("pe", [MM(ps_C[0:65, hh * 128:(hh + 1) * 128], Vtm[:, 2, :], Dg[:, hh, :]) for hh in range(2)], reads=["Vtm", "s_Dg"], writes=[kC])
        P.op("act", ACT(sC[:], ps_C[0:65, 0:256], AF.Copy), reads=[kC], writes=["s_sC"])
        P.op("dve", TT(tot[:], ps_B[0:65, 0:256], sC[:], ALU.add), reads=[kB, "s_sC"], writes=["s_tot"])
        ps_D, kD = self.bank()
        P.op("pe", MM(ps_D[0:64, 0:256], self.sel65[0:65, :], tot[:]), reads=["sel65", "s_tot"], writes=[kD])
        P.op("dve", TS(rd[:], ps_D[0:64, 0:256], 1.0, ALU.add), reads=[kD], writes=["s_rd"])
        P.op("dve", RECIP(rd[:], rd[:]), reads=["s_rd"], writes=["s_rd"])
        P.op("dve", TT(oas[:], tot[0:64, :], rd[:], ALU.mult), reads=["s_tot", "s_rd"], writes=["s_oas"])
        for hh in range(2):
            P.dma("sp", self.mb_in[64 * hh:64 * hh + 64, 2 + GS:2 + G], oas[:, hh * 128:(hh + 1) * 128], reads=["s_oas"])
        if hg % 2 == 0:
            ktm = T("s_ktm", [128, 4, 64], F32)
            for vi, (src, skey, nm) in enumerate(((kf, "kf", "k"), (vf, "vf", "v"))):
                for i in range(2):
                    pst, kst = self.bank()
                    P.op("pe", TR(pst[:, 0:64], src[0:64, i * 128:(i + 1) * 128], self.ident_f[0:64, 0:64]), reads=[skey, "ident_f"], writes=[kst])
                    P.op("act", ACT(ktm[:, 2 * vi + i, :], pst[:, 0:64], AF.Copy), reads=[kst], writes=["s_ktm"])
                P.dma("sp", O[f"{nm}p_out"][l, kvh], ktm[:, 2 * vi, :], reads=["s_ktm"])
                P.dma("sp", O[f"{nm}s_out"][l, kvh, :, 127, :], ktm[:, 2 * vi + 1, :], reads=["s_ktm"])
                P.dma("sp", O[f"{nm}s_out"][l, kvh, :, 0:127, :], I["ckN" if nm == "k" else "cvN"][l, kvh, :, 1:128, :])
        Eps = T("g_Eps", [64, 128], F32)
        qins = T("g_qins", [64, 128], BF16)
        prodg = T("g_prodg", [64, 128], F32)
        vts = T("g_vts", [128, 64], F32)
        S0f = [T("g_S0f", [64, 16, 64], F32) for _ in range(2)]
        S0b = [T("g_S0b", [64, 16, 64], BF16) for _ in range(2)]
        Vrep = [T("g_Vrep", [64, 16, 64], F32) for _ in range(2)]
        Snew = [T("g_Snew", [64, 16, 64], F32) for _ in range(2)]
        tkv = T("g_tkv", [64, 64], F32)
        osb = T("g_osb", [64, 128], BF16)
        otmp = T("g_otmp", [64, 128], F32)
        P.op("act", ACT(Eps[:], lg[:, sc], AF.Exp, scale=-1.0 / 16.0), reads=["lg"], writes=["g_Eps"])
        P.op("dve", TT(qins[:], qg_f[:, sc], Eps[:], ALU.mult), reads=["qg_f", "g_Eps"], writes=["g_qins"])
        P.op("dve", TT(prodg[:], qg_f[:, sc], kg_f[:, sc], ALU.mult), reads=["qg_f", "kg_f"], writes=["g_prodg"])
        ps_qk, kqk = self.bank()
        P.op("pe", MM(ps_qk[0:64, 0:128], self.ones_f[0:64, 0:64], prodg[:]), reads=["ones_f", "g_prodg"], writes=[kqk])
        pst, kst = self.bank()
        P.op("pe", TR(pst[:, 0:64], vgf[:, :], self.ident_f[0:64, 0:64]), reads=["vgf", "ident_f"], writes=[kst])
        P.op("act", ACT(vts[:], pst[:, 0:64], AF.Copy), reads=[kst], writes=["g_vts"])
        P.dma("sp", self.vs_scr[:, :], vts[:], reads=["g_vts"], writes=["vs_scr"])
        ps_os, kos = self.bank()
        NQ = 16
        for qd in range(128 // NQ):
            j = qd % 2
            bq = slice(qd * NQ, qd * NQ + NQ)
            P.dma("sp", S0f[j][:], I["gst"][l, hg][:, bq, :], writes=[f"g_S0f{j}"])
            P.dma("sp", Vrep[j][:].rearrange("p b e -> p (b e)"),
                  self.vs_scr.rearrange("(o b) e -> o (b e)", o=128 // NQ)[qd:qd + 1, :].partition_broadcast(64),
                  reads=["vs_scr"], writes=[f"g_Vrep{j}"])
            P.op("act", ACT(S0b[j][:], S0f[j][:], AF.Copy), reads=[f"g_S0f{j}"], writes=[f"g_S0b{j}"])
            P.op("pe", [MM(ps_os[0:64, qd * NQ + b:qd * NQ + b + 1], S0b[j][:, b, :], qins[:, qd * NQ + b:qd * NQ + b + 1]) for b in range(NQ)],
                 reads=[f"g_S0b{j}", "g_qins"], writes=[kos])
            for b in range(NQ):
                col = 128 + qd * NQ + b
                P.op("dve", TS(tkv[:], Vrep[j][:, b, :], kg_f[:, col:col + 1], ALU.mult), reads=[f"g_Vrep{j}", "kg_f"], writes=["g_tkv"])
                P.op("dve", STT(Snew[j][:, b, :], S0f[j][:, b, :], Eps[:, qd * NQ + b:qd * NQ + b + 1], tkv[:], ALU.mult, ALU.add),
                     reads=[f"g_S0f{j}", "g_Eps", "g_tkv"], writes=[f"g_Snew{j}"])
            P.dma("sp", O["glas_out"][l, hg][:, bq, :], Snew[j][:], reads=[f"g_Snew{j}"])
        P.op("dve", TT(otmp[:], vgf[:, :], ps_qk[0:64, 0:128], ALU.mult), reads=["vgf", kqk], writes=["g_otmp"])
        P.op("dve", TT(osb[:], otmp[:], ps_os[0:64, 0:128], ALU.add), reads=["g_otmp", kos], writes=["g_osb"])
        P.dma("sp", self.mb_in[192:256, 2 + GS:2 + G], osb[:], reads=["g_osb"])

    @staticmethod
    def blocks(total, off=0):
        nb = (total + 511) // 512
        base, rem = total // nb, total % nb
        out, a = [], off
        for i in range(nb):
            w = base + (1 if i < rem else 0)
            out.append((a, a + w))
            a += w
        return out

    def p2(self, l, last):
        P, I, O = self.P, self.I, self.O
        n = CPC + 2
        blksH = self.blocks(n)
        blksO = self.blocks(CPC)
        pid = self.nc.sync.partition_id()
        P.dma("sp", self.mb_loc[:, :], self.mbufs[l][:, bass.ds(pid * CPC, n)], reads=[f"mbuf{l}"], writes=["mb_loc"])
        mb = self.mb_loc.rearrange("(r q) c -> r q c", r=NHG)
        mkey = "mb_loc"
        with ExitStack() as es:
            T = lambda nm, sh, d: self.T(es, nm, sh, d)
            xr = T("xr", [128, 16, n], F32)
            hb = T("halo_b", [128, 16, 2], BF16)
            P.dma("sp", xr[:, :, 2:n], chunked(self.xres_loc), reads=["xres_loc"], writes=["xr"])
            xg2 = self.xg[l].rearrange("(rk p) n -> p rk n", p=128)
            P.dma("sp", hb[:], xg2[:, bass.ds(((pid + 7) % 8) * 16, 16), CPC - 2:CPC], reads=[f"xg{l}"], writes=["halo_b"])
            P.op("act", ACT(xr[:, :, 0:2], hb[:], AF.Copy), reads=["halo_b"], writes=["xr"])
            gb1 = T("gb1", [128, 2, 16], F32)
            gb2 = T("gb2", [128, 2, 16], F32)
            cw = T("cw", [128, 4, NFC], F32)
            bgl = T("bgl", [128, 4], F32)
            gno = T("gno", [128, 4], F32)
            P.dma("sp", gb1[:], I["ln1"][l].rearrange("t p k -> p t k"), writes=["gb1"])
            P.dma("sp", gb2[:], I["ln2"][l].rearrange("t p k -> p t k"), writes=["gb2"])
            P.dma("sp", cw[:], I["convw"][l].rearrange("t p k -> p t k"), writes=["cw"])
            P.dma("sp", bgl[:], I["bglu"][l], writes=["bgl"])
            P.dma("sp", gno[:], I["gnorm"][l], writes=["gno"])
            P.dma("sp", O["convs_out"][l, 0], I["cprevT"][l, 1])
            with ExitStack() as esa:
                Ta = lambda nm, sh, d: self.T(esa, nm, sh, d)
                oaT = Ta("oaT", [128, 8, n], BF16)
                zT = Ta("zT", [128, 4, n], BF16)
                oT = Ta("oT", [128, 4, n], BF16)
                gT = Ta("gT", [128, 4, n], BF16)
                obT = Ta("obT", [128, 4, n], BF16)
                wgl = Ta("wgl", [128, 4, 512], BF16)
                tA = Ta("tA", [128, 512], F32)
                tB = Ta("tB", [128, 512], F32)
                tCb = Ta("tCb", [128, 512], BF16)
                wo = [Ta("wo", [128, 16, 128], BF16) for _ in range(2)]
                cols = slice(0, n)
                for r in range(NHG):
                    hp = slice(64 * (r % 2), 64 * (r % 2) + 64)
                    P.dma("sp", oaT[:, r, :], mb[r, 0:128, cols], reads=[mkey], writes=["oaT"])
                    P.dma("sp", zT[hp, r // 2, :], mb[r, 128:192, cols], reads=[mkey], writes=["zT"])
                    P.dma("sp", oT[hp, r // 2, :], mb[r, 192:256, cols], reads=[mkey], writes=["oT"])
                    P.dma("sp", gT[hp, r // 2, :], mb[r, 256:320, cols], reads=[mkey], writes=["gT"])
                P.dma("sp", wgl[:], chunked(self.Wb["wglu"][l]), reads=self.wb_keys[("wglu", l)], writes=["wgl"])
                for oc in range(4):
                    for (b0, b1) in blksH:
                        nb = b1 - b0
                        ps, kp = self.bank()
                        P.op("pe", [MM(ps[:, 0:nb], wgl[:, k, oc * 128:(oc + 1) * 128], zT[:, k, b0:b1], start=(k == 0), stop=(k == 3)) for k in range(4)],
                             reads=["wgl", "zT"], writes=[kp])
                        P.op("act", ACT(tA[:, 0:nb], ps[:, 0:nb], AF.Sigmoid, bias=bgl[:, oc:oc + 1]), reads=[kp, "bgl"], writes=["tA"])
                        P.op("dve", TT(obT[:, oc, b0:b1], zT[:, oc, b0:b1], tA[:, 0:nb], ALU.mult), reads=["zT", "tA"], writes=["obT"])
                for h in range(4):
                    for (b0, b1) in blksH:
                        nb = b1 - b0
                        P.op("act", ACT(tCb[:, 0:nb], oT[:, h, b0:b1], AF.Square), reads=["oT"], writes=["tCb"])
                        ps, kp = self.bank()
                        P.op("pe", MM(ps[:, 0:nb], self.ones_b[:], tCb[:, 0:nb]), reads=["ones_b", "tCb"], writes=[kp])
                        P.op("act", ACT(tA[:, 0:nb], ps[:, 0:nb], AF.Sqrt, scale=1.0 / 128.0, bias=EPS), reads=[kp], writes=["tA"])
                        P.op("dve", RECIP(tA[:, 0:nb], tA[:, 0:nb]), reads=["tA"], writes=["tA"])
                        P.op("dve", TT(tA[:, 0:nb], oT[:, h, b0:b1], tA[:, 0:nb], ALU.mult), reads=["oT", "tA"], writes=["tA"])
                        P.op("act", ACT(tB[:, 0:nb], gT[:, h, b0:b1], AF.Silu), reads=["gT"], writes=["tB"])
                        P.op("dve", STT(oT[:, h, b0:b1], tA[:, 0:nb], gno[:, h:h + 1], tB[:, 0:nb], ALU.mult, ALU.mult), reads=["tA", "gno", "tB"], writes=["oT"])
                for oc in range(16):
                    w = wo[oc % 2]
                    wk = f"wo{oc % 2}"
                    P.dma("sp", w[:], chunked(self.Wb["wout"][l][:, oc * 128:(oc + 1) * 128]), reads=self.wb_keys[("wout", l)], writes=[wk])
                    for (b0, b1) in blksH:
                        nb = b1 - b0
                        ps, kp = self.bank()
                        fns = []
                        for k in range(16):
                            rhs = oaT[:, k, b0:b1] if k < 8 else (obT[:, k - 8, b0:b1] if k < 12 else oT[:, k - 12, b0:b1])
                            fns.append(MM(ps[:, 0:nb], w[:, k, :], rhs, start=(k == 0), stop=(k == 15)))
                        P.op("pe", fns, reads=[wk, "oaT", "obT", "oT"], writes=[kp])
                        P.op("dve", STT(xr[:, oc, b0:b1], xr[:, oc, b0:b1], ALPHA, ps[:, 0:nb], ALU.mult, ALU.add), reads=["xr", kp], writes=["xr"])
                P.barrier()
            x1b = T("x1b", [128, 16, n], BF16)
            with ExitStack() as esl:
                self.layernorm(esl, xr, "xr", n, gb1, "gb1", x1b, "x1b", mask=self.cmask[:, 0:n])
                P.barrier()
            for k in range(16):
                P.op("act", ACT(xr[:, k, :], xr[:, k, :], AF.Copy, scale=ALPHA), reads=["xr"], writes=["xr"])
            with ExitStack() as esf:
                Tf = lambda nm, sh, d: self.T(esf, nm, sh, d)
                hT = Tf("hT", [128, 11, CPC], BF16)
                U = [Tf("U", [128, n], F32) for _ in range(2)]
                cc = [Tf("cc", [128, CPC], F32) for _ in range(2)]
                e01 = [Tf("e01", [128, 2, 128], F32) for _ in range(2)]
                wu = [Tf("wu", [128, 16, 128], BF16) for _ in range(2)]
                wg = [Tf("wg", [128, 16, 128], BF16) for _ in range(2)]
                wd = [Tf("wd", [128, 11, 128], BF16) for _ in range(2)]
                cpv = [Tf("cpv", [128, 2, 128], F32) for _ in range(2)]
                npz = CPC - 128
                sc = slice(npz, CPC)
                for grp in range(4):
                    for fi in range(11):
                        f = grp * 11 + fi
                        j = f % 2
                        P.dma("sp", wu[j][:], chunked(self.Wb["wup"][l][:, f * 128:(f + 1) * 128]), reads=self.wb_keys[("wup", l)], writes=[f"wu{j}"])
                        P.dma("sp", wg[j][:], chunked(self.Wb["wup"][l][:, DFF + f * 128:DFF + (f + 1) * 128]), reads=self.wb_keys[("wup", l)], writes=[f"wg{j}"])
                        P.dma("sp", cpv[j][:], I["cprevT"][l][:, f * 128:(f + 1) * 128, :].rearrange("t p b -> p t b"), writes=[f"cpv{j}"])
                        Uj, uk = U[j], f"U{j}"
                        cj, ck_ = cc[j], f"cc{j}"
                        if self.pending:
                            self.pump(1.0 / 40 if f < NFC - 1 else 1.0, getattr(self, "pace_tok", None))
                        for (b0, b1) in blksH:
                            nb = b1 - b0
                            ps, kp = self.bank()
                            P.op("pe", [MM(ps[:, 0:nb], wu[j][:, k, :], x1b[:, k, b0:b1], start=(k == 0), stop=(k == 15)) for k in range(16)],
                                 reads=[f"wu{j}", "x1b"], writes=[kp])
                            P.op("act", ACT(Uj[:, b0:b1], ps[:, 0:nb], AF.Copy), reads=[kp], writes=[uk])
                        gps = []
                        for (b0, b1) in blksO:
                            nb = b1 - b0
                            ps2, kp2 = self.bank()
                            P.op("pe", [MM(ps2[:, 0:nb], wg[j][:, k, :], x1b[:, k, 2 + b0:2 + b1], start=(k == 0), stop=(k == 15)) for k in range(16)],
                                 reads=[f"wg{j}", "x1b"], writes=[kp2])
                            gps.append((ps2, kp2))
                        w0, w1, w2, cb = (cw[:, t_, f:f + 1] for t_ in range(4))
                        P.op("dve", TS(cj[:, 0:CPC], Uj[:, 0:CPC], w0, ALU.mult, cb, ALU.add), reads=[uk, "cw"], writes=[ck_])
                        P.op("dve", STT(cj[:, 0:CPC], Uj[:, 1:CPC + 1], w1, cj[:, 0:CPC], ALU.mult, ALU.add), reads=[uk, "cw", ck_], writes=[ck_])
                        P.op("dve", STT(cj[:, 0:CPC], Uj[:, 2:CPC + 2], w2, cj[:, 0:CPC], ALU.mult, ALU.add), reads=[uk, "cw", ck_], writes=[ck_])
                        ej, ek_ = e01[j], f"e01{j}"
                        P.op("dve", STT(ej[:, 0, :], Uj[:, npz:npz + 128], self.nsm[:, 0:1], cpv[j][:, 0, :], ALU.mult, ALU.add), reads=[uk, "nsm", f"cpv{j}"], writes=[ek_])
                        P.op("dve", STT(ej[:, 1, :], Uj[:, npz + 1:npz + 129], self.nsm[:, 0:1], cpv[j][:, 1, :], ALU.mult, ALU.add), reads=[uk, "nsm", f"cpv{j}"], writes=[ek_])
                        P.op("dve", TS(cj[:, sc], ej[:, 0, :], w0, ALU.mult, cb, ALU.add), reads=[ek_, "cw", ck_], writes=[ck_])
                        P.op("dve", STT(cj[:, sc], ej[:, 1, :], w1, cj[:, sc], ALU.mult, ALU.add), reads=[ek_, "cw", ck_], writes=[ck_])
                        P.op("dve", STT(cj[:, sc], Uj[:, 2 + npz:2 + CPC], w2, cj[:, sc], ALU.mult, ALU.add), reads=[uk, "cw", ck_], writes=[ck_])
                        P.dma("sp", O["convp_out"][l, :, f, :], Uj[:, npz:npz + 2], reads=[uk])
                        P.dma("sp", O["convs_out"][l, 1, f * 128:(f + 1) * 128, :], Uj[:, 2 + npz:2 + CPC], reads=[uk])
                        self.pace_tok = P.op("act", ACT(cj[:, 0:CPC], cj[:, 0:CPC], AF.Gelu), reads=[ck_], writes=[ck_])
                        for bi, (b0, b1) in enumerate(blksO):
                            ps2, kp2 = gps[bi]
                            P.op("dve", TT(hT[:, fi, b0:b1], cj[:, b0:b1], ps2[:, 0:b1 - b0], ALU.mult), reads=[ck_, kp2], writes=["hT"])
                    for oc in range(16):
                        j = oc % 2
                        P.dma("sp", wd[j][:], chunked(self.Wb["wdown"][l][grp * 1408:(grp + 1) * 1408, oc * 128:(oc + 1) * 128]), reads=self.wb_keys[("wdown", l)], writes=[f"wd{j}"])
                        for (b0, b1) in blksO:
                            nb = b1 - b0
                            ps, kp = self.bank()
                            P.op("pe", [MM(ps[:, 0:nb], wd[j][:, fi, :], hT[:, fi, b0:b1], start=(fi == 0), stop=(fi == 10)) for fi in range(11)],
                                 reads=[f"wd{j}", "hT"], writes=[kp])
                            P.op("dve", TT(xr[:, oc, 2 + b0:2 + b1], xr[:, oc, 2 + b0:2 + b1], ps[:, 0:nb], ALU.add), reads=["xr", kp], writes=["xr"])
                P.barrier()
            xo = xr[:, :, 2:n]
            with ExitStack() as eso:
                To = lambda nm, sh, d: self.T(eso, nm, sh, d)
                if not last:
                    self.layernorm(eso, xo, "xr", CPC, gb2, "gb2", x1b, "x1b", mask=self.cmask[:, 2:n])
                    P.dma("sp", chunked(self.xres_loc), xo, reads=["xr"], writes=["xres_loc"])
                    P.dma("sp", chunked(self.xn_in), x1b[:, :, 0:CPC], reads=["x1b"], writes=["xn_in"])
                    P.barrier()
                    P.collective("AllGather", self.xn_in, self.xg[l + 1], reads=["xn_in"], writes=[f"xg{l + 1}"])
                else:
                    self.layernorm(eso, xo, "xr", CPC, gb2, "gb2")
                    yt = [To("yt", [128, D], F32) for _ in range(2)]
                    nfull = CPC // 128
                    for jb in range((CPC + 127) // 128):
                        w = min(128, CPC - jb * 128)
                        y_, yk = yt[jb % 2], f"yt{jb % 2}"
                        for kg in range(4):
                            ps, kp = self.bank()
                            for kk in range(4):
                                P.op("pe", TR(ps[0:w, kk * 128:(kk + 1) * 128], xo[:, 4 * kg + kk, jb * 128:jb * 128 + w], self.ident_f[:]),
                                     reads=["xr", "ident_f"], writes=[kp])
                            P.op("act", ACT(y_[0:w, kg * 512:(kg + 1) * 512], ps[0:w, :], AF.Copy), reads=[kp], writes=[yk])
                        P.dma("sp", O["y_out"][jb * 128:jb * 128 + w, :], y_[0:w, :], reads=[yk])
                P.barrier()


SHARED = ["ln_in", "rope", "pm", "swamask", "ident", "wout", "wup", "wdown", "wglu", "ln1", "ln2", "convw", "bglu", "gnorm"]


def _percore(d, c):
    f = np.ascontiguousarray
    e = {k: d[k] for k in SHARED}
    e["xT"] = f(d["xT"][:, c * CPC:(c + 1) * CPC])
    for k in ("win_all", "sink", "s5col", "s5row", "s5bz", "s5cz", "s5dz", "wa2", "ba", "gst"):
        e[k] = f(d[k][:, c:c + 1])
    for k in ("ckT", "cvS", "ckN", "cvN"):
        e[k] = f(d[k][:, c // 2:c // 2 + 1])
    e["ssm0"] = f(d["ssm0"][:, :, 2 * c:2 * c + 2])
    e["cprevT"] = d["cprevT"] if c == NCORES - 1 else np.zeros_like(d["cprevT"])
    cm = np.zeros((1, CPC + 2), np.float32)
    g = c * CPC - 2 + np.arange(CPC + 2)
    cm[0, g >= NPAD] = 1.0
    e["cmask"] = cm
    e["nsm"] = np.full((128, 1), 0.0 if c == NCORES - 1 else 1.0, np.float32)
    return e


def _assemble(rs, inp):
    B = 128
    cat = lambda k, ax: np.concatenate([r[k] for r in rs], axis=ax)
    y = cat("y_out", 0)
    y_prompt = y[128:G - 128][None]
    y_sample = y[G - 128:][:, None, :]
    ev = [rs[c] for c in range(0, NCORES, 2)]
    kp = np.concatenate([r["kp_out"] for r in ev], 1).transpose(0, 2, 1, 3)[:, None]
    vp = np.concatenate([r["vp_out"] for r in ev], 1).transpose(0, 2, 1, 3)[:, None]
    ks = np.concatenate([r["ks_out"] for r in ev], 1).transpose(0, 2, 3, 1, 4)
    vs = np.concatenate([r["vs_out"] for r in ev], 1).transpose(0, 2, 3, 1, 4)
    sp = cat("ssmp_out", 2).reshape(L, 2, 16, 2, 64).reshape(L, 2, 32, 64)
    ssm_re_p, ssm_im_p = sp[:, 0][:, None], sp[:, 1][:, None]
    ss = cat("ssms_out", 2).transpose(0, 1, 3, 2, 4).reshape(L, 2, B, 32, 64)
    ssm_re_s, ssm_im_s = ss[:, 0], ss[:, 1]
    gp = cat("glap_out", 1).reshape(L, 4, 2, 64, 64).transpose(0, 1, 3, 2, 4).reshape(L, 4, 64, 128)[:, None]
    gs = cat("glas_out", 1).reshape(L, 4, 2, 64, B, 64).transpose(0, 4, 1, 3, 2, 5).reshape(L, B, 4, 64, 128)
    rl = rs[NCORES - 1]
    conv_p = rl["convp_out"].transpose(0, 3, 2, 1).reshape(L, 2, DFF)[:, None]
    conv_s = rl["convs_out"].transpose(0, 3, 1, 2)
    outs = (y_prompt, y_sample, kp, vp, ssm_re_p, ssm_im_p, gp, conv_p, ks, vs, ssm_re_s, ssm_im_s, gs, conv_s)
    return tuple(np.ascontiguousarray(o, dtype=np.float32) for o in outs)


def kernel(**inp):
    d = _prep_inputs(inp)
    in_maps = [_percore(d, c) for c in range(NCORES)]
    shapes = {k: v.shape for k, v in in_maps[0].items()}
    b = Builder(shapes)
    nc = b.build()
    res = run_bass_kernel_spmd(nc, in_maps, core_ids=list(range(NCORES)))
    return _assemble(res.results, inp)
```
